# Optimizing a Trainium2 kernel written in Bass

```python
import math
import jax, jax.numpy as jnp
from jax import lax
import numpy as np

D_MODEL = 1024
BATCH = 16
SEQ = 2048
DEPTH = 2

GRID_W = 64
ROPE_THETA = 10000.0
Q_BLOCK = 128
EPS = 1e-6
PLE_DIM = 256

MLA_HEADS = 8
MLA_Q_LORA = 256
MLA_KV_LORA = 128
MLA_NOPE = 64
MLA_ROPE = 32
MLA_V = 64

GLA_HEADS = 4
GLA_DK = 64
GLA_DV = 128
GLA_GATE_RANK = 16
GLA_GATE_NORM = 16.0
GLA_CHUNK = 64

GQA_HEADS = 16
GQA_KV_HEADS = 4
GQA_HEAD_DIM = 64

N_EXPERTS = 16
EC_CAPACITY_FACTOR = 2
EXPERT_FF = 1024

D_MIX = MLA_HEADS * MLA_V + GLA_HEADS * GLA_DV
EVEN_SPLITS = (MLA_Q_LORA, MLA_KV_LORA, MLA_ROPE,
               GLA_HEADS * GLA_DK, GLA_HEADS * GLA_DK, GLA_HEADS * GLA_DV,
               2 * GLA_GATE_RANK, GLA_HEADS * GLA_DV)
ODD_SPLITS = (GQA_HEADS * GQA_HEAD_DIM, GQA_KV_HEADS * GQA_HEAD_DIM, GQA_KV_HEADS * GQA_HEAD_DIM)
N_EVEN = (DEPTH + 1) // 2
N_ODD = DEPTH // 2
DEEPNORM_ALPHA = (2.0 * DEPTH) ** 0.25
DEEPNORM_BETA = (8.0 * DEPTH) ** -0.25

kernel_name = "hybrid_mla_gla_gqa_ecmoe_deepnorm"


def _split(h, sizes):
    offs = np.cumsum(sizes)[:-1].tolist()
    return jnp.split(h, offs, axis=-1)


def rms_norm(x, g):
    x32 = x.astype(jnp.float32)
    y = x32 * lax.rsqrt(jnp.mean(x32 * x32, axis=-1, keepdims=True) + EPS)
    return (y * g.astype(jnp.float32)).astype(x.dtype)


def layer_norm(x, g, b):
    x32 = x.astype(jnp.float32)
    mu = jnp.mean(x32, axis=-1, keepdims=True)
    xc = x32 - mu
    var = jnp.mean(xc * xc, axis=-1, keepdims=True)
    y = xc * lax.rsqrt(var + EPS) * g.astype(jnp.float32) + b.astype(jnp.float32)
    return y.astype(x.dtype)


def axial_rope(seq, rot_dim):
    rows = seq // GRID_W
    row = jnp.repeat(jnp.arange(rows, dtype=jnp.float32), GRID_W)
    col = jnp.tile(jnp.arange(GRID_W, dtype=jnp.float32), rows)
    axis_dim = rot_dim // 2
    inv = ROPE_THETA ** (-jnp.arange(0, axis_dim, 2, dtype=jnp.float32) / axis_dim)
    ang = jnp.concatenate([row[:, None] * inv, col[:, None] * inv], axis=-1)
    return jnp.cos(ang), jnp.sin(ang)


def apply_rope(x, cos, sin):
    half = x.shape[-1] // 2
    c = cos.astype(x.dtype)
    s = sin.astype(x.dtype)
    x1, x2 = x[..., :half], x[..., half:]
    return jnp.concatenate([x1 * c - x2 * s, x1 * s + x2 * c], axis=-1)


def _to_blocks(t):
    b, s = t.shape[:2]
    nb = s // Q_BLOCK
    return t.reshape((b, nb, Q_BLOCK) + t.shape[2:]).swapaxes(0, 1)


def _from_blocks(t):
    nb, b, qb = t.shape[:3]
    return t.swapaxes(0, 1).reshape((b, nb * qb) + t.shape[3:])


def mla_attention(q_nope, q_pe, k_nope, k_pe, v):
    scale = (MLA_NOPE + MLA_ROPE) ** -0.5

    def block(args):
        qn, qp = args
        s = (jnp.einsum('bqhd,bkhd->bhqk', qn, k_nope)
             + jnp.einsum('bqhd,bkd->bhqk', qp, k_pe)) * scale
        pr = jax.nn.softmax(s.astype(jnp.float32), axis=-1).astype(v.dtype)
        return jnp.einsum('bhqk,bkhd->bqhd', pr, v)

    o = lax.map(block, (_to_blocks(q_nope), _to_blocks(q_pe)))
    return _from_blocks(o)


def gla_chunked(q, k, v, log_a):
    b, s, h, dk = q.shape
    dv = v.shape[-1]
    n = s // GLA_CHUNK
    L = GLA_CHUNK

    def chunk(t):
        return t.astype(jnp.float32).reshape(b, n, L, h, t.shape[-1]).transpose(0, 3, 1, 2, 4)

    qc, kc, vc, lac = chunk(q), chunk(k), chunk(v), chunk(log_a)
    cum = jnp.cumsum(lac, axis=-2)
    qg = qc * jnp.exp(cum)
    kg = kc * jnp.exp(-cum)
    tri = jnp.tril(jnp.ones((L, L), dtype=bool))
    att = jnp.where(tri, jnp.einsum('bhnld,bhnmd->bhnlm', qg, kg), 0.0)
    o_intra = jnp.einsum('bhnlm,bhnme->bhnle', att, vc)
    last = cum[..., -1:, :]
    u = jnp.einsum('bhnld,bhnle->bhnde', kc * jnp.exp(last - cum), vc)
    decay = jnp.exp(last[..., 0, :])

    def step(state, inp):
        d, du = inp
        return d[..., None] * state + du, state

    init = jnp.zeros((b, h, dk, dv), jnp.float32)
    _, s_prev = lax.scan(step, init, (jnp.moveaxis(decay, 2, 0), jnp.moveaxis(u, 2, 0)))
    o_inter = jnp.einsum('bhnld,nbhde->bhnle', qg, s_prev)
    o = (o_intra + o_inter).transpose(0, 2, 3, 1, 4).reshape(b, s, h, dv)
    return o.astype(v.dtype)


def even_mixer(x, w_in, mla_q_norm, w_uq, mla_kv_norm, w_ukv,
               gla_gate_w_fwd, gla_gate_b_fwd, gla_gate_w_bwd, gla_gate_b_bwd, gla_norm,
               cos_a, sin_a):
    b, s, _ = x.shape
    h = x @ w_in
    c_q, c_kv, k_pe, gq, gk, gv, g_lr, gr = _split(h, EVEN_SPLITS)

    q = (rms_norm(c_q, mla_q_norm) @ w_uq).reshape(b, s, MLA_HEADS, MLA_NOPE + MLA_ROPE)
    q_nope = q[..., :MLA_NOPE]
    q_pe = apply_rope(q[..., MLA_NOPE:], cos_a[:, None, :], sin_a[:, None, :])
    kv = (rms_norm(c_kv, mla_kv_norm) @ w_ukv).reshape(b, s, MLA_HEADS, MLA_NOPE + MLA_V)
    k_nope, v_mla = kv[..., :MLA_NOPE], kv[..., MLA_NOPE:]
    k_pe = apply_rope(k_pe, cos_a, sin_a)
    o_mla = mla_attention(q_nope, q_pe, k_nope, k_pe, v_mla).reshape(b, s, MLA_HEADS * MLA_V)

    qh = (gq * (GLA_DK ** -0.5)).reshape(b, s, GLA_HEADS, GLA_DK)
    kh = gk.reshape(b, s, GLA_HEADS, GLA_DK)
    vh = gv.reshape(b, s, GLA_HEADS, GLA_DV)
    lr_f, lr_b = g_lr[..., :GLA_GATE_RANK], g_lr[..., GLA_GATE_RANK:]
    la_f = (jax.nn.log_sigmoid((lr_f @ gla_gate_w_fwd + gla_gate_b_fwd).astype(jnp.float32))
            / GLA_GATE_NORM).reshape(b, s, GLA_HEADS, GLA_DK)
    la_b = (jax.nn.log_sigmoid((lr_b @ gla_gate_w_bwd + gla_gate_b_bwd).astype(jnp.float32))
            / GLA_GATE_NORM).reshape(b, s, GLA_HEADS, GLA_DK)
    o_f = gla_chunked(qh, kh, vh, la_f)
    flip = lambda t: jnp.flip(t, axis=1)
    o_b = flip(gla_chunked(flip(qh), flip(kh), flip(vh), flip(la_b)))
    o_gla = rms_norm(o_f + o_b, gla_norm).reshape(b, s, GLA_HEADS * GLA_DV) * jax.nn.silu(gr)

    return jnp.concatenate([o_mla, o_gla], axis=-1)


def odd_mixer(x, w_in, gqa_q_norm, gqa_k_norm, cos_c, sin_c):
    b, s, _ = x.shape
    g = GQA_HEADS // GQA_KV_HEADS
    q, k, v = _split(x @ w_in, ODD_SPLITS)
    q = rms_norm(q.reshape(b, s, GQA_HEADS, GQA_HEAD_DIM), gqa_q_norm)
    k = rms_norm(k.reshape(b, s, GQA_KV_HEADS, GQA_HEAD_DIM), gqa_k_norm)
    v = v.reshape(b, s, GQA_KV_HEADS, GQA_HEAD_DIM)
    q = apply_rope(q, cos_c[:, None, :], sin_c[:, None, :]).reshape(b, s, GQA_KV_HEADS, g, GQA_HEAD_DIM)
    k = apply_rope(k, cos_c[:, None, :], sin_c[:, None, :])
    scale = GQA_HEAD_DIM ** -0.5

    def block(qb):
        sc = jnp.einsum('bqkgd,bskd->bkgqs', qb, k) * scale
        pr = jax.nn.softmax(sc.astype(jnp.float32), axis=-1).astype(v.dtype)
        return jnp.einsum('bkgqs,bskd->bqkgd', pr, v)

    o = _from_blocks(lax.map(block, _to_blocks(q)))
    return o.reshape(b, s, GQA_HEADS * GQA_HEAD_DIM)


def expert_choice_moe(x, router_w, w1, w3, w2):
    b, s, d = x.shape
    cap = EC_CAPACITY_FACTOR * s // N_EXPERTS
    aff = jax.nn.softmax(jnp.einsum('bsd,de->bse', x, router_w).astype(jnp.float32), axis=-1)
    gates, idx = lax.top_k(aff.transpose(0, 2, 1), cap)
    xg = jax.vmap(lambda xb, ib: xb[ib])(x, idx)
    hid = jax.nn.silu(jnp.einsum('becd,edf->becf', xg, w1)) * jnp.einsum('becd,edf->becf', xg, w3)
    ye = jnp.einsum('becf,efd->becd', hid, w2) * gates[..., None].astype(x.dtype)
    return jax.vmap(lambda yb, ib: jnp.zeros((s, d), x.dtype).at[ib.reshape(-1)].add(yb.reshape(-1, d)))(ye, idx)


def setup_inputs(seed: int = 0) -> dict:
    key = jax.random.key(seed)
    ks = iter(jax.random.split(key, 40))
    f32 = jnp.float32

    def nrm(shape, scale):
        return jax.random.normal(next(ks), shape, f32) * scale

    def gain(shape):
        return 1.0 + 0.02 * jax.random.normal(next(ks), shape, f32)

    d_even_in = sum(EVEN_SPLITS)
    d_odd_in = sum(ODD_SPLITS)
    NE, NO = N_EVEN, N_ODD
    return {
        "x": nrm((BATCH, SEQ, D_MODEL), 1.0),
        "p": nrm((DEPTH, BATCH, SEQ, PLE_DIM), 1.0),
        "w_in_even": nrm((NE, D_MODEL, d_even_in), D_MODEL ** -0.5),
        "mla_q_norm": gain((NE, MLA_Q_LORA)),
        "w_uq": nrm((NE, MLA_Q_LORA, MLA_HEADS * (MLA_NOPE + MLA_ROPE)), MLA_Q_LORA ** -0.5),
        "mla_kv_norm": gain((NE, MLA_KV_LORA)),
        "w_ukv": nrm((NE, MLA_KV_LORA, MLA_HEADS * (MLA_NOPE + MLA_V)), MLA_KV_LORA ** -0.5),
        "gla_gate_w_fwd": nrm((NE, GLA_GATE_RANK, GLA_HEADS * GLA_DK), GLA_GATE_RANK ** -0.5),
        "gla_gate_b_fwd": nrm((NE, GLA_HEADS * GLA_DK), 0.1),
        "gla_gate_w_bwd": nrm((NE, GLA_GATE_RANK, GLA_HEADS * GLA_DK), GLA_GATE_RANK ** -0.5),
        "gla_gate_b_bwd": nrm((NE, GLA_HEADS * GLA_DK), 0.1),
        "gla_norm": gain((NE, GLA_DV)),
        "w_in_odd": nrm((NO, D_MODEL, d_odd_in), D_MODEL ** -0.5),
        "gqa_q_norm": gain((NO, GQA_HEAD_DIM)),
        "gqa_k_norm": gain((NO, GQA_HEAD_DIM)),
        "w_o": nrm((DEPTH, D_MIX, D_MODEL), D_MIX ** -0.5 * DEEPNORM_BETA),
        "ln1_g": gain((DEPTH, D_MODEL)),
        "ln1_b": nrm((DEPTH, D_MODEL), 0.02),
        "router_w": nrm((DEPTH, D_MODEL, N_EXPERTS), D_MODEL ** -0.5),
        "w1": nrm((DEPTH, N_EXPERTS, D_MODEL, EXPERT_FF), D_MODEL ** -0.5),
        "w3": nrm((DEPTH, N_EXPERTS, D_MODEL, EXPERT_FF), D_MODEL ** -0.5),
        "w2": nrm((DEPTH, N_EXPERTS, EXPERT_FF, D_MODEL), EXPERT_FF ** -0.5 * DEEPNORM_BETA),
        "ple_gate_w": nrm((DEPTH, D_MODEL, D_MODEL), D_MODEL ** -0.5),
        "ple_gate_b": nrm((DEPTH, D_MODEL), 0.02),
        "ple_w": nrm((DEPTH, PLE_DIM, D_MODEL), PLE_DIM ** -0.5 * DEEPNORM_BETA),
        "ln2_g": gain((DEPTH, D_MODEL)),
        "ln2_b": nrm((DEPTH, D_MODEL), 0.02),
    }


def reference(x, p, w_in_even, mla_q_norm, w_uq, mla_kv_norm, w_ukv,
              gla_gate_w_fwd, gla_gate_b_fwd, gla_gate_w_bwd, gla_gate_b_bwd, gla_norm,
              w_in_odd, gqa_q_norm, gqa_k_norm,
              w_o, ln1_g, ln1_b, router_w, w1, w3, w2,
              ple_gate_w, ple_gate_b, ple_w, ln2_g, ln2_b):
    seq = x.shape[1]
    cos_a, sin_a = axial_rope(seq, MLA_ROPE)
    cos_c, sin_c = axial_rope(seq, GQA_HEAD_DIM)
    for i in range(DEPTH):
        if i % 2 == 0:
            j = i // 2
            mix = even_mixer(x, w_in_even[j], mla_q_norm[j], w_uq[j], mla_kv_norm[j], w_ukv[j],
                             gla_gate_w_fwd[j], gla_gate_b_fwd[j], gla_gate_w_bwd[j], gla_gate_b_bwd[j],
                             gla_norm[j], cos_a, sin_a)
        else:
            j = i // 2
            mix = odd_mixer(x, w_in_odd[j], gqa_q_norm[j], gqa_k_norm[j], cos_c, sin_c)
        x = layer_norm(DEEPNORM_ALPHA * x + mix @ w_o[i], ln1_g[i], ln1_b[i])
        ffn = expert_choice_moe(x, router_w[i], w1[i], w3[i], w2[i])
        ple = jax.nn.sigmoid(x @ ple_gate_w[i] + ple_gate_b[i]) * (p[i] @ ple_w[i])
        x = layer_norm(DEEPNORM_ALPHA * x + ffn + ple, ln2_g[i], ln2_b[i])
    return x
```

```python
from concourse.bass_utils import run_bass_kernel_spmd
import numpy as np
from contextlib import ExitStack
import concourse.bass as bass
import concourse.mybir as mybir

F32 = mybir.dt.float32
BF16 = mybir.dt.bfloat16
U32 = mybir.dt.uint32
I32 = mybir.dt.int32
ALU = mybir.AluOpType
AF = mybir.ActivationFunctionType
AX = mybir.AxisListType

EPOCH = 16000
N_DMA_SEMS = 20
SAME_ENGINE_SYNC = {"pe": False, "act": True, "dve": True, "pool": True, "sp": False}


class V:
    __slots__ = ("tile", "ap", "lo", "hi")

    def __init__(self, tile, ap, lo, hi):
        self.tile, self.ap, self.lo, self.hi = tile, ap, lo, hi


class T:
    def __init__(self, mk, name, handle, shape, kind):
        self.mk, self.name, self.h, self.shape, self.kind = mk, name, handle, list(shape), kind
        fd = self.shape[1:] if kind != "dram" else self.shape
        self.fshape = fd
        st = [1] * len(fd)
        for i in range(len(fd) - 2, -1, -1):
            st[i] = st[i + 1] * fd[i + 1]
        self.fstride = st
        self.size = int(np.prod(fd)) if fd else 1
        self.recs = {}

    def __getitem__(self, idx):
        if not isinstance(idx, tuple):
            idx = (idx,)
        idx = tuple(idx) + (slice(None),) * (len(self.shape) - len(idx))
        ap = self.h[idx]
        fidx = idx[1:] if self.kind != "dram" else idx
        if self.kind == "psum":
            return V(self, ap, 0, self.size)
        lo = 0
        hi = 0
        for i, ix in enumerate(fidx):
            n = self.fshape[i]
            if isinstance(ix, slice):
                a, b, stp = ix.indices(n)
                assert stp == 1 and b > a, (self.name, idx)
                lo += a * self.fstride[i]
                hi += (b - 1) * self.fstride[i]
            else:
                assert 0 <= ix < n, (self.name, idx)
                lo += ix * self.fstride[i]
                hi += ix * self.fstride[i]
        return V(self, ap, lo, hi + 1)

    def all(self):
        return self[tuple(slice(None) for _ in self.shape)]


class MK:
    def __init__(self, nc):
        self.nc = nc
        self.es = ExitStack()
        self.eng = {"pe": nc.tensor, "act": nc.scalar, "dve": nc.vector, "pool": nc.gpsimd, "sp": nc.sync}
        self.count = {e: 0 for e in self.eng}
        self.esems = {e: [] for e in self.eng}
        self.seen = {e: {} for e in self.eng}
        self.semh = {}
        self.dma_sems = []
        self.dma_val = []
        self.dma_rr = 0
        self.n_wait = 0
        self.n_inst = 0
        for i in range(N_DMA_SEMS):
            s = self.es.enter_context(nc.semaphore("dq%d" % i))
            key = ("dma", i)
            self.semh[key] = s
            self.dma_sems.append(key)
            self.dma_val.append(0)
        self.phase_stack = []

    def sb(self, name, shape, dtype, stack=None):
        self.uid = getattr(self, "uid", 0) + 1
        name = "sb%d_%s" % (self.uid, name)
        h = (stack or self.es).enter_context(self.nc.sbuf_tensor(name, list(shape), dtype))
        return T(self, name, h, shape, "sbuf")

    def ps(self, name, shape, dtype, stack=None):
        h = (stack or self.es).enter_context(self.nc.psum_tensor(name, list(shape), dtype))
        return T(self, name, h, shape, "psum")

    def dram(self, name, shape, dtype, kind="Internal"):
        h = self.nc.dram_tensor(name, list(shape), dtype, kind=kind)
        return T(self, name, h, shape, "dram")

    def _eng_token(self, e):
        c = self.count[e]
        ep = c // EPOCH
        while len(self.esems[e]) <= ep:
            s = self.es.enter_context(self.nc.semaphore("e_%s_%d" % (e, len(self.esems[e]))))
            key = ("eng", e, len(self.esems[e]))
            self.semh[key] = s
            self.esems[e].append(key)
        return self.esems[e][ep], (c % EPOCH) + 1

    def _wait(self, e, key, val):
        if self.seen[e].get(key, 0) >= val:
            return
        self.eng[e].wait_ge(self.semh[key], val)
        self.seen[e][key] = val
        self.n_wait += 1

    def _deps(self, e, reads, writes, strict=()):
        deps = {}
        sdeps = {}
        for v in strict:
            for (k, key, lo, hi), val in v.tile.recs.items():
                if k == "w" and lo < v.hi and v.lo < hi:
                    if sdeps.get(key, 0) < val:
                        sdeps[key] = val
        for key, val in sdeps.items():
            self._wait(e, key, val)

        def add(key, val):
            if deps.get(key, 0) < val:
                deps[key] = val

        for v in reads:
            for (k, key, lo, hi), val in v.tile.recs.items():
                if k == "w" and lo < v.hi and v.lo < hi:
                    add(key, val)
        for v in writes:
            for (k, key, lo, hi), val in v.tile.recs.items():
                if lo < v.hi and v.lo < hi:
                    add(key, val)
        for key, val in deps.items():
            if key[0] == "eng" and key[1] == e and not SAME_ENGINE_SYNC[e]:
                continue
            self._wait(e, key, val)

    def _record(self, key, val, reads, writes):
        for v in reads:
            v.tile.recs[("r", key, v.lo, v.hi)] = val
        for v in writes:
            recs = v.tile.recs
            dead = [r for r in recs if v.lo <= r[2] and r[3] <= v.hi]
            for r in dead:
                del recs[r]
            recs[("w", key, v.lo, v.hi)] = val

    def op(self, e, build, reads=(), writes=(), strict=()):
        reads = [r for r in reads if r is not None]
        writes = list(writes) + [r for r in reads if r.tile.kind == "psum"]
        reads = [r for r in reads if r.tile.kind != "psum"]
        self._deps(e, reads, writes, strict)
        key, val = self._eng_token(e)
        ins = build(self.eng[e])
        ins.then_inc(self.semh[key], 1)
        self.count[e] += 1
        self.n_inst += 1
        self._record(key, val, reads, writes)
        return ins

    def dma(self, q, out, in_, reads=None, writes=None, indirect=None, in_ap=None, out_ap=None, **kw):
        reads = [in_] if reads is None else reads
        writes = [out] if writes is None else writes
        in_ap = in_.ap if in_ap is None else in_ap
        out_ap = out.ap if out_ap is None else out_ap
        i = self.dma_rr
        self.dma_rr = (self.dma_rr + 1) % N_DMA_SEMS
        key = self.dma_sems[i]
        self._wait(q, key, self.dma_val[i])
        self._deps(q, reads, writes)
        self.dma_val[i] += 16
        val = self.dma_val[i]
        if indirect is not None:
            ins = indirect(self.eng[q])
        else:
            ins = self.eng[q].dma_start(out=out_ap, in_=in_ap, **kw)
        ins.then_inc(self.semh[key], 16)
        self.n_inst += 1
        self._record(key, val, reads, writes)
        return ins

    def barrier(self):
        toks = []
        for e in self.eng:
            c = self.count[e]
            if c == 0:
                continue
            ep = (c - 1) // EPOCH
            toks.append((self.esems[e][ep], ((c - 1) % EPOCH) + 1))
        for i, key in enumerate(self.dma_sems):
            if self.dma_val[i]:
                toks.append((key, self.dma_val[i]))
        for e in self.eng:
            for key, val in toks:
                if key[0] == "eng" and key[1] == e:
                    continue
                self._wait(e, key, val)

    def finish(self):
        self.barrier()
        self.es.close()

    def mm(self, out, lhsT, rhs, start=True, stop=True, **kw):
        return self.op("pe", lambda g: g.matmul(out.ap, lhsT.ap, rhs.ap, start=start, stop=stop, **kw),
                       reads=[lhsT, rhs], writes=[out])

    def transpose(self, out, in_, ident):
        return self.op("pe", lambda g: g.transpose(out.ap, in_.ap, ident.ap), reads=[in_, ident], writes=[out])

    def act(self, out, in_, func, bias=None, scale=None, accum=None, e="act"):
        kw = {}
        rd = [in_]
        wr = [out]
        sr = []
        if bias is not None:
            if isinstance(bias, V):
                kw["bias"] = bias.ap
                rd.append(bias)
                sr.append(bias)
            else:
                kw["bias"] = bias
        if scale is not None:
            if isinstance(scale, V):
                kw["scale"] = scale.ap
                rd.append(scale)
                sr.append(scale)
            else:
                kw["scale"] = scale
        if accum is not None:
            kw["accum_out"] = accum.ap
            wr.append(accum)
        return self.op(e, lambda g: g.activation(out.ap, in_.ap, func, **kw), reads=rd, writes=wr, strict=sr)

    def tt(self, e, out, a, b, op):
        return self.op(e, lambda g: g.tensor_tensor(out.ap, a.ap, b.ap, op), reads=[a, b], writes=[out])

    def ts(self, e, out, a, s1, s2, op0, op1=None, accum=None):
        rd = [a]
        wr = [out]
        sr = []
        s1a = s1
        s2a = s2
        if isinstance(s1, V):
            rd.append(s1)
            sr.append(s1)
            s1a = s1.ap
        if isinstance(s2, V):
            rd.append(s2)
            sr.append(s2)
            s2a = s2.ap
        kw = {}
        if op1 is not None:
            kw["op1"] = op1
        if accum is not None:
            kw["accum_out"] = accum.ap
            wr.append(accum)
        return self.op(e, lambda g: g.tensor_scalar(out.ap, a.ap, s1a, s2a, op0, **kw), reads=rd, writes=wr, strict=sr)

    def stt(self, e, out, a, s, b, op0, op1):
        rd = [a, b]
        sr = []
        sa = s
        if isinstance(s, V):
            rd.append(s)
            sr.append(s)
            sa = s.ap
        return self.op(e, lambda g: g.scalar_tensor_tensor(out.ap, a.ap, sa, b.ap, op0, op1), reads=rd, writes=[out], strict=sr)

    def copy(self, e, out, in_):
        if e == "act":
            return self.op(e, lambda g: g.copy(out.ap, in_.ap), reads=[in_], writes=[out])
        return self.op(e, lambda g: g.tensor_copy(out.ap, in_.ap), reads=[in_], writes=[out])

    def memset(self, e, out, val):
        return self.op(e, lambda g: g.memset(out.ap, val), reads=[], writes=[out])

    def recip(self, out, in_):
        return self.op("dve", lambda g: g.reciprocal(out.ap, in_.ap), reads=[in_], writes=[out])


import numpy as np
import math
from contextlib import ExitStack

S = 2048
D = 1024
NT = 16
ALPHA = (2.0 * 2) ** 0.25
EPS = 1e-6


class Rot:
    def __init__(self, items):
        self.items, self.i = list(items), 0

    def __call__(self):
        x = self.items[self.i % len(self.items)]
        self.i += 1
        return x


def pv(bank, p0, p1, c0, c1, shape=None):
    ap = bank.h[p0:p1, c0:c1]
    if shape is not None:
        ap = ap.rearrange(shape[0], **shape[1])
    return V(bank, ap, 0, bank.size)


def wsrc(h, r0, r1, c0, c1):
    return h[r0:r1, c0:c1].rearrange("(c p) n -> p c n", p=128)


class Ctx:
    pass


class Cut(Exception):
    pass


def cut(C, n):
    if getattr(C, "cut", None) == n:
        raise Cut()


def setup(mk, dbg=False):
    C = Ctx()
    C.mk = mk
    C.dbg = dbg
    C.dumps = {}
    d = lambda n, s, t=F32: mk.dram(n, s, t, kind="ExternalInput")
    C.x_in = d("x_in", [2, S, D])
    C.pT = d("pT", [2, 2, 256, S])
    C.w_in_even = d("w_in_even", [D, 1984])
    C.w_kpe_sw = d("w_kpe_sw", [D, 32])
    C.w_uq = d("w_uq", [256, 768])
    C.w_uq_sw = d("w_uq_sw", [256, 256])
    C.w_ukv = d("w_ukv", [128, 1024])
    C.mla_qn = d("mla_qn", [128, 2])
    C.mla_kvn = d("mla_kvn", [128, 1])
    C.gla_gw = d("gla_gw", [2, 17, 256])
    C.gla_norm_bc = d("gla_norm_bc", [128, 512])
    C.w_in_odd = d("w_in_odd", [D, 1536])
    C.w_q_sw = d("w_q_sw", [D, 1024])
    C.w_k_sw = d("w_k_sw", [D, 256])
    C.gqa_n = d("gqa_n", [128, 4])
    C.w_o = d("w_o", [2, D, D])
    C.lnbc = d("lnbc", [2, 5, 128, D])
    C.router_w = d("router_w", [2, D, 16])
    C.w1 = d("w1", [2, 16, D, D])
    C.w3 = d("w3", [2, 16, D, D])
    C.w2 = d("w2", [2, 16, D, D])
    C.ple_gate_w = d("ple_gate_w", [2, D, D])
    C.ple_w = d("ple_w", [2, 256, D])
    C.cmat = d("cmat", [7, 128, 128])
    C.cmask = d("cmask", [2, 128, 512])
    C.rope_a = d("rope_a", [2, 128, S])
    C.rope_c = d("rope_c", [2, 128, S])
    C.out = mk.dram("out", [2, S, D], F32, kind="ExternalOutput")
    C.acc = [[mk.dram("acc_%d_%d" % (L, s), [S, D], F32) for s in range(2)] for L in range(2)]
    C.xrows = [[mk.dram("xrows_%d_%d" % (L, s), [S, D], BF16) for s in range(2)] for L in range(2)]
    C.xln = [mk.dram("xln_%d" % s, [S, D], F32) for s in range(2)]
    C.P = [mk.ps("pb%d" % i, [128, 512], F32) for i in range(7)]
    C.PH = mk.ps("pbh", [128, 1024], BF16)
    names = ["ident", "ones", "bd_ones", "triF", "triS", "triB", "triSB"]
    C.cm = mk.sb("cm", [128, 7, 128], F32)
    mk.dma("sp", C.cm.all(), None, reads=[], in_ap=C.cmat.h[:, :, :].rearrange("k p n -> p k n"))
    for i, n in enumerate(names):
        setattr(C, n, C.cm[:, i, :])
    C.cmb = mk.sb("cmb", [128, 7, 128], BF16)
    mk.copy("dve", C.cmb.all(), C.cm.all())
    for i, n in enumerate(names):
        setattr(C, n + "b", C.cmb[:, i, :])
    C.maskt = mk.sb("maskt", [128, 2, 512], BF16)
    mk.dma("pool", C.maskt.all(), None, reads=[], in_ap=C.cmask.h[:, :, :].rearrange("k p n -> p k n"))
    C.eps = mk.sb("eps", [128, 1], F32)
    mk.memset("dve", C.eps.all(), EPS)
    return C


def dump(C, name, view, shape, dtype):
    if not C.dbg:
        return
    mk = C.mk
    t = mk.dram("dbg_" + name, shape, dtype, kind="ExternalOutput")
    mk.dma("sp", t.all(), view)
    C.dumps[name] = "dbg_" + name


def layer_norm_tile(mk, C, xt, g, b, tmp, st):
    FM = 512
    for j in range(2):
        mk.op("dve", lambda e, j=j: e.bn_stats(st[:, j * 6:(j + 1) * 6].ap, xt[:, j * FM:(j + 1) * FM].ap),
              reads=[xt[:, j * FM:(j + 1) * FM]], writes=[st[:, j * 6:(j + 1) * 6]])
    mk.op("dve", lambda e: e.bn_aggr(st[:, 12:14].ap, st.h[:, 0:12].rearrange("p (n k) -> p n k", k=6)),
          reads=[st[:, 0:12]], writes=[st[:, 12:14]])
    mk.act(st[:, 14:15], st[:, 13:14], AF.Sqrt, bias=C.eps[:, 0:1])
    mk.recip(st[:, 15:16], st[:, 14:15])
    mk.ts("dve", tmp.all(), xt.all(), st[:, 12:13], st[:, 15:16], ALU.subtract, ALU.mult)
    mk.tt("pool", tmp.all(), tmp.all(), g, ALU.mult)
    mk.tt("dve", xt.all(), tmp.all(), b, ALU.add)


def transpose_tile_to_xT(mk, C, xt, xT, tt, banks, ei):
    for g in range(2):
        bank = banks()
        for j in range(4):
            dc = g * 4 + j
            mk.transpose(pv(bank, 0, 128, j * 128, (j + 1) * 128), xt[:, dc * 128:(dc + 1) * 128], C.ident)
        src = pv(bank, 0, 128, 0, 512, ("p (c n) -> p c n", dict(c=4)))
        dst = xT[:, g * 4:(g + 1) * 4, tt * 128:(tt + 1) * 128]
        mk.copy(ei(), dst, src)


def phase_prologue(mk, C, L, s, xT, ln_g=None, ln_b=None, stk=None):
    src = C.x_in if L == 0 else C.acc[0][s]
    xts = [mk.sb("pro_x%d" % i, [128, D], F32, stk) for i in range(3)]
    tmp = mk.sb("pro_tmp", [128, D], F32, stk)
    sts = [mk.sb("pro_st%d" % i, [128, 16], F32, stk) for i in range(2)]
    banks = Rot([C.P[0], C.P[1]])
    ei = Rot(["act", "dve"])
    for tt in range(NT):
        xt = xts[tt % 3]
        if L == 0:
            mk.dma("sp", xt.all(), src[s, tt * 128:(tt + 1) * 128, :], reads=[])
        else:
            mk.dma("sp", xt.all(), src[tt * 128:(tt + 1) * 128, :])
            layer_norm_tile(mk, C, xt, ln_g, ln_b, tmp, sts[tt % 2])
            mk.dma("sp", C.xln[s][tt * 128:(tt + 1) * 128, :], xt.all())
        transpose_tile_to_xT(mk, C, xt, xT, tt, banks, ei)


def proj_fm(mk, out_ps, w, c0, c1, xT, t0, t1, nk=8):
    for kc in range(nk):
        mk.mm(out_ps, w[:, kc, c0:c1], xT[:, kc, t0:t1], start=(kc == 0), stop=(kc == nk - 1))


def rope_evac(mk, C, psA, psB, rope, p0, p1, t0, t1, dst, tmpa, tmpb):
    mk.tt("dve", tmpa[p0:p1, 0:t1 - t0], psA, rope[p0:p1, 0, t0:t1], ALU.mult)
    mk.tt("dve", tmpb[p0:p1, 0:t1 - t0], psB, rope[p0:p1, 1, t0:t1], ALU.mult)
    mk.tt("pool", dst, tmpa[p0:p1, 0:t1 - t0], tmpb[p0:p1, 0:t1 - t0], ALU.add)


def attention(mk, C, QT, KT, VA, mixT, nheads, kdim, scale, kmap, stk):
    pts = [mk.sb("att_pt%d" % i, [128, 512], BF16, stk) for i in range(3)]
    rec = [mk.sb("att_rec%d" % i, [128, 512], F32, stk) for i in range(2)]
    sbk = [C.P[0], C.P[1], C.P[2]]
    obk = [C.P[3], C.P[4]]
    steps = [(h, qb, kt) for h in range(nheads) for qb in range(4) for kt in range(NT)]

    def qk(i):
        h, qb, kt = steps[i]
        qb0, qs, ks, vs, oc, ob0 = kmap(h)
        mk.mm(pv(sbk[i % 3], 0, 128, 0, 512), KT[qb0:qb0 + kdim, ks, kt * 128:(kt + 1) * 128],
              QT[qb0:qb0 + kdim, qs, qb * 512:(qb + 1) * 512])

    qk(0)
    for i, (h, qb, kt) in enumerate(steps):
        qb0, qs, ks, vs, oc, ob0 = kmap(h)
        if i + 1 < len(steps):
            qk(i + 1)
        obank = obk[(h * 4 + qb) % 2]
        pt = pts[i % 3]
        mk.act(pt.all(), pv(sbk[i % 3], 0, 128, 0, 512), AF.Exp, scale=scale)
        mk.mm(pv(obank, 0, 128, 0, 512), VA[:, kt, vs, :], pt.all(), start=(kt == 0), stop=(kt == NT - 1))
        if kt == NT - 1:
            r = rec[(h * 4 + qb) % 2]
            mk.recip(r[ob0:ob0 + 64, :], pv(obank, 64, 128, 0, 512))
            mk.tt("dve", mixT[ob0:ob0 + 64, oc, qb * 512:(qb + 1) * 512], pv(obank, 0, 64, 0, 512),
                  r[ob0:ob0 + 64, :], ALU.mult)


def even_mla(mk, C, s, xT, mixT):
    stk = ExitStack()
    we = C.w_in_even.h
    wq = mk.sb("mla_wq", [128, 8, 256], BF16, stk)
    wkv = mk.sb("mla_wkv", [128, 8, 128], BF16, stk)
    wkpe = mk.sb("mla_wkpe", [128, 8, 2, 96], BF16, stk)
    wuq = mk.sb("mla_wuq", [128, 2, 768], BF16, stk)
    wuqs = mk.sb("mla_wuqs", [128, 2, 8, 96], BF16, stk)
    wukv = mk.sb("mla_wukv", [128, 1024], BF16, stk)
    qn = mk.sb("mla_qn", [128, 2], F32, stk)
    kvn = mk.sb("mla_kvn", [128, 1], F32, stk)
    rope = mk.sb("mla_rope", [128, 2, S], F32, stk)
    mk.dma("pool", wq.all(), None, reads=[], in_ap=wsrc(we, 0, D, 0, 256))
    mk.dma("pool", wkv.all(), None, reads=[], in_ap=wsrc(we, 0, D, 256, 384))
    mk.memset("pool", wkpe.all(), 0.0)
    mk.memset("pool", wuqs.all(), 0.0)
    mk.dma("pool", wkpe[:, :, 0, 64:96], None, reads=[], in_ap=wsrc(we, 0, D, 384, 416))
    mk.dma("pool", wkpe[:, :, 1, 64:96], None, reads=[], in_ap=wsrc(C.w_kpe_sw.h, 0, D, 0, 32))
    mk.dma("pool", wuq.all(), None, reads=[], in_ap=wsrc(C.w_uq.h, 0, 256, 0, 768))
    for kc in range(2):
        mk.dma("pool", wuqs[:, kc, :, 64:96], None, reads=[],
               in_ap=C.w_uq_sw.h[kc * 128:(kc + 1) * 128, :].rearrange("p (h e) -> p h e", h=8))
    mk.dma("pool", wukv.all(), None, reads=[], in_ap=C.w_ukv.h[:, :])
    mk.dma("sp", qn.all(), None, reads=[], in_ap=C.mla_qn.h[:, :])
    mk.dma("sp", kvn.all(), None, reads=[], in_ap=C.mla_kvn.h[:, :])
    mk.dma("sp", rope[64:96, :, :], None, reads=[], in_ap=C.rope_a.h[:, 64:96, :].rearrange("k p n -> p k n"))

    cqn = mk.sb("mla_cqn", [128, 2, S], BF16, stk)
    ckvn = mk.sb("mla_ckvn", [128, S], BF16, stk)
    kper = mk.sb("mla_kper", [128, S], BF16, stk)
    QT = mk.sb("mla_QT", [128, 4, S], BF16, stk)
    KT = mk.sb("mla_KT", [128, 4, S], BF16, stk)
    VA = mk.sb("mla_VA", [128, NT, 4, 128], BF16, stk)
    st2 = ExitStack()
    cqf = [mk.sb("mla_cqf%d" % i, [128, 512], F32, st2) for i in range(3)]
    sq = [mk.sb("mla_sq%d" % i, [128, 512], F32, st2) for i in range(3)]
    rs = [mk.sb("mla_rs%d" % i, [128, 512], F32, st2) for i in range(2)]
    sqh = [mk.sb("mla_sqh%d" % i, [128, 2, 512], BF16, st2) for i in range(3)]
    tmpa = mk.sb("mla_tmpa", [128, 512], F32, st2)
    tmpb = mk.sb("mla_tmpb", [128, 512], F32, st2)
    P = C.P
    cut(C, 1)
    for tb in range(4):
        t0, t1 = tb * 512, (tb + 1) * 512
        groups = [(wq, 0, 128, qn[:, 0:1], cqn[:, 0, t0:t1]),
                  (wq, 128, 256, qn[:, 1:2], cqn[:, 1, t0:t1]),
                  (wkv, 0, 128, kvn[:, 0:1], ckvn[:, t0:t1])]
        for gi, (w, c0, c1, gain, dst) in enumerate(groups):
            bank = P[gi]
            proj_fm(mk, pv(bank, 0, 128, 0, 512), w, c0, c1, xT, t0, t1)
            mk.act(sq[gi].all(), pv(bank, 0, 128, 0, 512), AF.Square)
            mk.copy("dve", cqf[gi].all(), pv(bank, 0, 128, 0, 512))
        cut(C, 2)
        for gi in range(3):
            mk.copy("pool", sqh[gi][:, 0, :], sq[gi].all())
            mk.tt("pool", sqh[gi][:, 1, :], sq[gi].all(), sqh[gi][:, 0, :], ALU.subtract)
        for j, (gi, hl) in enumerate([(0, 0), (0, 1), (1, 0), (1, 1)]):
            mk.mm(pv(P[3], 0, 128, 0, 512), C.onesb, sqh[gi][:, hl, :], start=(j == 0), stop=(j == 3))
        for hl in range(2):
            mk.mm(pv(P[4], 0, 128, 0, 512), C.onesb, sqh[2][:, hl, :], start=(hl == 0), stop=(hl == 1))
        cut(C, 3)
        mk.act(rs[0].all(), pv(P[3], 0, 128, 0, 512), AF.Sqrt, bias=C.eps[:, 0:1], scale=1.0 / 256)
        mk.recip(rs[0].all(), rs[0].all())
        mk.act(rs[1].all(), pv(P[4], 0, 128, 0, 512), AF.Sqrt, bias=C.eps[:, 0:1], scale=1.0 / 128)
        mk.recip(rs[1].all(), rs[1].all())
        for gi, (w, c0, c1, gain, dst) in enumerate(groups):
            mk.stt("dve", dst, cqf[gi].all(), gain, rs[0 if gi < 2 else 1].all(), ALU.mult, ALU.mult)
        cut(C, 4)
        for kc in range(8):
            mk.mm(pv(P[5], 0, 96, 0, 512), wkpe[:, kc, 0, :], xT[:, kc, t0:t1], start=(kc == 0), stop=(kc == 7))
        for kc in range(8):
            mk.mm(pv(P[6], 0, 96, 0, 512), wkpe[:, kc, 1, :], xT[:, kc, t0:t1], start=(kc == 0), stop=(kc == 7))
        cut(C, 5)
        mk.tt("dve", tmpa[64:96, :], pv(P[5], 64, 96, 0, 512), rope[64:96, 0, t0:t1], ALU.mult)
        mk.tt("dve", tmpb[64:96, :], pv(P[6], 64, 96, 0, 512), rope[64:96, 1, t0:t1], ALU.mult)
        mk.tt("pool", kper[64:96, t0:t1], tmpa[64:96, :], tmpb[64:96, :], ALU.add)
        cut(C, 6)
    stop = getattr(C, "stop", 99)
    if stop <= 1:
        dump(C, "cqn", cqn.all(), [128, 2, S], BF16)
        dump(C, "kper", kper[64:96, :], [32, S], BF16)
    for hg in range(2 if stop > 1 else 0):
        mk.memset("pool", VA.all(), 1.0)
        for tb in range(4):
            t0, t1 = tb * 512, (tb + 1) * 512
            ab = Rot([P[0], P[1]])
            bb = Rot([P[2], P[5]])
            kb = Rot([P[3], P[4]])
            for hl in range(4):
                h = hg * 4 + hl
                A = ab()
                B = bb()
                K = kb()
                for kc in range(2):
                    mk.mm(pv(A, 0, 96, 0, 512), wuq[:, kc, h * 96:(h + 1) * 96], cqn[:, kc, t0:t1], start=(kc == 0), stop=(kc == 1))
                for kc in range(2):
                    mk.mm(pv(B, 0, 96, 0, 512), wuqs[:, kc, h, :], cqn[:, kc, t0:t1], start=(kc == 0), stop=(kc == 1))
                mk.mm(pv(K, 0, 64, 0, 512), wukv[:, h * 128:h * 128 + 64], ckvn[:, t0:t1])
                mk.copy("act", QT[0:64, hl, t0:t1], pv(A, 0, 64, 0, 512))
                ta, tb_ = (tmpa, tmpb) if hl % 2 == 0 else (sq[0], sq[1])
                mk.tt("dve", ta[64:96, :], pv(A, 64, 96, 0, 512), rope[64:96, 0, t0:t1], ALU.mult)
                mk.tt("dve", tb_[64:96, :], pv(B, 64, 96, 0, 512), rope[64:96, 1, t0:t1], ALU.mult)
                mk.tt("pool", QT[64:96, hl, t0:t1], ta[64:96, :], tb_[64:96, :], ALU.add)
                mk.copy("act", KT[0:64, hl, t0:t1], pv(K, 0, 64, 0, 512))
                mk.copy("pool", KT[64:96, hl, t0:t1], kper[64:96, t0:t1])
            for j in range(4):
                tt = tb * 4 + j
                bank = P[6]
                rhs_ap = wukv.h[:, hg * 512:(hg + 1) * 512].rearrange("p (h e) -> p h e", h=4)[:, :, 64:128]
                rhs = V(wukv, rhs_ap, hg * 512, (hg + 1) * 512)
                mk.mm(pv(bank, 0, 128, 0, 256, ("p (h e) -> p h e", dict(h=4))), ckvn[:, tt * 128:(tt + 1) * 128], rhs)
                mk.copy("act" if j % 2 else "dve", VA[:, tt, :, 0:64], pv(bank, 0, 128, 0, 256, ("p (h e) -> p h e", dict(h=4))))
        if C.dbg and s == 0:
            dump(C, "QT%d" % hg, QT.all(), [128, 4, S], BF16)
            dump(C, "KT%d" % hg, KT.all(), [128, 4, S], BF16)
            dump(C, "VA%d" % hg, VA.all(), [128, NT, 4, 128], BF16)
        if stop <= 2:
            continue
        st3 = ExitStack()
        attention(mk, C, QT, KT, VA, mixT, 4, 96, 96.0 ** -0.5,
                  lambda hl: (0, hl, hl, hl, (hg * 4 + hl) // 2, (hl % 2) * 64), st3)
        st3.close()
    st2.close()
    stk.close()
    mk.barrier()


def even_gla(mk, C, s, xT, mixT):
    stk = ExitStack()
    we = C.w_in_even.h
    P = C.P
    wfm = mk.sb("gla_wfm", [128, 8, 512], BF16, stk)
    wtm = mk.sb("gla_wtm", [128, 8, 1280], BF16, stk)
    wlr = mk.sb("gla_wlr", [128, 8, 32], BF16, stk)
    gw = mk.sb("gla_gw", [17, 2, 256], BF16, stk)
    gnorm = mk.sb("gla_gnorm", [128, 512], F32, stk)
    one1 = mk.sb("gla_one1", [128, 1], F32, stk)
    mk.memset("dve", one1.all(), 1.0)
    mk.dma("pool", wfm.all(), None, reads=[], in_ap=wsrc(we, 0, D, 416, 928))
    mk.dma("pool", wtm[:, :, 0:768], None, reads=[], in_ap=wsrc(we, 0, D, 672, 1440))
    mk.dma("pool", wtm[:, :, 768:1280], None, reads=[], in_ap=wsrc(we, 0, D, 1472, 1984))
    mk.dma("pool", wlr.all(), None, reads=[], in_ap=wsrc(we, 0, D, 1440, 1472))
    mk.dma("pool", gw.all(), None, reads=[], in_ap=C.gla_gw.h[:, :, :].rearrange("k r n -> r k n"))
    mk.dma("sp", gnorm.all(), None, reads=[], in_ap=C.gla_norm_bc.h[:, :])
    gqT = mk.sb("gla_gqT", [128, 2, S], BF16, stk)
    gkT = mk.sb("gla_gkT", [128, 2, S], BF16, stk)
    gk_tok = mk.sb("gla_gk_tok", [128, NT, 256], BF16, stk)
    gv_tok = mk.sb("gla_gv_tok", [128, NT, 512], BF16, stk)
    o_f = mk.sb("gla_of", [128, NT, 512], F32, stk)
    ei = Rot(["act", "dve"])
    bk = Rot([P[0], P[1], P[2]])
    for tb in range(4):
        t0, t1 = tb * 512, (tb + 1) * 512
        for mc in range(4):
            bank = bk()
            proj_fm(mk, pv(bank, 0, 128, 0, 512), wfm, mc * 128, (mc + 1) * 128, xT, t0, t1)
            dst = (gqT if mc < 2 else gkT)[:, mc % 2, t0:t1]
            mk.copy(ei(), dst, pv(bank, 0, 128, 0, 512))
    for tt in range(NT):
        tk = slice(tt * 128, (tt + 1) * 128)
        b1, b2 = bk(), bk()
        for kc in range(8):
            mk.mm(pv(b1, 0, 128, 0, 256), xT[:, kc, tk], wtm[:, kc, 0:256], start=(kc == 0), stop=(kc == 7))
        for kc in range(8):
            mk.mm(pv(b2, 0, 128, 0, 512), xT[:, kc, tk], wtm[:, kc, 256:768], start=(kc == 0), stop=(kc == 7))
        mk.copy("act", gk_tok[:, tt, :], pv(b1, 0, 128, 0, 256))
        mk.copy("dve", gv_tok[:, tt, :], pv(b2, 0, 128, 0, 512))
    cut(C, 11)
    st2 = ExitStack()
    lrT = [mk.sb("gla_lrT%d" % i, [17, 128], BF16, st2) for i in range(2)]
    for t in lrT:
        mk.memset("dve", t.all(), 1.0)
    ez = [mk.sb("gla_ez%d" % i, [128, 256], F32, st2) for i in range(2)]
    nla = [mk.sb("gla_nla%d" % i, [128, 256], F32, st2) for i in range(2)]
    nlah = [mk.sb("gla_nlah%d" % i, [128, 2, 256], BF16, st2) for i in range(2)]
    E1 = [mk.sb("gla_E1%d" % i, [128, 2, 128], F32, st2) for i in range(3)]
    E2 = [mk.sb("gla_E2%d" % i, [128, 2, 128], F32, st2) for i in range(2)]
    E3 = [mk.sb("gla_E3%d" % i, [128, 256], F32, st2) for i in range(2)]
    ke = [mk.sb("gla_ke%d" % i, [128, 256], BF16, st2) for i in range(2)]
    qgT = [mk.sb("gla_qgT%d" % i, [128, 2, 128], BF16, st2) for i in range(2)]
    kgT = [mk.sb("gla_kgT%d" % i, [128, 2, 128], BF16, st2) for i in range(2)]
    attm = [mk.sb("gla_attm%d" % i, [128, 2, 2, 128], BF16, st2) for i in range(2)]
    Sf = mk.sb("gla_Sf", [128, 2, 128], F32, st2)
    Sb = [mk.sb("gla_Sb%d" % i, [128, 2, 128], BF16, st2) for i in range(4)]
    osum = [mk.sb("gla_osum%d" % i, [128, 512], F32, st2) for i in range(1)] * 2
    osq = mk.sb("gla_osq", [128, 512], F32, st2)
    sg = [mk.sb("gla_sg%d" % i, [128, 512], F32, st2) for i in range(1)] * 2
    og = [mk.sb("gla_og%d" % i, [128, 512], BF16, st2) for i in range(1)] * 2
    stt_ = [mk.sb("gla_st%d" % i, [128, 12], F32, st2) for i in range(2)]
    it = 0
    sbi = 0
    for d in range(2):
        tri = C.triFb if d == 0 else C.triBb
        tris = C.triSb if d == 0 else C.triSBb
        mk.memset("dve", Sf.all(), 0.0)
        mk.memset("dve", Sb[sbi % 4].all(), 0.0)
        tiles = range(NT) if d == 0 else range(NT - 1, -1, -1)
        order = [0, 1] if d == 0 else [1, 0]
        for tt in tiles:
            i2 = it % 2
            it += 1
            tk = slice(tt * 128, (tt + 1) * 128)
            for kc in range(8):
                mk.mm(pv(P[0], 0, 16, 0, 128), wlr[:, kc, d * 16:(d + 1) * 16], xT[:, kc, tk], start=(kc == 0), stop=(kc == 7))
            mk.copy("act", lrT[i2][0:16, :], pv(P[0], 0, 16, 0, 128))
            mk.mm(pv(P[0], 0, 128, 128, 384), lrT[i2][0:17, :], gw[0:17, d, :])
            mk.act(ez[i2].all(), pv(P[0], 0, 128, 128, 384), AF.Exp, scale=-1.0)
            mk.act(nla[i2].all(), ez[i2].all(), AF.Ln, bias=one1[:, 0:1])
            mk.copy("pool", nlah[i2][:, 0, :], nla[i2].all())
            mk.tt("pool", nlah[i2][:, 1, :], nla[i2].all(), nlah[i2][:, 0, :], ALU.subtract)
            cut(C, 12)
            for pc in range(2):
                for hl in range(2):
                    mk.mm(pv(P[1], 0, 128, pc * 128, (pc + 1) * 128), nlah[i2][:, hl, pc * 128:(pc + 1) * 128], tri,
                          start=(hl == 0), stop=(hl == 1))
            for hl in range(2):
                mk.mm(pv(P[1], 0, 128, 256, 512), tris, nlah[i2][:, hl, :], start=(hl == 0), stop=(hl == 1))
            e1 = E1[it % 3]
            cumv = pv(P[1], 0, 128, 0, 256, ("p (c n) -> p c n", dict(c=2)))
            mk.act(e1.all(), cumv, AF.Exp, scale=-1.0 / 16)
            mk.act(E2[i2].all(), cumv, AF.Exp, scale=1.0 / 16)
            mk.act(E3[i2].all(), pv(P[1], 0, 128, 256, 512), AF.Exp, scale=-1.0 / 16)
            mk.tt("pool", ke[i2].all(), gk_tok[:, tt, :], E3[i2].all(), ALU.mult)
            mk.stt("dve", qgT[i2].all(), gqT[:, :, tk], 0.125, e1.all(), ALU.mult, ALU.mult)
            mk.tt("dve", kgT[i2].all(), gkT[:, :, tk], E2[i2].all(), ALU.mult)
            cut(C, 13)
            for h in range(4):
                pc, b0 = h // 2, (h % 2) * 64
                bank = P[2] if h % 2 == 0 else P[5]
                mk.mm(pv(bank, 0, 128, pc * 128, (pc + 1) * 128), kgT[i2][b0:b0 + 64, pc, :], qgT[i2][b0:b0 + 64, pc, :])
            for hp in range(2):
                bank = P[2] if hp == 0 else P[5]
                mk.tt("dve", attm[i2][:, :, hp, :], pv(bank, 0, 128, 0, 256, ("p (a n) -> p a n", dict(a=2))),
                      V(C.maskt, C.maskt.h[:, d, 0:256].rearrange("p (a n) -> p a n", a=2), d * 512, d * 512 + 256), ALU.mult)
            cut(C, 14)
            for c in range(2):
                ubank = P[3] if c == 0 else P[6]
                for h in range(4):
                    pc, j = h // 2, h % 2
                    mk.mm(pv(ubank, j * 64, j * 64 + 64, pc * 128, (pc + 1) * 128), ke[i2][c * 64:(c + 1) * 64, h * 64:(h + 1) * 64],
                          gv_tok[c * 64:(c + 1) * 64, tt, h * 128:(h + 1) * 128])
            cut(C, 15)
            sb_for = {}
            for c in order:
                sb_for[c] = Sb[sbi % 4]
                dcol = c * 64 + (63 if d == 0 else 0)
                ubank = P[3] if c == 0 else P[6]
                for pc in range(2):
                    mk.stt("dve", Sf[:, pc, :], Sf[:, pc, :], e1[:, pc, dcol:dcol + 1], pv(ubank, 0, 128, pc * 128, (pc + 1) * 128), ALU.mult, ALU.add)
                sbi += 1
                mk.copy("pool", Sb[sbi % 4].all(), Sf.all())
            cut(C, 16)
            for h in range(4):
                pc, b0 = h // 2, (h % 2) * 64
                obank = P[4] if h % 2 == 0 else P[1]
                mk.mm(pv(obank, 0, 128, pc * 128, (pc + 1) * 128), attm[i2][:, pc, h % 2, :], gv_tok[:, tt, h * 128:(h + 1) * 128],
                      start=True, stop=False, skip_group_check=True)
                for ci, c in enumerate(order):
                    mk.mm(pv(obank, c * 64, c * 64 + 64, pc * 128, (pc + 1) * 128), qgT[i2][b0:b0 + 64, pc, c * 64:(c + 1) * 64],
                          sb_for[c][b0:b0 + 64, pc, :], start=False, stop=(ci == 1), skip_group_check=True)
            of4 = o_f.h[:, tt, :].rearrange("p (a b e) -> p a b e", a=2, b=2)
            if d == 0:
                for hp in range(2):
                    obank = P[4] if hp == 0 else P[1]
                    mk.copy("act", V(o_f, of4[:, :, hp, :], tt * 512, (tt + 1) * 512),
                            pv(obank, 0, 128, 0, 256, ("p (a e) -> p a e", dict(a=2))))
                cut(C, 17)
                continue
            os_ = osum[i2]
            os4 = os_.h[:, :].rearrange("p (a b e) -> p a b e", a=2, b=2)
            for hp in range(2):
                obank = P[4] if hp == 0 else P[1]
                mk.tt("dve", V(os_, os4[:, :, hp, :], 0, 512), V(o_f, of4[:, :, hp, :], tt * 512, (tt + 1) * 512),
                      pv(obank, 0, 128, 0, 256, ("p (a e) -> p a e", dict(a=2))), ALU.add)
            if C.dbg and s == 0 and getattr(C, "dbg_osum_on", False):
                if "osum" not in C.dumps:
                    C.dbg_osum = mk.dram("dbg_osum", [128, NT, 512], F32, kind="ExternalOutput")
                    C.dumps["osum"] = 1
                mk.dma("sp", C.dbg_osum[:, tt, :], os_.all())
            mk.tt("pool", osq.all(), os_.all(), os_.all(), ALU.mult)
            st = stt_[i2]
            osq3 = V(osq, osq.h[:, :].rearrange("p (h e) -> p h e", h=4), 0, 512)
            mk.op("dve", lambda e, st=st, osq3=osq3: e.tensor_reduce(st[:, 0:4].ap, osq3.ap, AX.X, ALU.add), reads=[osq3], writes=[st[:, 0:4]])
            mk.act(st[:, 4:8], st[:, 0:4], AF.Sqrt, bias=C.eps[:, 0:1], scale=1.0 / 128)
            mk.recip(st[:, 8:12], st[:, 4:8])
            for kc in range(8):
                mk.mm(pv(P[0], 0, 128, 0, 512), xT[:, kc, tk], wtm[:, kc, 768:1280], start=(kc == 0), stop=(kc == 7))
            mk.act(sg[i2].all(), pv(P[0], 0, 128, 0, 512), AF.Silu)
            for h in range(4):
                mk.ts("dve" if h % 2 else "pool", os_[:, h * 128:(h + 1) * 128], os_[:, h * 128:(h + 1) * 128], st[:, 8 + h:9 + h], None, ALU.mult)
            mk.tt("pool", os_.all(), os_.all(), gnorm.all(), ALU.mult)
            mk.tt("dve", og[i2].all(), os_.all(), sg[i2].all(), ALU.mult)
            for h in range(4):
                mk.transpose(pv(C.PH, 0, 128, h * 128, (h + 1) * 128), og[i2][:, h * 128:(h + 1) * 128], C.identb)
            mk.copy("act", mixT[:, 4:8, tk], pv(C.PH, 0, 128, 0, 512, ("p (c n) -> p c n", dict(c=4))))
    if C.dbg and s == 0:
        dump(C, "o_f", o_f.all(), [128, NT, 512], F32)
    st2.close()
    stk.close()
    mk.barrier()


def phase_post(mk, C, L, s, mixT, idx_s, gate_s):
    stk = ExitStack()
    P = C.P
    wo = mk.sb("po_wo", [128, 8, D], BF16, stk)
    wg = mk.sb("po_wg", [128, 8, D], BF16, stk)
    wp = mk.sb("po_wp", [128, 2, D], BF16, stk)
    pT = mk.sb("po_pT", [128, 2, S], BF16, stk)
    rwf = mk.sb("po_rwf", [128, 8, 16], F32, stk)
    rw = mk.sb("po_rw", [128, 8, 2, 16], BF16, stk)
    lnb = mk.sb("po_lnb", [128, 5, D], F32, stk)
    affT = mk.sb("po_affT", [16, S], F32, stk)
    mk.dma("pool", wo.all(), None, reads=[], in_ap=wsrc(C.w_o.h[L], 0, D, 0, D))
    mk.dma("pool", wg.all(), None, reads=[], in_ap=wsrc(C.ple_gate_w.h[L], 0, D, 0, D))
    mk.dma("pool", wp.all(), None, reads=[], in_ap=wsrc(C.ple_w.h[L], 0, 256, 0, D))
    mk.dma("pool", pT.all(), None, reads=[], in_ap=C.pT.h[L, s].rearrange("(c p) n -> p c n", p=128))
    mk.dma("sp", rwf.all(), None, reads=[], in_ap=wsrc(C.router_w.h[L], 0, D, 0, 16))
    mk.dma("sp", lnb.all(), None, reads=[], in_ap=C.lnbc.h[L].rearrange("k p n -> p k n"))
    mk.copy("dve", rw[:, :, 0, :], rwf.all())
    mk.tt("dve", rw[:, :, 1, :], rwf.all(), rw[:, :, 0, :], ALU.subtract)
    xts = [mk.sb("po_xt%d" % i, [128, D], F32, stk) for i in range(2)]
    rts = [mk.sb("po_r%d" % i, [128, D], F32, stk) for i in range(2)]
    x1bs = [mk.sb("po_x1b%d" % i, [128, D], BF16, stk) for i in range(2)]
    x1Th = [mk.sb("po_x1Th%d" % i, [128, 8, 128], BF16, stk) for i in range(2)]
    x1Tl = [mk.sb("po_x1Tl%d" % i, [128, 8, 128], BF16, stk) for i in range(2)]
    accs = [mk.sb("po_acc%d" % i, [128, D], F32, stk) for i in range(2)]
    tmp = mk.sb("po_tmp", [128, D], F32, stk)
    gbs = [mk.sb("po_gb%d" % i, [128, 512], F32, stk) for i in range(2)]
    sgs = [mk.sb("po_sg%d" % i, [128, 512], F32, stk) for i in range(2)]
    sts = [mk.sb("po_st%d" % i, [128, 16], F32, stk) for i in range(2)]
    sm = [mk.sb("po_sm%d" % i, [128, 40], F32, stk) for i in range(2)]
    xsrc = C.x_in if L == 0 else C.xln[s]
    for tt in range(NT):
        i2 = tt % 2
        tk = slice(tt * 128, (tt + 1) * 128)
        xt, r, x1b, acc = xts[i2], rts[i2], x1bs[i2], accs[i2]
        if L == 0:
            mk.dma("sp", xt.all(), xsrc[s, tk, :], reads=[])
        else:
            mk.dma("sp", xt.all(), xsrc[tk, :])
        for half in range(2):
            for kc in range(8):
                mk.mm(pv(P[half], 0, 128, 0, 512), mixT[:, kc, tk], wo[:, kc, half * 512:(half + 1) * 512], start=(kc == 0), stop=(kc == 7))
        for half in range(2):
            hs = slice(half * 512, (half + 1) * 512)
            mk.stt("dve", r[:, hs], xt[:, hs], ALPHA, pv(P[half], 0, 128, 0, 512), ALU.mult, ALU.add)
        layer_norm_tile(mk, C, r, lnb[:, 0, :], lnb[:, 1, :], tmp, sts[i2])
        mk.copy("act", x1b.all(), r.all())
        mk.dma("sp", C.xrows[L][s][tk, :], x1b.all())
        for g in range(2):
            bank = P[2 + g]
            for j in range(4):
                dc = g * 4 + j
                mk.transpose(pv(bank, 0, 128, j * 128, (j + 1) * 128), r[:, dc * 128:(dc + 1) * 128], C.ident)
            src = pv(bank, 0, 128, 0, 512, ("p (c n) -> p c n", dict(c=4)))
            mk.copy("act", x1Th[i2][:, g * 4:(g + 1) * 4, :], src)
            mk.tt("dve", x1Tl[i2][:, g * 4:(g + 1) * 4, :], src, x1Th[i2][:, g * 4:(g + 1) * 4, :], ALU.subtract)
        n = 0
        for kc in range(8):
            for (xa, wa) in [(x1Th[i2], 0), (x1Tl[i2], 0), (x1Th[i2], 1)]:
                mk.mm(pv(P[4], 0, 128, 0, 16), xa[:, kc, :], rw[:, kc, wa, :], start=(n == 0), stop=(n == 23))
                n += 1
        m = sm[i2]
        mk.op("dve", lambda e, m=m: e.tensor_reduce(m[:, 0:1].ap, P[4].h[:, 0:16], AX.X, ALU.max), reads=[pv(P[4], 0, 128, 0, 16)], writes=[m[:, 0:1]])
        mk.ts("dve", m[:, 1:2], m[:, 0:1], -1.0, None, ALU.mult)
        mk.act(m[:, 8:24], pv(P[4], 0, 128, 0, 16), AF.Exp, bias=m[:, 1:2], accum=m[:, 2:3])
        mk.recip(m[:, 3:4], m[:, 2:3])
        mk.ts("dve", m[:, 24:40], m[:, 8:24], m[:, 3:4], None, ALU.mult)
        mk.transpose(pv(P[4], 0, 16, 128, 256), m[:, 24:40], C.ident)
        mk.copy("act", affT[0:16, tk], pv(P[4], 0, 16, 128, 256))
        for half in range(2):
            hs = slice(half * 512, (half + 1) * 512)
            for kc in range(8):
                mk.mm(pv(P[5], 0, 128, 0, 512), x1Th[i2][:, kc, :], wg[:, kc, hs], start=(kc == 0), stop=(kc == 7))
            for kc in range(2):
                mk.mm(pv(P[6], 0, 128, 0, 512), pT[:, kc, tk], wp[:, kc, hs], start=(kc == 0), stop=(kc == 1))
            mk.tt("dve", gbs[half].all(), pv(P[5], 0, 128, 0, 512), lnb[:, 4, hs], ALU.add)
            mk.act(sgs[half].all(), gbs[half].all(), AF.Sigmoid)
            mk.tt("dve", gbs[half].all(), sgs[half].all(), pv(P[6], 0, 128, 0, 512), ALU.mult)
            mk.stt("dve", acc[:, hs], r[:, hs], ALPHA, gbs[half].all(), ALU.mult, ALU.add)
        mk.dma("sp", C.acc[L][s][tk, :], acc.all())
    if C.dbg and s == 0:
        dump(C, "affT%d" % L, affT.all(), [16, S], F32)
    work = mk.sb("po_work", [16, S], F32, stk)
    gat = mk.sb("po_gat", [16, 256], F32, stk)
    idxu = mk.sb("po_idxu", [16, 256], U32, stk)
    idxf = mk.sb("po_idxf", [16, 256], F32, stk)
    mk.copy("dve", work.all(), affT.all())
    for r_ in range(32):
        g8 = gat[:, r_ * 8:(r_ + 1) * 8]
        i8 = idxu[:, r_ * 8:(r_ + 1) * 8]
        mk.op("dve", lambda e, g8=g8: e.max(out=g8.ap, in_=work.all().ap), reads=[work.all()], writes=[g8])
        mk.op("dve", lambda e, g8=g8, i8=i8: e.max_index(out=i8.ap, in_max=g8.ap, in_values=work.all().ap),
              reads=[g8, work.all()], writes=[i8], strict=[g8])
        mk.op("dve", lambda e, g8=g8: e.match_replace(out=work.all().ap, in_to_replace=g8.ap, in_values=work.all().ap, imm_value=-1.0),
              reads=[g8, work.all()], writes=[work.all()], strict=[g8])
    mk.copy("dve", idxf.all(), idxu.all())
    for half in range(2):
        cs = slice(half * 128, (half + 1) * 128)
        mk.transpose(pv(P[0], 0, 128, half * 16, (half + 1) * 16), idxf[0:16, cs], C.cm[0:16, 0, 0:16])
        mk.transpose(pv(P[0], 0, 128, 32 + half * 16, 32 + (half + 1) * 16), gat[0:16, cs], C.cm[0:16, 0, 0:16])
    mk.copy("dve", idx_s.all(), pv(P[0], 0, 128, 0, 32, ("p (h e) -> p h e", dict(h=2))))
    mk.copy("dve", gate_s.all(), pv(P[0], 0, 128, 32, 64, ("p (h e) -> p h e", dict(h=2))))
    if C.dbg and s == 0:
        dump(C, "idx%d" % L, idx_s.all(), [128, 2, 16], I32)
        dump(C, "gate%d" % L, gate_s.all(), [128, 2, 16], F32)
    stk.close()
    mk.barrier()


def phase_experts(mk, C, L, seqs, idx_s, gate_s):
    stk = ExitStack()
    P = C.P
    ns = len(seqs)
    NSL = ns * 256
    wb = [[mk.sb("ex_w%d_%d" % (j, i), [128, 8, D], BF16, stk) for j in range(3)] for i in range(2)]
    xg = [mk.sb("ex_xg%d" % i, [128, D], BF16, stk) for i in range(4)]
    xgT = [mk.sb("ex_xgT%d" % i, [128, 8, NSL], BF16, stk) for i in range(2)]
    hidT = mk.sb("ex_hidT", [128, 8, NSL], BF16, stk)
    sl = [mk.sb("ex_sl%d" % i, [128, NSL], F32, stk) for i in range(2)]
    ye = [mk.sb("ex_ye%d" % i, [128, D], F32, stk) for i in range(3)]
    wsrcs = [C.w1, C.w3, C.w2]

    def load_w(e):
        for j in range(3):
            mk.dma("pool", wb[e % 2][j].all(), None, reads=[], in_ap=wsrc(wsrcs[j].h[L, e], 0, D, 0, D))

    load_w(0)
    gi = 0
    yi = 0
    for e in range(16):
        w1b, w3b, w2b = wb[e % 2]
        xt_ = xgT[e % 2]
        for si, s in enumerate(seqs):
            for half in range(2):
                g = xg[gi % 4]
                gi += 1
                idxv = idx_s[s][:, half, e:e + 1]
                src = C.xrows[L][s]
                mk.dma("pool", g.all(), src.all(), reads=[src.all(), idxv],
                       indirect=lambda eng, g=g, src=src, idxv=idxv: eng.indirect_dma_start(
                           out=g.all().ap, out_offset=None, in_=src.h[:, :],
                           in_offset=bass.IndirectOffsetOnAxis(ap=idxv.ap, axis=0)))
                for dc in range(8):
                    mk.transpose(pv(C.PH, 0, 128, dc * 128, (dc + 1) * 128), g[:, dc * 128:(dc + 1) * 128], C.identb)
                sl0 = (si * 2 + half) * 128
                mk.copy("act" if half else "dve", xt_[:, :, sl0:sl0 + 128],
                        pv(C.PH, 0, 128, 0, 1024, ("p (c n) -> p c n", dict(c=8))))
        cut(C, 21)
        if e + 1 < 16:
            load_w(e + 1)
        for fc in range(8):
            fs = slice(fc * 128, (fc + 1) * 128)
            b1, b3 = (P[0], P[1]) if fc % 2 == 0 else (P[2], P[3])
            for kc in range(8):
                mk.mm(pv(b1, 0, 128, 0, NSL), w1b[:, kc, fs], xt_[:, kc, :], start=(kc == 0), stop=(kc == 7))
            for kc in range(8):
                mk.mm(pv(b3, 0, 128, 0, NSL), w3b[:, kc, fs], xt_[:, kc, :], start=(kc == 0), stop=(kc == 7))
            mk.act(sl[fc % 2].all(), pv(b1, 0, 128, 0, NSL), AF.Silu)
            mk.tt("dve", hidT[:, fc, :], sl[fc % 2].all(), pv(b3, 0, 128, 0, NSL), ALU.mult)
        cut(C, 22)
        for si, s in enumerate(seqs):
            for half in range(2):
                sl0 = (si * 2 + half) * 128
                y = ye[yi % 3]
                yi += 1
                gv = gate_s[s][:, half, e:e + 1]
                for h2 in range(2):
                    bank = P[4 + h2]
                    for fc in range(8):
                        mk.mm(pv(bank, 0, 128, 0, 512), hidT[:, fc, sl0:sl0 + 128], w2b[:, fc, h2 * 512:(h2 + 1) * 512], start=(fc == 0), stop=(fc == 7))
                    if h2 == 0:
                        mk.act(y[:, 0:512], pv(bank, 0, 128, 0, 512), AF.Copy, scale=gv)
                    else:
                        mk.ts("dve", y[:, 512:1024], pv(bank, 0, 128, 0, 512), gv, None, ALU.mult)
                idxv = idx_s[s][:, half, e:e + 1]
                dst = C.acc[L][s]
                mk.dma("pool", dst.all(), y.all(), reads=[y.all(), idxv], writes=[dst.all()],
                       indirect=lambda eng, y=y, dst=dst, idxv=idxv: eng.indirect_dma_start(
                           out=dst.h[:, :], out_offset=bass.IndirectOffsetOnAxis(ap=idxv.ap, axis=0),
                           in_=y.all().ap, in_offset=None, compute_op=ALU.add, oob_is_err=True))
                cut(C, 23)
        cut(C, 24)
    stk.close()
    mk.barrier()


def odd_mixer(mk, C, s, xT, mixT):
    stk = ExitStack()
    P = C.P
    wi = C.w_in_odd.h
    wk = mk.sb("gq_wk", [128, 8, 4, 2, 64], BF16, stk)
    wks = mk.sb("gq_wks", [128, 8, 4, 2, 64], BF16, stk)
    wv = mk.sb("gq_wv", [128, 8, 256], BF16, stk)
    gn = mk.sb("gq_gn", [128, 4], F32, stk)
    rope = mk.sb("gq_rope", [128, 2, S], F32, stk)
    for dup in range(2):
        for kc in range(8):
            mk.dma("pool", wk[:, kc, :, dup, :], None, reads=[],
                   in_ap=wi[kc * 128:(kc + 1) * 128, 1024:1280].rearrange("p (k e) -> p k e", k=4))
            mk.dma("pool", wks[:, kc, :, dup, :], None, reads=[],
                   in_ap=C.w_k_sw.h[kc * 128:(kc + 1) * 128, :].rearrange("p (k e) -> p k e", k=4))
    mk.dma("pool", wv.all(), None, reads=[], in_ap=wsrc(wi, 0, D, 1280, 1536))
    mk.dma("sp", gn.all(), None, reads=[], in_ap=C.gqa_n.h[:, :])
    mk.dma("sp", rope.all(), None, reads=[], in_ap=C.rope_c.h[:, :, :].rearrange("k p n -> p k n"))
    KT = mk.sb("gq_KT", [128, 4, S], BF16, stk)
    VA = mk.sb("gq_VA", [128, NT, 4, 128], BF16, stk)
    QT = mk.sb("gq_QT", [128, 4, S], BF16, stk)
    mk.memset("pool", VA.all(), 1.0)

    def norm_rope(tb, lhsA, lhsB, g0, dst, tmps, it):
        t0, t1 = tb * 512, (tb + 1) * 512
        sq, sqh, rs, ta, tb_ = tmps
        A = P[0] if it % 2 == 0 else P[3]
        B = P[1] if it % 2 == 0 else P[4]
        Sb = P[2] if it % 2 == 0 else P[5]
        for kc in range(8):
            mk.mm(pv(A, 0, 128, 0, 512), lhsA(kc), xT[:, kc, t0:t1], start=(kc == 0), stop=(kc == 7))
        for kc in range(8):
            mk.mm(pv(B, 0, 128, 0, 512), lhsB(kc), xT[:, kc, t0:t1], start=(kc == 0), stop=(kc == 7))
        mk.act(sq.all(), pv(A, 0, 128, 0, 512), AF.Square)
        mk.copy("pool", sqh[:, 0, :], sq.all())
        mk.tt("pool", sqh[:, 1, :], sq.all(), sqh[:, 0, :], ALU.subtract)
        for hl in range(2):
            mk.mm(pv(Sb, 0, 128, 0, 512), C.bd_onesb, sqh[:, hl, :], start=(hl == 0), stop=(hl == 1))
        mk.act(rs.all(), pv(Sb, 0, 128, 0, 512), AF.Sqrt, bias=C.eps[:, 0:1], scale=1.0 / 64)
        mk.recip(rs.all(), rs.all())
        mk.stt("dve", ta.all(), pv(A, 0, 128, 0, 512), gn[:, g0:g0 + 1], rope[:, 0, t0:t1], ALU.mult, ALU.mult)
        mk.stt("dve", tb_.all(), pv(B, 0, 128, 0, 512), gn[:, g0 + 1:g0 + 2], rope[:, 1, t0:t1], ALU.mult, ALU.mult)
        mk.tt("pool", ta.all(), ta.all(), tb_.all(), ALU.add)
        mk.tt("dve", dst, ta.all(), rs.all(), ALU.mult)

    st2 = ExitStack()
    tmps = [(mk.sb("gq_sq%d" % i, [128, 512], F32, st2), mk.sb("gq_sqh%d" % i, [128, 2, 512], BF16, st2),
             mk.sb("gq_rs%d" % i, [128, 512], F32, st2), mk.sb("gq_ta%d" % i, [128, 512], F32, st2),
             mk.sb("gq_tb%d" % i, [128, 512], F32, st2)) for i in range(2)]
    it = 0
    for tb in range(4):
        for kv in range(4):
            norm_rope(tb, lambda kc, kv=kv: V(wk, wk.h[:, kc, kv, :, :].rearrange("p a e -> p (a e)"), 0, wk.size),
                      lambda kc, kv=kv: V(wks, wks.h[:, kc, kv, :, :].rearrange("p a e -> p (a e)"), 0, wks.size),
                      2, KT[:, kv, tb * 512:(tb + 1) * 512], tmps[it % 2], it)
            it += 1
        for j in range(4):
            tt = tb * 4 + j
            for kc in range(8):
                mk.mm(pv(P[6], 0, 128, 0, 256), xT[:, kc, tt * 128:(tt + 1) * 128], wv[:, kc, :], start=(kc == 0), stop=(kc == 7))
            mk.copy("act", VA[:, tt, :, 0:64], pv(P[6], 0, 128, 0, 256, ("p (h e) -> p h e", dict(h=4))))
    for hg in range(2):
        st3 = ExitStack()
        wq = mk.sb("gq_wq", [128, 8, 512], BF16, st3)
        wqs = mk.sb("gq_wqs", [128, 8, 512], BF16, st3)
        mk.dma("pool", wq.all(), None, reads=[], in_ap=wsrc(wi, 0, D, hg * 512, (hg + 1) * 512))
        mk.dma("pool", wqs.all(), None, reads=[], in_ap=wsrc(C.w_q_sw.h, 0, D, hg * 512, (hg + 1) * 512))
        for tb in range(4):
            for c in range(4):
                norm_rope(tb, lambda kc, c=c: wq[:, kc, c * 128:(c + 1) * 128], lambda kc, c=c: wqs[:, kc, c * 128:(c + 1) * 128],
                          0, QT[:, c, tb * 512:(tb + 1) * 512], tmps[it % 2], it)
                it += 1
        if C.dbg and s == 0:
            dump(C, "gQT%d" % hg, QT.all(), [128, 4, S], BF16)
            if hg == 0:
                dump(C, "gKT", KT.all(), [128, 4, S], BF16)
                dump(C, "gVA", VA.all(), [128, NT, 4, 128], BF16)
        attention(mk, C, QT, KT, VA, mixT, 8, 64, 64.0 ** -0.5,
                  lambda hl: ((hl % 2) * 64, hl // 2, (hg * 8 + hl) // 4, (hg * 8 + hl) // 4, (hg * 8 + hl) // 2, (hl % 2) * 64), st3)
        st3.close()
        mk.barrier()
    st2.close()
    stk.close()
    mk.barrier()


def phase_final(mk, C):
    stk = ExitStack()
    lnb = mk.sb("fin_lnb", [128, 2, D], F32, stk)
    mk.dma("sp", lnb.all(), None, reads=[], in_ap=C.lnbc.h[1, 2:4].rearrange("k p n -> p k n"))
    xts = [mk.sb("fin_x%d" % i, [128, D], F32, stk) for i in range(3)]
    tmp = mk.sb("fin_tmp", [128, D], F32, stk)
    sts = [mk.sb("fin_st%d" % i, [128, 16], F32, stk) for i in range(2)]
    i = 0
    for s in range(2):
        for tt in range(NT):
            xt = xts[i % 3]
            mk.dma("sp", xt.all(), C.acc[1][s][tt * 128:(tt + 1) * 128, :])
            layer_norm_tile(mk, C, xt, lnb[:, 0, :], lnb[:, 1, :], tmp, sts[i % 2])
            mk.dma("sp", C.out[s, tt * 128:(tt + 1) * 128, :], xt.all())
            i += 1
    stk.close()
    mk.barrier()


def build_program(nc, dbg=False, layers=(0, 1)):
    mk = MK(nc)
    C = setup(mk, dbg=dbg)
    idx_s = [mk.sb("idx_s%d" % i, [128, 2, 16], I32) for i in range(2)]
    gate_s = [mk.sb("gate_s%d" % i, [128, 2, 16], F32) for i in range(2)]
    for L in layers:
        for s in range(2):
            stk = ExitStack()
            xT = mk.sb("xT", [128, 8, S], BF16, stk)
            mixT = mk.sb("mixT", [128, 8, S], BF16, stk)
            st = ExitStack()
            if L == 1:
                lnb2 = mk.sb("pro_lnb", [128, 2, D], F32, st)
                mk.dma("sp", lnb2.all(), None, reads=[], in_ap=C.lnbc.h[0, 2:4].rearrange("k p n -> p k n"))
                phase_prologue(mk, C, L, s, xT, lnb2[:, 0, :], lnb2[:, 1, :], stk=st)
            else:
                phase_prologue(mk, C, L, s, xT, stk=st)
            st.close()
            mk.barrier()
            if L == 0:
                even_mla(mk, C, s, xT, mixT)
                even_gla(mk, C, s, xT, mixT)
            else:
                odd_mixer(mk, C, s, xT, mixT)
            phase_post(mk, C, L, s, mixT, idx_s[s], gate_s[s])
            stk.close()
            mk.barrier()
        phase_experts(mk, C, L, [0, 1], idx_s, gate_s)
    if 1 in layers:
        phase_final(mk, C)
    mk.finish()
    return mk, C


def _axial_rope_np(seq, rot_dim):
    rows = seq // 64
    row = np.repeat(np.arange(rows, dtype=np.float32), 64)
    col = np.tile(np.arange(64, dtype=np.float32), rows)
    axis_dim = rot_dim // 2
    inv = (np.float32(10000.0) ** (-np.arange(0, axis_dim, 2, dtype=np.float32) / np.float32(axis_dim))).astype(np.float32)
    ang = np.concatenate([row[:, None] * inv, col[:, None] * inv], axis=-1).astype(np.float32)
    return np.cos(ang).astype(np.float32), np.sin(ang).astype(np.float32)


def host_consts():
    f = np.float32
    idx = np.arange(128)
    same = (idx[:, None] // 64) == (idx[None, :] // 64)
    ident = np.eye(128, dtype=f)
    ones = np.ones((128, 128), f)
    bd = same.astype(f)
    s_, t_ = idx[:, None], idx[None, :]
    triF = (same & (s_ <= t_)).astype(f)
    triS = (same & (s_ > t_)).astype(f)
    triB = (same & (s_ >= t_)).astype(f)
    triSB = (same & (s_ < t_)).astype(f)
    cmat = np.stack([ident, ones, bd, triF, triS, triB, triSB]).astype(f)
    cmask = np.stack([np.tile(triF, (1, 4)), np.tile(triB, (1, 4))]).astype(f)
    ca, sa = _axial_rope_np(S, 32)
    rope_a = np.zeros((2, 128, S), f)
    rope_a[0, 64:96] = np.concatenate([ca, ca], 1).T
    rope_a[1, 64:96] = np.concatenate([-sa, sa], 1).T
    cc, sc = _axial_rope_np(S, 64)
    CC = np.concatenate([cc, cc], 1).T
    SS = np.concatenate([-sc, sc], 1).T
    rope_c = np.stack([np.concatenate([CC, CC], 0), np.concatenate([SS, SS], 0)]).astype(f)
    return dict(cmat=cmat, cmask=cmask, rope_a=rope_a, rope_c=rope_c)


def host_shared(I):
    f = np.float32
    g = lambda k: np.asarray(I[k], dtype=f)
    sh = dict(host_consts())
    we = g("w_in_even")[0]
    sh["w_in_even"] = np.ascontiguousarray(we)
    perm32 = (np.arange(32) + 16) % 32
    sh["w_kpe_sw"] = np.ascontiguousarray(we[:, 384:416][:, perm32])
    wuq = g("w_uq")[0]
    sh["w_uq"] = np.ascontiguousarray(wuq)
    sh["w_uq_sw"] = np.ascontiguousarray(wuq.reshape(256, 8, 96)[:, :, 64:][:, :, perm32].reshape(256, 256))
    sh["w_ukv"] = np.ascontiguousarray(g("w_ukv")[0])
    sh["mla_qn"] = np.ascontiguousarray(g("mla_q_norm")[0].reshape(2, 128).T)
    sh["mla_kvn"] = np.ascontiguousarray(g("mla_kv_norm")[0].reshape(128, 1))
    gw = np.zeros((2, 17, 256), f)
    gw[0, :16] = g("gla_gate_w_fwd")[0]
    gw[0, 16] = g("gla_gate_b_fwd")[0]
    gw[1, :16] = g("gla_gate_w_bwd")[0]
    gw[1, 16] = g("gla_gate_b_bwd")[0]
    sh["gla_gw"] = gw
    sh["gla_norm_bc"] = np.ascontiguousarray(np.broadcast_to(np.tile(g("gla_norm")[0], 4)[None, :], (128, 512)))
    wo_ = g("w_in_odd")[0]
    sh["w_in_odd"] = np.ascontiguousarray(wo_)
    perm64 = (np.arange(64) + 32) % 64
    sh["w_q_sw"] = np.ascontiguousarray(wo_[:, :1024].reshape(D, 16, 64)[:, :, perm64].reshape(D, 1024))
    sh["w_k_sw"] = np.ascontiguousarray(wo_[:, 1024:1280].reshape(D, 4, 64)[:, :, perm64].reshape(D, 256))
    qn, kn = g("gqa_q_norm")[0], g("gqa_k_norm")[0]
    sh["gqa_n"] = np.ascontiguousarray(np.stack([np.tile(qn, 2), np.tile(qn[perm64], 2), np.tile(kn, 2), np.tile(kn[perm64], 2)], 1))
    sh["w_o"] = g("w_o")
    lnbc = np.zeros((2, 5, 128, D), f)
    for L in range(2):
        for j, k in enumerate(["ln1_g", "ln1_b", "ln2_g", "ln2_b", "ple_gate_b"]):
            lnbc[L, j] = np.broadcast_to(g(k)[L][None, :], (128, D))
    sh["lnbc"] = lnbc
    for k in ["router_w", "w1", "w3", "w2", "ple_gate_w", "ple_w"]:
        sh[k] = g(k)
    return sh


def host_percore(I, c):
    f = np.float32
    x = np.asarray(I["x"], dtype=f)
    p = np.asarray(I["p"], dtype=f)
    return dict(x_in=np.ascontiguousarray(x[2 * c:2 * c + 2]),
                pT=np.ascontiguousarray(p[:, 2 * c:2 * c + 2].transpose(0, 1, 3, 2)))


def kernel(**inputs):
    sh = host_shared(inputs)
    in_maps = []
    for c in range(8):
        m = dict(sh)
        m.update(host_percore(inputs, c))
        in_maps.append(m)
    nc = bass.Bass("TRN2", target_bir_lowering=False)
    build_program(nc)
    res = run_bass_kernel_spmd(nc, in_maps, core_ids=list(range(8)))
    out = np.concatenate([np.asarray(r["out"]) for r in res.results], axis=0)
    return np.ascontiguousarray(out.astype(np.float32))
```

```python
from concourse.bass_utils import run_bass_kernel_spmd
import numpy as np
from contextlib import ExitStack
import concourse.bass as bass
import concourse.mybir as mybir

F32 = mybir.dt.float32
BF16 = mybir.dt.bfloat16
U32 = mybir.dt.uint32
I32 = mybir.dt.int32
ALU = mybir.AluOpType
AF = mybir.ActivationFunctionType
AX = mybir.AxisListType

EPOCH = 16000
N_DMA_SEMS = 20
SAME_ENGINE_SYNC = {"pe": False, "act": True, "dve": True, "pool": True, "sp": False}


class V:
    __slots__ = ("tile", "ap", "lo", "hi")

    def __init__(self, tile, ap, lo, hi):
        self.tile, self.ap, self.lo, self.hi = tile, ap, lo, hi


class T:
    def __init__(self, mk, name, handle, shape, kind):
        self.mk, self.name, self.h, self.shape, self.kind = mk, name, handle, list(shape), kind
        fd = self.shape[1:] if kind != "dram" else self.shape
        self.fshape = fd
        st = [1] * len(fd)
        for i in range(len(fd) - 2, -1, -1):
            st[i] = st[i + 1] * fd[i + 1]
        self.fstride = st
        self.size = int(np.prod(fd)) if fd else 1
        self.recs = {}

    def __getitem__(self, idx):
        if not isinstance(idx, tuple):
            idx = (idx,)
        idx = tuple(idx) + (slice(None),) * (len(self.shape) - len(idx))
        ap = self.h[idx]
        fidx = idx[1:] if self.kind != "dram" else idx
        if self.kind == "psum":
            return V(self, ap, 0, self.size)
        lo = 0
        hi = 0
        for i, ix in enumerate(fidx):
            n = self.fshape[i]
            if isinstance(ix, slice):
                a, b, stp = ix.indices(n)
                assert stp == 1 and b > a, (self.name, idx)
                lo += a * self.fstride[i]
                hi += (b - 1) * self.fstride[i]
            else:
                assert 0 <= ix < n, (self.name, idx)
                lo += ix * self.fstride[i]
                hi += ix * self.fstride[i]
        return V(self, ap, lo, hi + 1)

    def all(self):
        return self[tuple(slice(None) for _ in self.shape)]


class MK:
    def __init__(self, nc):
        self.nc = nc
        self.es = ExitStack()
        self.eng = {"pe": nc.tensor, "act": nc.scalar, "dve": nc.vector, "pool": nc.gpsimd, "sp": nc.sync}
        self.count = {e: 0 for e in self.eng}
        self.esems = {e: [] for e in self.eng}
        self.seen = {e: {} for e in self.eng}
        self.semh = {}
        self.dma_sems = []
        self.dma_val = []
        self.dma_rr = 0
        self.n_wait = 0
        self.n_inst = 0
        for i in range(N_DMA_SEMS):
            s = self.es.enter_context(nc.semaphore("dq%d" % i))
            key = ("dma", i)
            self.semh[key] = s
            self.dma_sems.append(key)
            self.dma_val.append(0)
        self.phase_stack = []

    def sb(self, name, shape, dtype, stack=None):
        self.uid = getattr(self, "uid", 0) + 1
        name = "sb%d_%s" % (self.uid, name)
        h = (stack or self.es).enter_context(self.nc.sbuf_tensor(name, list(shape), dtype))
        return T(self, name, h, shape, "sbuf")

    def ps(self, name, shape, dtype, stack=None):
        h = (stack or self.es).enter_context(self.nc.psum_tensor(name, list(shape), dtype))
        return T(self, name, h, shape, "psum")

    def dram(self, name, shape, dtype, kind="Internal"):
        h = self.nc.dram_tensor(name, list(shape), dtype, kind=kind)
        return T(self, name, h, shape, "dram")

    def _eng_token(self, e):
        c = self.count[e]
        ep = c // EPOCH
        while len(self.esems[e]) <= ep:
            s = self.es.enter_context(self.nc.semaphore("e_%s_%d" % (e, len(self.esems[e]))))
            key = ("eng", e, len(self.esems[e]))
            self.semh[key] = s
            self.esems[e].append(key)
        return self.esems[e][ep], (c % EPOCH) + 1

    def _wait(self, e, key, val):
        if self.seen[e].get(key, 0) >= val:
            return
        self.eng[e].wait_ge(self.semh[key], val)
        self.seen[e][key] = val
        self.n_wait += 1

    def _deps(self, e, reads, writes, strict=()):
        deps = {}
        sdeps = {}
        for v in strict:
            for (k, key, lo, hi), val in v.tile.recs.items():
                if k == "w" and lo < v.hi and v.lo < hi:
                    if sdeps.get(key, 0) < val:
                        sdeps[key] = val
        for key, val in sdeps.items():
            self._wait(e, key, val)

        def add(key, val):
            if deps.get(key, 0) < val:
                deps[key] = val

        for v in reads:
            for (k, key, lo, hi), val in v.tile.recs.items():
                if k == "w" and lo < v.hi and v.lo < hi:
                    add(key, val)
        for v in writes:
            for (k, key, lo, hi), val in v.tile.recs.items():
                if lo < v.hi and v.lo < hi:
                    add(key, val)
        for key, val in deps.items():
            if key[0] == "eng" and key[1] == e and not SAME_ENGINE_SYNC[e]:
                continue
            self._wait(e, key, val)

    def _record(self, key, val, reads, writes):
        for v in reads:
            v.tile.recs[("r", key, v.lo, v.hi)] = val
        for v in writes:
            recs = v.tile.recs
            dead = [r for r in recs if v.lo <= r[2] and r[3] <= v.hi]
            for r in dead:
                del recs[r]
            recs[("w", key, v.lo, v.hi)] = val

    def op(self, e, build, reads=(), writes=(), strict=()):
        reads = [r for r in reads if r is not None]
        writes = list(writes) + [r for r in reads if r.tile.kind == "psum"]
        reads = [r for r in reads if r.tile.kind != "psum"]
        self._deps(e, reads, writes, strict)
        key, val = self._eng_token(e)
        ins = build(self.eng[e])
        ins.then_inc(self.semh[key], 1)
        self.count[e] += 1
        self.n_inst += 1
        self._record(key, val, reads, writes)
        return ins

    def dma(self, q, out, in_, reads=None, writes=None, indirect=None, in_ap=None, out_ap=None, **kw):
        reads = [in_] if reads is None else reads
        writes = [out] if writes is None else writes
        in_ap = in_.ap if in_ap is None else in_ap
        out_ap = out.ap if out_ap is None else out_ap
        i = self.dma_rr
        self.dma_rr = (self.dma_rr + 1) % N_DMA_SEMS
        key = self.dma_sems[i]
        self._wait(q, key, self.dma_val[i])
        self._deps(q, reads, writes)
        self.dma_val[i] += 16
        val = self.dma_val[i]
        if indirect is not None:
            ins = indirect(self.eng[q])
        else:
            ins = self.eng[q].dma_start(out=out_ap, in_=in_ap, **kw)
        ins.then_inc(self.semh[key], 16)
        self.n_inst += 1
        self._record(key, val, reads, writes)
        return ins

    def barrier(self):
        toks = []
        for e in self.eng:
            c = self.count[e]
            if c == 0:
                continue
            ep = (c - 1) // EPOCH
            toks.append((self.esems[e][ep], ((c - 1) % EPOCH) + 1))
        for i, key in enumerate(self.dma_sems):
            if self.dma_val[i]:
                toks.append((key, self.dma_val[i]))
        for e in self.eng:
            for key, val in toks:
                if key[0] == "eng" and key[1] == e:
                    continue
                self._wait(e, key, val)

    def finish(self):
        self.barrier()
        self.es.close()

    def mm(self, out, lhsT, rhs, start=True, stop=True, **kw):
        return self.op("pe", lambda g: g.matmul(out.ap, lhsT.ap, rhs.ap, start=start, stop=stop, **kw),
                       reads=[lhsT, rhs], writes=[out])

    def transpose(self, out, in_, ident):
        return self.op("pe", lambda g: g.transpose(out.ap, in_.ap, ident.ap), reads=[in_, ident], writes=[out])

    def act(self, out, in_, func, bias=None, scale=None, accum=None, e="act"):
        kw = {}
        rd = [in_]
        wr = [out]
        sr = []
        if bias is not None:
            if isinstance(bias, V):
                kw["bias"] = bias.ap
                rd.append(bias)
                sr.append(bias)
            else:
                kw["bias"] = bias
        if scale is not None:
            if isinstance(scale, V):
                kw["scale"] = scale.ap
                rd.append(scale)
                sr.append(scale)
            else:
                kw["scale"] = scale
        if accum is not None:
            kw["accum_out"] = accum.ap
            wr.append(accum)
        return self.op(e, lambda g: g.activation(out.ap, in_.ap, func, **kw), reads=rd, writes=wr, strict=sr)

    def tt(self, e, out, a, b, op):
        return self.op(e, lambda g: g.tensor_tensor(out.ap, a.ap, b.ap, op), reads=[a, b], writes=[out])

    def ts(self, e, out, a, s1, s2, op0, op1=None, accum=None):
        rd = [a]
        wr = [out]
        sr = []
        s1a = s1
        s2a = s2
        if isinstance(s1, V):
            rd.append(s1)
            sr.append(s1)
            s1a = s1.ap
        if isinstance(s2, V):
            rd.append(s2)
            sr.append(s2)
            s2a = s2.ap
        kw = {}
        if op1 is not None:
            kw["op1"] = op1
        if accum is not None:
            kw["accum_out"] = accum.ap
            wr.append(accum)
        return self.op(e, lambda g: g.tensor_scalar(out.ap, a.ap, s1a, s2a, op0, **kw), reads=rd, writes=wr, strict=sr)

    def stt(self, e, out, a, s, b, op0, op1):
        rd = [a, b]
        sr = []
        sa = s
        if isinstance(s, V):
            rd.append(s)
            sr.append(s)
            sa = s.ap
        return self.op(e, lambda g: g.scalar_tensor_tensor(out.ap, a.ap, sa, b.ap, op0, op1), reads=rd, writes=[out], strict=sr)

    def copy(self, e, out, in_):
        if e == "act":
            return self.op(e, lambda g: g.copy(out.ap, in_.ap), reads=[in_], writes=[out])
        return self.op(e, lambda g: g.tensor_copy(out.ap, in_.ap), reads=[in_], writes=[out])

    def memset(self, e, out, val):
        return self.op(e, lambda g: g.memset(out.ap, val), reads=[], writes=[out])

    def recip(self, out, in_):
        return self.op("dve", lambda g: g.reciprocal(out.ap, in_.ap), reads=[in_], writes=[out])


import numpy as np
import math
from contextlib import ExitStack

S = 2048
D = 1024
NT = 16
ALPHA = (2.0 * 2) ** 0.25
EPS = 1e-6


class Rot:
    def __init__(self, items):
        self.items, self.i = list(items), 0

    def __call__(self):
        x = self.items[self.i % len(self.items)]
        self.i += 1
        return x


def pv(bank, p0, p1, c0, c1, shape=None):
    ap = bank.h[p0:p1, c0:c1]
    if shape is not None:
        ap = ap.rearrange(shape[0], **shape[1])
    return V(bank, ap, 0, bank.size)


def wsrc(h, r0, r1, c0, c1):
    return h[r0:r1, c0:c1].rearrange("(c p) n -> p c n", p=128)


class Ctx:
    pass


class Cut(Exception):
    pass


def cut(C, n):
    if getattr(C, "cut", None) == n:
        raise Cut()


def setup(mk, dbg=False):
    C = Ctx()
    C.mk = mk
    C.dbg = dbg
    C.dumps = {}
    d = lambda n, s, t=F32: mk.dram(n, s, t, kind="ExternalInput")
    C.x_in = d("x_in", [2, S, D])
    C.pT = d("pT", [2, 2, 256, S])
    C.w_in_even = d("w_in_even", [D, 1984])
    C.w_kpe_sw = d("w_kpe_sw", [D, 32])
    C.w_uq = d("w_uq", [256, 768])
    C.w_uq_sw = d("w_uq_sw", [256, 256])
    C.w_ukv = d("w_ukv", [128, 1024])
    C.mla_qn = d("mla_qn", [128, 2])
    C.mla_kvn = d("mla_kvn", [128, 1])
    C.gla_gw = d("gla_gw", [2, 17, 256])
    C.gla_norm_bc = d("gla_norm_bc", [128, 512])
    C.w_in_odd = d("w_in_odd", [D, 1536])
    C.w_q_sw = d("w_q_sw", [D, 1024])
    C.w_k_sw = d("w_k_sw", [D, 256])
    C.gqa_n = d("gqa_n", [128, 4])
    C.w_o = d("w_o", [2, D, D])
    C.lnbc = d("lnbc", [2, 5, 128, D])
    C.router_w = d("router_w", [2, D, 16])
    C.w1 = d("w1", [2, 16, D, D])
    C.w3 = d("w3", [2, 16, D, D])
    C.w2 = d("w2", [2, 16, D, D])
    C.ple_gate_w = d("ple_gate_w", [2, D, D])
    C.ple_w = d("ple_w", [2, 256, D])
    C.cmat = d("cmat", [7, 128, 128])
    C.cmask = d("cmask", [2, 128, 512])
    C.rope_a = d("rope_a", [2, 128, S])
    C.rope_c = d("rope_c", [2, 128, S])
    C.out = mk.dram("out", [2, S, D], F32, kind="ExternalOutput")
    C.acc = [[mk.dram("acc_%d_%d" % (L, s), [S, D], F32) for s in range(2)] for L in range(2)]
    C.xrows = [[mk.dram("xrows_%d_%d" % (L, s), [S, D], BF16) for s in range(2)] for L in range(2)]
    C.xln = [mk.dram("xln_%d" % s, [S, D], F32) for s in range(2)]
    C.P = [mk.ps("pb%d" % i, [128, 512], F32) for i in range(7)]
    C.PH = mk.ps("pbh", [128, 1024], BF16)
    names = ["ident", "ones", "bd_ones", "triF", "triS", "triB", "triSB"]
    C.cm = mk.sb("cm", [128, 7, 128], F32)
    mk.dma("sp", C.cm.all(), None, reads=[], in_ap=C.cmat.h[:, :, :].rearrange("k p n -> p k n"))
    for i, n in enumerate(names):
        setattr(C, n, C.cm[:, i, :])
    C.cmb = mk.sb("cmb", [128, 7, 128], BF16)
    mk.copy("dve", C.cmb.all(), C.cm.all())
    for i, n in enumerate(names):
        setattr(C, n + "b", C.cmb[:, i, :])
    C.maskt = mk.sb("maskt", [128, 2, 512], BF16)
    mk.dma("pool", C.maskt.all(), None, reads=[], in_ap=C.cmask.h[:, :, :].rearrange("k p n -> p k n"))
    C.eps = mk.sb("eps", [128, 1], F32)
    mk.memset("dve", C.eps.all(), EPS)
    return C


def dump(C, name, view, shape, dtype):
    if not C.dbg:
        return
    mk = C.mk
    t = mk.dram("dbg_" + name, shape, dtype, kind="ExternalOutput")
    mk.dma("sp", t.all(), view)
    C.dumps[name] = "dbg_" + name


def layer_norm_tile(mk, C, xt, g, b, tmp, st):
    FM = 512
    for j in range(2):
        mk.op("dve", lambda e, j=j: e.bn_stats(st[:, j * 6:(j + 1) * 6].ap, xt[:, j * FM:(j + 1) * FM].ap),
              reads=[xt[:, j * FM:(j + 1) * FM]], writes=[st[:, j * 6:(j + 1) * 6]])
    mk.op("dve", lambda e: e.bn_aggr(st[:, 12:14].ap, st.h[:, 0:12].rearrange("p (n k) -> p n k", k=6)),
          reads=[st[:, 0:12]], writes=[st[:, 12:14]])
    mk.act(st[:, 14:15], st[:, 13:14], AF.Sqrt, bias=C.eps[:, 0:1])
    mk.recip(st[:, 15:16], st[:, 14:15])
    mk.ts("dve", tmp.all(), xt.all(), st[:, 12:13], st[:, 15:16], ALU.subtract, ALU.mult)
    mk.tt("pool", tmp.all(), tmp.all(), g, ALU.mult)
    mk.tt("dve", xt.all(), tmp.all(), b, ALU.add)


def transpose_tile_to_xT(mk, C, xt, xT, tt, banks, ei):
    for g in range(2):
        bank = banks()
        for j in range(4):
            dc = g * 4 + j
            mk.transpose(pv(bank, 0, 128, j * 128, (j + 1) * 128), xt[:, dc * 128:(dc + 1) * 128], C.ident)
        src = pv(bank, 0, 128, 0, 512, ("p (c n) -> p c n", dict(c=4)))
        dst = xT[:, g * 4:(g + 1) * 4, tt * 128:(tt + 1) * 128]
        mk.copy(ei(), dst, src)


def phase_prologue(mk, C, L, s, xT, ln_g=None, ln_b=None, stk=None):
    src = C.x_in if L == 0 else C.acc[0][s]
    xts = [mk.sb("pro_x%d" % i, [128, D], F32, stk) for i in range(3)]
    tmp = mk.sb("pro_tmp", [128, D], F32, stk)
    sts = [mk.sb("pro_st%d" % i, [128, 16], F32, stk) for i in range(2)]
    banks = Rot([C.P[0], C.P[1]])
    ei = Rot(["act", "dve"])
    for tt in range(NT):
        xt = xts[tt % 3]
        if L == 0:
            mk.dma("sp", xt.all(), src[s, tt * 128:(tt + 1) * 128, :], reads=[])
        else:
            mk.dma("sp", xt.all(), src[tt * 128:(tt + 1) * 128, :])
            layer_norm_tile(mk, C, xt, ln_g, ln_b, tmp, sts[tt % 2])
            mk.dma("sp", C.xln[s][tt * 128:(tt + 1) * 128, :], xt.all())
        transpose_tile_to_xT(mk, C, xt, xT, tt, banks, ei)


def proj_fm(mk, out_ps, w, c0, c1, xT, t0, t1, nk=8):
    for kc in range(nk):
        mk.mm(out_ps, w[:, kc, c0:c1], xT[:, kc, t0:t1], start=(kc == 0), stop=(kc == nk - 1))


def rope_evac(mk, C, psA, psB, rope, p0, p1, t0, t1, dst, tmpa, tmpb):
    mk.tt("dve", tmpa[p0:p1, 0:t1 - t0], psA, rope[p0:p1, 0, t0:t1], ALU.mult)
    mk.tt("dve", tmpb[p0:p1, 0:t1 - t0], psB, rope[p0:p1, 1, t0:t1], ALU.mult)
    mk.tt("pool", dst, tmpa[p0:p1, 0:t1 - t0], tmpb[p0:p1, 0:t1 - t0], ALU.add)


def attention(mk, C, QT, KT, VA, mixT, nheads, kdim, scale, kmap, stk):
    pts = [mk.sb("att_pt%d" % i, [128, 512], BF16, stk) for i in range(3)]
    rec = [mk.sb("att_rec%d" % i, [128, 512], F32, stk) for i in range(2)]
    sbk = [C.P[0], C.P[1], C.P[2]]
    obk = [C.P[3], C.P[4]]
    steps = [(h, qb, kt) for h in range(nheads) for qb in range(4) for kt in range(NT)]

    def qk(i):
        h, qb, kt = steps[i]
        qb0, qs, ks, vs, oc, ob0 = kmap(h)
        mk.mm(pv(sbk[i % 3], 0, 128, 0, 512), KT[qb0:qb0 + kdim, ks, kt * 128:(kt + 1) * 128],
              QT[qb0:qb0 + kdim, qs, qb * 512:(qb + 1) * 512])

    qk(0)
    for i, (h, qb, kt) in enumerate(steps):
        qb0, qs, ks, vs, oc, ob0 = kmap(h)
        if i + 1 < len(steps):
            qk(i + 1)
        obank = obk[(h * 4 + qb) % 2]
        pt = pts[i % 3]
        mk.act(pt.all(), pv(sbk[i % 3], 0, 128, 0, 512), AF.Exp, scale=scale)
        mk.mm(pv(obank, 0, 128, 0, 512), VA[:, kt, vs, :], pt.all(), start=(kt == 0), stop=(kt == NT - 1))
        if kt == NT - 1:
            r = rec[(h * 4 + qb) % 2]
            mk.recip(r[ob0:ob0 + 64, :], pv(obank, 64, 128, 0, 512))
            mk.tt("dve", mixT[ob0:ob0 + 64, oc, qb * 512:(qb + 1) * 512], pv(obank, 0, 64, 0, 512),
                  r[ob0:ob0 + 64, :], ALU.mult)


def even_mla(mk, C, s, xT, mixT):
    stk = ExitStack()
    we = C.w_in_even.h
    wq = mk.sb("mla_wq", [128, 8, 256], BF16, stk)
    wkv = mk.sb("mla_wkv", [128, 8, 128], BF16, stk)
    wkpe = mk.sb("mla_wkpe", [128, 8, 2, 96], BF16, stk)
    wuq = mk.sb("mla_wuq", [128, 2, 768], BF16, stk)
    wuqs = mk.sb("mla_wuqs", [128, 2, 8, 96], BF16, stk)
    wukv = mk.sb("mla_wukv", [128, 1024], BF16, stk)
    qn = mk.sb("mla_qn", [128, 2], F32, stk)
    kvn = mk.sb("mla_kvn", [128, 1], F32, stk)
    rope = mk.sb("mla_rope", [128, 2, S], F32, stk)
    mk.dma("pool", wq.all(), None, reads=[], in_ap=wsrc(we, 0, D, 0, 256))
    mk.dma("pool", wkv.all(), None, reads=[], in_ap=wsrc(we, 0, D, 256, 384))
    mk.memset("pool", wkpe.all(), 0.0)
    mk.memset("pool", wuqs.all(), 0.0)
    mk.dma("pool", wkpe[:, :, 0, 64:96], None, reads=[], in_ap=wsrc(we, 0, D, 384, 416))
    mk.dma("pool", wkpe[:, :, 1, 64:96], None, reads=[], in_ap=wsrc(C.w_kpe_sw.h, 0, D, 0, 32))
    mk.dma("pool", wuq.all(), None, reads=[], in_ap=wsrc(C.w_uq.h, 0, 256, 0, 768))
    for kc in range(2):
        mk.dma("pool", wuqs[:, kc, :, 64:96], None, reads=[],
               in_ap=C.w_uq_sw.h[kc * 128:(kc + 1) * 128, :].rearrange("p (h e) -> p h e", h=8))
    mk.dma("pool", wukv.all(), None, reads=[], in_ap=C.w_ukv.h[:, :])
    mk.dma("sp", qn.all(), None, reads=[], in_ap=C.mla_qn.h[:, :])
    mk.dma("sp", kvn.all(), None, reads=[], in_ap=C.mla_kvn.h[:, :])
    mk.dma("sp", rope[64:96, :, :], None, reads=[], in_ap=C.rope_a.h[:, 64:96, :].rearrange("k p n -> p k n"))

    cqn = mk.sb("mla_cqn", [128, 2, S], BF16, stk)
    ckvn = mk.sb("mla_ckvn", [128, S], BF16, stk)
    kper = mk.sb("mla_kper", [128, S], BF16, stk)
    QT = mk.sb("mla_QT", [128, 4, S], BF16, stk)
    KT = mk.sb("mla_KT", [128, 4, S], BF16, stk)
    VA = mk.sb("mla_VA", [128, NT, 4, 128], BF16, stk)
    st2 = ExitStack()
    cqf = [mk.sb("mla_cqf%d" % i, [128, 512], F32, st2) for i in range(3)]
    sq = [mk.sb("mla_sq%d" % i, [128, 512], F32, st2) for i in range(3)]
    rs = [mk.sb("mla_rs%d" % i, [128, 512], F32, st2) for i in range(2)]
    sqh = [mk.sb("mla_sqh%d" % i, [128, 2, 512], BF16, st2) for i in range(3)]
    tmpa = mk.sb("mla_tmpa", [128, 512], F32, st2)
    tmpb = mk.sb("mla_tmpb", [128, 512], F32, st2)
    P = C.P
    cut(C, 1)
    for tb in range(4):
        t0, t1 = tb * 512, (tb + 1) * 512
        groups = [(wq, 0, 128, qn[:, 0:1], cqn[:, 0, t0:t1]),
                  (wq, 128, 256, qn[:, 1:2], cqn[:, 1, t0:t1]),
                  (wkv, 0, 128, kvn[:, 0:1], ckvn[:, t0:t1])]
        for gi, (w, c0, c1, gain, dst) in enumerate(groups):
            bank = P[gi]
            proj_fm(mk, pv(bank, 0, 128, 0, 512), w, c0, c1, xT, t0, t1)
            mk.act(sq[gi].all(), pv(bank, 0, 128, 0, 512), AF.Square)
            mk.copy("dve", cqf[gi].all(), pv(bank, 0, 128, 0, 512))
        cut(C, 2)
        for gi in range(3):
            mk.copy("pool", sqh[gi][:, 0, :], sq[gi].all())
            mk.tt("pool", sqh[gi][:, 1, :], sq[gi].all(), sqh[gi][:, 0, :], ALU.subtract)
        for j, (gi, hl) in enumerate([(0, 0), (0, 1), (1, 0), (1, 1)]):
            mk.mm(pv(P[3], 0, 128, 0, 512), C.onesb, sqh[gi][:, hl, :], start=(j == 0), stop=(j == 3))
        for hl in range(2):
            mk.mm(pv(P[4], 0, 128, 0, 512), C.onesb, sqh[2][:, hl, :], start=(hl == 0), stop=(hl == 1))
        cut(C, 3)
        mk.act(rs[0].all(), pv(P[3], 0, 128, 0, 512), AF.Sqrt, bias=C.eps[:, 0:1], scale=1.0 / 256)
        mk.recip(rs[0].all(), rs[0].all())
        mk.act(rs[1].all(), pv(P[4], 0, 128, 0, 512), AF.Sqrt, bias=C.eps[:, 0:1], scale=1.0 / 128)
        mk.recip(rs[1].all(), rs[1].all())
        for gi, (w, c0, c1, gain, dst) in enumerate(groups):
            mk.stt("dve", dst, cqf[gi].all(), gain, rs[0 if gi < 2 else 1].all(), ALU.mult, ALU.mult)
        cut(C, 4)
        for kc in range(8):
            mk.mm(pv(P[5], 0, 96, 0, 512), wkpe[:, kc, 0, :], xT[:, kc, t0:t1], start=(kc == 0), stop=(kc == 7))
        for kc in range(8):
            mk.mm(pv(P[6], 0, 96, 0, 512), wkpe[:, kc, 1, :], xT[:, kc, t0:t1], start=(kc == 0), stop=(kc == 7))
        cut(C, 5)
        mk.tt("dve", tmpa[64:96, :], pv(P[5], 64, 96, 0, 512), rope[64:96, 0, t0:t1], ALU.mult)
        mk.tt("dve", tmpb[64:96, :], pv(P[6], 64, 96, 0, 512), rope[64:96, 1, t0:t1], ALU.mult)
        mk.tt("pool", kper[64:96, t0:t1], tmpa[64:96, :], tmpb[64:96, :], ALU.add)
        cut(C, 6)
    stop = getattr(C, "stop", 99)
    if stop <= 1:
        dump(C, "cqn", cqn.all(), [128, 2, S], BF16)
        dump(C, "kper", kper[64:96, :], [32, S], BF16)
    for hg in range(2 if stop > 1 else 0):
        mk.memset("pool", VA.all(), 1.0)
        for tb in range(4):
            t0, t1 = tb * 512, (tb + 1) * 512
            ab = Rot([P[0], P[1]])
            bb = Rot([P[2], P[5]])
            kb = Rot([P[3], P[4]])
            for hl in range(4):
                h = hg * 4 + hl
                A = ab()
                B = bb()
                K = kb()
                for kc in range(2):
                    mk.mm(pv(A, 0, 96, 0, 512), wuq[:, kc, h * 96:(h + 1) * 96], cqn[:, kc, t0:t1], start=(kc == 0), stop=(kc == 1))
                for kc in range(2):
                    mk.mm(pv(B, 0, 96, 0, 512), wuqs[:, kc, h, :], cqn[:, kc, t0:t1], start=(kc == 0), stop=(kc == 1))
                mk.mm(pv(K, 0, 64, 0, 512), wukv[:, h * 128:h * 128 + 64], ckvn[:, t0:t1])
                mk.copy("act", QT[0:64, hl, t0:t1], pv(A, 0, 64, 0, 512))
                ta, tb_ = (tmpa, tmpb) if hl % 2 == 0 else (sq[0], sq[1])
                mk.tt("dve", ta[64:96, :], pv(A, 64, 96, 0, 512), rope[64:96, 0, t0:t1], ALU.mult)
                mk.tt("dve", tb_[64:96, :], pv(B, 64, 96, 0, 512), rope[64:96, 1, t0:t1], ALU.mult)
                mk.tt("pool", QT[64:96, hl, t0:t1], ta[64:96, :], tb_[64:96, :], ALU.add)
                mk.copy("act", KT[0:64, hl, t0:t1], pv(K, 0, 64, 0, 512))
                mk.copy("pool", KT[64:96, hl, t0:t1], kper[64:96, t0:t1])
            for j in range(4):
                tt = tb * 4 + j
                bank = P[6]
                rhs_ap = wukv.h[:, hg * 512:(hg + 1) * 512].rearrange("p (h e) -> p h e", h=4)[:, :, 64:128]
                rhs = V(wukv, rhs_ap, hg * 512, (hg + 1) * 512)
                mk.mm(pv(bank, 0, 128, 0, 256, ("p (h e) -> p h e", dict(h=4))), ckvn[:, tt * 128:(tt + 1) * 128], rhs)
                mk.copy("act" if j % 2 else "dve", VA[:, tt, :, 0:64], pv(bank, 0, 128, 0, 256, ("p (h e) -> p h e", dict(h=4))))
        if C.dbg and s == 0:
            dump(C, "QT%d" % hg, QT.all(), [128, 4, S], BF16)
            dump(C, "KT%d" % hg, KT.all(), [128, 4, S], BF16)
            dump(C, "VA%d" % hg, VA.all(), [128, NT, 4, 128], BF16)
        if stop <= 2:
            continue
        st3 = ExitStack()
        attention(mk, C, QT, KT, VA, mixT, 4, 96, 96.0 ** -0.5,
                  lambda hl: (0, hl, hl, hl, (hg * 4 + hl) // 2, (hl % 2) * 64), st3)
        st3.close()
    st2.close()
    stk.close()
    mk.barrier()


def even_gla(mk, C, s, xT, mixT):
    stk = ExitStack()
    we = C.w_in_even.h
    P = C.P
    wfm = mk.sb("gla_wfm", [128, 8, 512], BF16, stk)
    wtm = mk.sb("gla_wtm", [128, 8, 1280], BF16, stk)
    wlr = mk.sb("gla_wlr", [128, 8, 32], BF16, stk)
    gw = mk.sb("gla_gw", [17, 2, 256], BF16, stk)
    gnorm = mk.sb("gla_gnorm", [128, 512], F32, stk)
    one1 = mk.sb("gla_one1", [128, 1], F32, stk)
    mk.memset("dve", one1.all(), 1.0)
    mk.dma("pool", wfm.all(), None, reads=[], in_ap=wsrc(we, 0, D, 416, 928))
    mk.dma("pool", wtm[:, :, 0:768], None, reads=[], in_ap=wsrc(we, 0, D, 672, 1440))
    mk.dma("pool", wtm[:, :, 768:1280], None, reads=[], in_ap=wsrc(we, 0, D, 1472, 1984))
    mk.dma("pool", wlr.all(), None, reads=[], in_ap=wsrc(we, 0, D, 1440, 1472))
    mk.dma("pool", gw.all(), None, reads=[], in_ap=C.gla_gw.h[:, :, :].rearrange("k r n -> r k n"))
    mk.dma("sp", gnorm.all(), None, reads=[], in_ap=C.gla_norm_bc.h[:, :])
    gqT = mk.sb("gla_gqT", [128, 2, S], BF16, stk)
    gkT = mk.sb("gla_gkT", [128, 2, S], BF16, stk)
    gk_tok = mk.sb("gla_gk_tok", [128, NT, 256], BF16, stk)
    gv_tok = mk.sb("gla_gv_tok", [128, NT, 512], BF16, stk)
    o_f = mk.sb("gla_of", [128, NT, 512], F32, stk)
    ei = Rot(["act", "dve"])
    bk = Rot([P[0], P[1], P[2]])
    for tb in range(4):
        t0, t1 = tb * 512, (tb + 1) * 512
        for mc in range(4):
            bank = bk()
            proj_fm(mk, pv(bank, 0, 128, 0, 512), wfm, mc * 128, (mc + 1) * 128, xT, t0, t1)
            dst = (gqT if mc < 2 else gkT)[:, mc % 2, t0:t1]
            mk.copy(ei(), dst, pv(bank, 0, 128, 0, 512))
    for tt in range(NT):
        tk = slice(tt * 128, (tt + 1) * 128)
        b1, b2 = bk(), bk()
        for kc in range(8):
            mk.mm(pv(b1, 0, 128, 0, 256), xT[:, kc, tk], wtm[:, kc, 0:256], start=(kc == 0), stop=(kc == 7))
        for kc in range(8):
            mk.mm(pv(b2, 0, 128, 0, 512), xT[:, kc, tk], wtm[:, kc, 256:768], start=(kc == 0), stop=(kc == 7))
        mk.copy("act", gk_tok[:, tt, :], pv(b1, 0, 128, 0, 256))
        mk.copy("dve", gv_tok[:, tt, :], pv(b2, 0, 128, 0, 512))
    cut(C, 11)
    st2 = ExitStack()
    lrT = [mk.sb("gla_lrT%d" % i, [17, 128], BF16, st2) for i in range(2)]
    for t in lrT:
        mk.memset("dve", t.all(), 1.0)
    ez = [mk.sb("gla_ez%d" % i, [128, 256], F32, st2) for i in range(2)]
    nla = [mk.sb("gla_nla%d" % i, [128, 256], F32, st2) for i in range(2)]
    nlah = [mk.sb("gla_nlah%d" % i, [128, 2, 256], BF16, st2) for i in range(2)]
    E1 = [mk.sb("gla_E1%d" % i, [128, 2, 128], F32, st2) for i in range(3)]
    E2 = [mk.sb("gla_E2%d" % i, [128, 2, 128], F32, st2) for i in range(2)]
    E3 = [mk.sb("gla_E3%d" % i, [128, 256], F32, st2) for i in range(2)]
    ke = [mk.sb("gla_ke%d" % i, [128, 256], BF16, st2) for i in range(2)]
    qgT = [mk.sb("gla_qgT%d" % i, [128, 2, 128], BF16, st2) for i in range(2)]
    kgT = [mk.sb("gla_kgT%d" % i, [128, 2, 128], BF16, st2) for i in range(2)]
    attm = [mk.sb("gla_attm%d" % i, [128, 2, 2, 128], BF16, st2) for i in range(2)]
    Sf = mk.sb("gla_Sf", [128, 2, 128], F32, st2)
    Sb = [mk.sb("gla_Sb%d" % i, [128, 2, 128], BF16, st2) for i in range(4)]
    osum = [mk.sb("gla_osum%d" % i, [128, 512], F32, st2) for i in range(1)] * 2
    osq = mk.sb("gla_osq", [128, 512], F32, st2)
    sg = [mk.sb("gla_sg%d" % i, [128, 512], F32, st2) for i in range(1)] * 2
    og = [mk.sb("gla_og%d" % i, [128, 512], BF16, st2) for i in range(1)] * 2
    stt_ = [mk.sb("gla_st%d" % i, [128, 12], F32, st2) for i in range(2)]
    it = 0
    sbi = 0
    for d in range(2):
        tri = C.triFb if d == 0 else C.triBb
        tris = C.triSb if d == 0 else C.triSBb
        mk.memset("dve", Sf.all(), 0.0)
        mk.memset("dve", Sb[sbi % 4].all(), 0.0)
        tiles = range(NT) if d == 0 else range(NT - 1, -1, -1)
        order = [0, 1] if d == 0 else [1, 0]
        for tt in tiles:
            i2 = it % 2
            it += 1
            tk = slice(tt * 128, (tt + 1) * 128)
            for kc in range(8):
                mk.mm(pv(P[0], 0, 16, 0, 128), wlr[:, kc, d * 16:(d + 1) * 16], xT[:, kc, tk], start=(kc == 0), stop=(kc == 7))
            mk.copy("act", lrT[i2][0:16, :], pv(P[0], 0, 16, 0, 128))
            mk.mm(pv(P[0], 0, 128, 128, 384), lrT[i2][0:17, :], gw[0:17, d, :])
            mk.act(ez[i2].all(), pv(P[0], 0, 128, 128, 384), AF.Exp, scale=-1.0)
            mk.act(nla[i2].all(), ez[i2].all(), AF.Ln, bias=one1[:, 0:1])
            mk.copy("pool", nlah[i2][:, 0, :], nla[i2].all())
            mk.tt("pool", nlah[i2][:, 1, :], nla[i2].all(), nlah[i2][:, 0, :], ALU.subtract)
            cut(C, 12)
            for pc in range(2):
                for hl in range(2):
                    mk.mm(pv(P[1], 0, 128, pc * 128, (pc + 1) * 128), nlah[i2][:, hl, pc * 128:(pc + 1) * 128], tri,
                          start=(hl == 0), stop=(hl == 1))
            for hl in range(2):
                mk.mm(pv(P[1], 0, 128, 256, 512), tris, nlah[i2][:, hl, :], start=(hl == 0), stop=(hl == 1))
            e1 = E1[it % 3]
            cumv = pv(P[1], 0, 128, 0, 256, ("p (c n) -> p c n", dict(c=2)))
            mk.act(e1.all(), cumv, AF.Exp, scale=-1.0 / 16)
            mk.act(E2[i2].all(), cumv, AF.Exp, scale=1.0 / 16)
            mk.act(E3[i2].all(), pv(P[1], 0, 128, 256, 512), AF.Exp, scale=-1.0 / 16)
            mk.tt("pool", ke[i2].all(), gk_tok[:, tt, :], E3[i2].all(), ALU.mult)
            mk.stt("dve", qgT[i2].all(), gqT[:, :, tk], 0.125, e1.all(), ALU.mult, ALU.mult)
            mk.tt("dve", kgT[i2].all(), gkT[:, :, tk], E2[i2].all(), ALU.mult)
            cut(C, 13)
            for h in range(4):
                pc, b0 = h // 2, (h % 2) * 64
                bank = P[2] if h % 2 == 0 else P[5]
                mk.mm(pv(bank, 0, 128, pc * 128, (pc + 1) * 128), kgT[i2][b0:b0 + 64, pc, :], qgT[i2][b0:b0 + 64, pc, :])
            for hp in range(2):
                bank = P[2] if hp == 0 else P[5]
                mk.tt("dve", attm[i2][:, :, hp, :], pv(bank, 0, 128, 0, 256, ("p (a n) -> p a n", dict(a=2))),
                      V(C.maskt, C.maskt.h[:, d, 0:256].rearrange("p (a n) -> p a n", a=2), d * 512, d * 512 + 256), ALU.mult)
            cut(C, 14)
            for c in range(2):
                ubank = P[3] if c == 0 else P[6]
                for h in range(4):
                    pc, j = h // 2, h % 2
                    mk.mm(pv(ubank, j * 64, j * 64 + 64, pc * 128, (pc + 1) * 128), ke[i2][c * 64:(c + 1) * 64, h * 64:(h + 1) * 64],
                          gv_tok[c * 64:(c + 1) * 64, tt, h * 128:(h + 1) * 128])
            cut(C, 15)
            sb_for = {}
            for c in order:
                sb_for[c] = Sb[sbi % 4]
                dcol = c * 64 + (63 if d == 0 else 0)
                ubank = P[3] if c == 0 else P[6]
                for pc in range(2):
                    mk.stt("dve", Sf[:, pc, :], Sf[:, pc, :], e1[:, pc, dcol:dcol + 1], pv(ubank, 0, 128, pc * 128, (pc + 1) * 128), ALU.mult, ALU.add)
                sbi += 1
                mk.copy("pool", Sb[sbi % 4].all(), Sf.all())
            cut(C, 16)
            for h in range(4):
                pc, b0 = h // 2, (h % 2) * 64
                obank = P[4] if h % 2 == 0 else P[1]
                mk.mm(pv(obank, 0, 128, pc * 128, (pc + 1) * 128), attm[i2][:, pc, h % 2, :], gv_tok[:, tt, h * 128:(h + 1) * 128],
                      start=True, stop=False, skip_group_check=True)
                for ci, c in enumerate(order):
                    mk.mm(pv(obank, c * 64, c * 64 + 64, pc * 128, (pc + 1) * 128), qgT[i2][b0:b0 + 64, pc, c * 64:(c + 1) * 64],
                          sb_for[c][b0:b0 + 64, pc, :], start=False, stop=(ci == 1), skip_group_check=True)
            of4 = o_f.h[:, tt, :].rearrange("p (a b e) -> p a b e", a=2, b=2)
            if d == 0:
                for hp in range(2):
                    obank = P[4] if hp == 0 else P[1]
                    mk.copy("act", V(o_f, of4[:, :, hp, :], tt * 512, (tt + 1) * 512),
                            pv(obank, 0, 128, 0, 256, ("p (a e) -> p a e", dict(a=2))))
                cut(C, 17)
                continue
            os_ = osum[i2]
            os4 = os_.h[:, :].rearrange("p (a b e) -> p a b e", a=2, b=2)
            for hp in range(2):
                obank = P[4] if hp == 0 else P[1]
                mk.tt("dve", V(os_, os4[:, :, hp, :], 0, 512), V(o_f, of4[:, :, hp, :], tt * 512, (tt + 1) * 512),
                      pv(obank, 0, 128, 0, 256, ("p (a e) -> p a e", dict(a=2))), ALU.add)
            if C.dbg and s == 0 and getattr(C, "dbg_osum_on", False):
                if "osum" not in C.dumps:
                    C.dbg_osum = mk.dram("dbg_osum", [128, NT, 512], F32, kind="ExternalOutput")
                    C.dumps["osum"] = 1
                mk.dma("sp", C.dbg_osum[:, tt, :], os_.all())
            mk.tt("pool", osq.all(), os_.all(), os_.all(), ALU.mult)
            st = stt_[i2]
            osq3 = V(osq, osq.h[:, :].rearrange("p (h e) -> p h e", h=4), 0, 512)
            mk.op("dve", lambda e, st=st, osq3=osq3: e.tensor_reduce(st[:, 0:4].ap, osq3.ap, AX.X, ALU.add), reads=[osq3], writes=[st[:, 0:4]])
            mk.act(st[:, 4:8], st[:, 0:4], AF.Sqrt, bias=C.eps[:, 0:1], scale=1.0 / 128)
            mk.recip(st[:, 8:12], st[:, 4:8])
            for kc in range(8):
                mk.mm(pv(P[0], 0, 128, 0, 512), xT[:, kc, tk], wtm[:, kc, 768:1280], start=(kc == 0), stop=(kc == 7))
            mk.act(sg[i2].all(), pv(P[0], 0, 128, 0, 512), AF.Silu)
            for h in range(4):
                mk.ts("dve" if h % 2 else "pool", os_[:, h * 128:(h + 1) * 128], os_[:, h * 128:(h + 1) * 128], st[:, 8 + h:9 + h], None, ALU.mult)
            mk.tt("pool", os_.all(), os_.all(), gnorm.all(), ALU.mult)
            mk.tt("dve", og[i2].all(), os_.all(), sg[i2].all(), ALU.mult)
            for h in range(4):
                mk.transpose(pv(C.PH, 0, 128, h * 128, (h + 1) * 128), og[i2][:, h * 128:(h + 1) * 128], C.identb)
            mk.copy("act", mixT[:, 4:8, tk], pv(C.PH, 0, 128, 0, 512, ("p (c n) -> p c n", dict(c=4))))
    if C.dbg and s == 0:
        dump(C, "o_f", o_f.all(), [128, NT, 512], F32)
    st2.close()
    stk.close()
    mk.barrier()


def phase_post(mk, C, L, s, mixT, idx_s, gate_s):
    stk = ExitStack()
    P = C.P
    wo = mk.sb("po_wo", [128, 8, D], BF16, stk)
    wg = mk.sb("po_wg", [128, 8, D], BF16, stk)
    wp = mk.sb("po_wp", [128, 2, D], BF16, stk)
    pT = mk.sb("po_pT", [128, 2, S], BF16, stk)
    rwf = mk.sb("po_rwf", [128, 8, 16], F32, stk)
    rw = mk.sb("po_rw", [128, 8, 2, 16], BF16, stk)
    lnb = mk.sb("po_lnb", [128, 5, D], F32, stk)
    affT = mk.sb("po_affT", [16, S], F32, stk)
    mk.dma("pool", wo.all(), None, reads=[], in_ap=wsrc(C.w_o.h[L], 0, D, 0, D))
    mk.dma("pool", wg.all(), None, reads=[], in_ap=wsrc(C.ple_gate_w.h[L], 0, D, 0, D))
    mk.dma("pool", wp.all(), None, reads=[], in_ap=wsrc(C.ple_w.h[L], 0, 256, 0, D))
    mk.dma("pool", pT.all(), None, reads=[], in_ap=C.pT.h[L, s].rearrange("(c p) n -> p c n", p=128))
    mk.dma("sp", rwf.all(), None, reads=[], in_ap=wsrc(C.router_w.h[L], 0, D, 0, 16))
    mk.dma("sp", lnb.all(), None, reads=[], in_ap=C.lnbc.h[L].rearrange("k p n -> p k n"))
    mk.copy("dve", rw[:, :, 0, :], rwf.all())
    mk.tt("dve", rw[:, :, 1, :], rwf.all(), rw[:, :, 0, :], ALU.subtract)
    stt_ = ExitStack()
    xts = [mk.sb("po_xt%d" % i, [128, D], F32, stt_) for i in range(2)]
    rts = [mk.sb("po_r%d" % i, [128, D], F32, stt_) for i in range(3)]
    ras = [mk.sb("po_ra%d" % i, [128, D], F32, stt_) for i in range(2)]
    x1bs = [mk.sb("po_x1b%d" % i, [128, D], BF16, stt_) for i in range(2)]
    x1Th = [mk.sb("po_x1Th%d" % i, [128, 8, 128], BF16, stt_) for i in range(3)]
    x1Tl = [mk.sb("po_x1Tl%d" % i, [128, 8, 128], BF16, stt_) for i in range(2)]
    accs = [mk.sb("po_acc%d" % i, [128, D], F32, stt_) for i in range(2)]
    tmp = mk.sb("po_tmp", [128, D], F32, stt_)
    gbs = [mk.sb("po_gb%d" % i, [128, 512], F32, stt_) for i in range(2)]
    sgs = [mk.sb("po_sg%d" % i, [128, 512], F32, stt_) for i in range(2)]
    sts = [mk.sb("po_st%d" % i, [128, 20], F32, stt_) for i in range(2)]
    sm = [mk.sb("po_sm%d" % i, [128, 40], F32, stt_) for i in range(2)]
    xsrc = C.x_in if L == 0 else C.xln[s]

    def stage_a(tt):
        tk = slice(tt * 128, (tt + 1) * 128)
        xt, r, st = xts[tt % 2], rts[tt % 3], sts[tt % 2]
        if L == 0:
            mk.dma("sp", xt.all(), xsrc[s, tk, :], reads=[])
        else:
            mk.dma("sp", xt.all(), xsrc[tk, :])
        for half in range(2):
            for kc in range(8):
                mk.mm(pv(P[half], 0, 128, 0, 512), mixT[:, kc, tk], wo[:, kc, half * 512:(half + 1) * 512], start=(kc == 0), stop=(kc == 7))
        for half in range(2):
            hs = slice(half * 512, (half + 1) * 512)
            mk.stt("dve", r[:, hs], xt[:, hs], ALPHA, pv(P[half], 0, 128, 0, 512), ALU.mult, ALU.add)
        for j in range(2):
            mk.op("dve", lambda e, j=j: e.bn_stats(st[:, j * 6:(j + 1) * 6].ap, r[:, j * 512:(j + 1) * 512].ap),
                  reads=[r[:, j * 512:(j + 1) * 512]], writes=[st[:, j * 6:(j + 1) * 6]])
        mk.op("dve", lambda e: e.bn_aggr(st[:, 12:14].ap, st[:, 0:12].ap), reads=[st[:, 0:12]], writes=[st[:, 12:14]])
        mk.act(st[:, 14:15], st[:, 13:14], AF.Sqrt, bias=C.eps[:, 0:1])
        mk.recip(st[:, 15:16], st[:, 14:15])
        mk.stt("dve", st[:, 16:17], st[:, 12:13], -1.0, st[:, 15:16], ALU.mult, ALU.mult)
        mk.act(tmp.all(), r.all(), AF.Identity, bias=st[:, 16:17], scale=st[:, 15:16])
        mk.tt("pool", tmp.all(), tmp.all(), lnb[:, 0, :], ALU.mult)
        mk.tt("pool", r.all(), tmp.all(), lnb[:, 1, :], ALU.add)

    def stage_b(tt):
        tk = slice(tt * 128, (tt + 1) * 128)
        r, x1b, xh, xl, m = rts[tt % 3], x1bs[tt % 2], x1Th[tt % 3], x1Tl[tt % 2], sm[tt % 2]
        mk.copy("act", x1b.all(), r.all())
        mk.dma("sp", C.xrows[L][s][tk, :], x1b.all())
        mk.act(ras[tt % 2].all(), r.all(), AF.Copy, scale=ALPHA)
        for g in range(2):
            bank = P[2 + g]
            for j in range(4):
                dc = g * 4 + j
                mk.transpose(pv(bank, 0, 128, j * 128, (j + 1) * 128), r[:, dc * 128:(dc + 1) * 128], C.ident)
            src = pv(bank, 0, 128, 0, 512, ("p (c n) -> p c n", dict(c=4)))
            mk.copy("act", xh[:, g * 4:(g + 1) * 4, :], src)
            mk.tt("dve", xl[:, g * 4:(g + 1) * 4, :], src, xh[:, g * 4:(g + 1) * 4, :], ALU.subtract)
        n = 0
        for kc in range(8):
            for (xa, wa) in [(xh, 0), (xl, 0), (xh, 1)]:
                mk.mm(pv(P[4], 0, 128, 0, 16), xa[:, kc, :], rw[:, kc, wa, :], start=(n == 0), stop=(n == 23))
                n += 1
        mk.op("dve", lambda e: e.tensor_reduce(m[:, 0:1].ap, P[4].h[:, 0:16], AX.X, ALU.max), reads=[pv(P[4], 0, 128, 0, 16)], writes=[m[:, 0:1]])
        mk.ts("dve", m[:, 1:2], m[:, 0:1], -1.0, None, ALU.mult)
        mk.act(m[:, 8:24], pv(P[4], 0, 128, 0, 16), AF.Exp, bias=m[:, 1:2], accum=m[:, 2:3])
        mk.recip(m[:, 3:4], m[:, 2:3])
        mk.ts("dve", m[:, 24:40], m[:, 8:24], m[:, 3:4], None, ALU.mult)
        mk.transpose(pv(P[4], 0, 16, 128, 256), m[:, 24:40], C.ident)
        mk.copy("act", affT[0:16, tk], pv(P[4], 0, 16, 128, 256))

    def stage_c(tt):
        tk = slice(tt * 128, (tt + 1) * 128)
        xh, acc, ra = x1Th[tt % 3], accs[tt % 2], ras[tt % 2]
        for half in range(2):
            hs = slice(half * 512, (half + 1) * 512)
            for kc in range(8):
                mk.mm(pv(P[5], 0, 128, 0, 512), xh[:, kc, :], wg[:, kc, hs], start=(kc == 0), stop=(kc == 7))
            for kc in range(2):
                mk.mm(pv(P[6], 0, 128, 0, 512), pT[:, kc, tk], wp[:, kc, hs], start=(kc == 0), stop=(kc == 1))
            mk.tt("dve", gbs[half].all(), pv(P[5], 0, 128, 0, 512), lnb[:, 4, hs], ALU.add)
            mk.act(sgs[half].all(), gbs[half].all(), AF.Sigmoid)
            mk.tt("dve", gbs[half].all(), sgs[half].all(), pv(P[6], 0, 128, 0, 512), ALU.mult)
            mk.tt("pool", acc[:, hs], ra[:, hs], gbs[half].all(), ALU.add)
        mk.dma("sp", C.acc[L][s][tk, :], acc.all())

    SK1, SK2 = getattr(C, "skew", (1, 2))
    for i in range(NT + SK2):
        if i < NT:
            stage_a(i)
        if 0 <= i - SK1 < NT:
            stage_b(i - SK1)
        if 0 <= i - SK2 < NT:
            stage_c(i - SK2)
    stt_.close()
    mk.barrier()
    if C.dbg and s == 0:
        dump(C, "affT%d" % L, affT.all(), [16, S], F32)
    work = mk.sb("po_work", [16, S], F32, stk)
    gat = mk.sb("po_gat", [16, 256], F32, stk)
    idxu = mk.sb("po_idxu", [16, 256], U32, stk)
    idxf = mk.sb("po_idxf", [16, 256], F32, stk)
    mk.copy("dve", work.all(), affT.all())
    for r_ in range(32):
        g8 = gat[:, r_ * 8:(r_ + 1) * 8]
        i8 = idxu[:, r_ * 8:(r_ + 1) * 8]
        mk.op("dve", lambda e, g8=g8: e.max(out=g8.ap, in_=work.all().ap), reads=[work.all()], writes=[g8])
        mk.op("dve", lambda e, g8=g8, i8=i8: e.max_index(out=i8.ap, in_max=g8.ap, in_values=work.all().ap),
              reads=[g8, work.all()], writes=[i8], strict=[g8])
        mk.op("dve", lambda e, g8=g8: e.match_replace(out=work.all().ap, in_to_replace=g8.ap, in_values=work.all().ap, imm_value=-1.0),
              reads=[g8, work.all()], writes=[work.all()], strict=[g8])
    mk.copy("dve", idxf.all(), idxu.all())
    for half in range(2):
        cs = slice(half * 128, (half + 1) * 128)
        mk.transpose(pv(P[0], 0, 128, half * 16, (half + 1) * 16), idxf[0:16, cs], C.cm[0:16, 0, 0:16])
        mk.transpose(pv(P[0], 0, 128, 32 + half * 16, 32 + (half + 1) * 16), gat[0:16, cs], C.cm[0:16, 0, 0:16])
    mk.copy("dve", idx_s.all(), pv(P[0], 0, 128, 0, 32, ("p (h e) -> p h e", dict(h=2))))
    mk.copy("dve", gate_s.all(), pv(P[0], 0, 128, 32, 64, ("p (h e) -> p h e", dict(h=2))))
    if C.dbg and s == 0:
        dump(C, "idx%d" % L, idx_s.all(), [128, 2, 16], I32)
        dump(C, "gate%d" % L, gate_s.all(), [128, 2, 16], F32)
    stk.close()
    mk.barrier()


def phase_experts(mk, C, L, seqs, idx_s, gate_s):
    stk = ExitStack()
    P = C.P
    ns = len(seqs)
    NSL = ns * 256
    NWB = 3
    wb = [[mk.sb("ex_w%d_%d" % (j, i), [128, 8, D], BF16, stk) for j in range(3)] for i in range(NWB)]
    xg = [mk.sb("ex_xg%d" % i, [128, D], BF16, stk) for i in range(4)]
    xgT = [mk.sb("ex_xgT%d" % i, [128, 8, NSL], BF16, stk) for i in range(1)] * 2
    hidT = mk.sb("ex_hidT", [128, 8, NSL], BF16, stk)
    sl = [mk.sb("ex_sl%d" % i, [128, NSL], F32, stk) for i in range(2)]
    ye = [mk.sb("ex_ye%d" % i, [128, D], F32, stk) for i in range(2)]
    wsrcs = [C.w1, C.w3, C.w2]

    def load_w(e):
        for j in range(3):
            mk.dma("pool", wb[e % NWB][j].all(), None, reads=[], in_ap=wsrc(wsrcs[j].h[L, e], 0, D, 0, D))

    load_w(0)
    load_w(1)
    gi = 0
    yi = 0
    for e in range(16):
        w1b, w3b, w2b = wb[e % NWB]
        xt_ = xgT[e % 2]
        for si, s in enumerate(seqs):
            for half in range(2):
                g = xg[gi % 4]
                gi += 1
                idxv = idx_s[s][:, half, e:e + 1]
                src = C.xrows[L][s]
                mk.dma("pool", g.all(), src.all(), reads=[src.all(), idxv],
                       indirect=lambda eng, g=g, src=src, idxv=idxv: eng.indirect_dma_start(
                           out=g.all().ap, out_offset=None, in_=src.h[:, :],
                           in_offset=bass.IndirectOffsetOnAxis(ap=idxv.ap, axis=0)))
                for dc in range(8):
                    mk.transpose(pv(C.PH, 0, 128, dc * 128, (dc + 1) * 128), g[:, dc * 128:(dc + 1) * 128], C.identb)
                sl0 = (si * 2 + half) * 128
                mk.copy("act" if half else "dve", xt_[:, :, sl0:sl0 + 128],
                        pv(C.PH, 0, 128, 0, 1024, ("p (c n) -> p c n", dict(c=8))))
        cut(C, 21)
        if e + 2 < 16:
            load_w(e + 2)
        for fc in range(8):
            fs = slice(fc * 128, (fc + 1) * 128)
            b1, b3 = (P[0], P[1]) if fc % 2 == 0 else (P[2], P[3])
            for kc in range(8):
                mk.mm(pv(b1, 0, 128, 0, NSL), w1b[:, kc, fs], xt_[:, kc, :], start=(kc == 0), stop=(kc == 7))
            for kc in range(8):
                mk.mm(pv(b3, 0, 128, 0, NSL), w3b[:, kc, fs], xt_[:, kc, :], start=(kc == 0), stop=(kc == 7))
            mk.act(sl[fc % 2].all(), pv(b1, 0, 128, 0, NSL), AF.Silu)
            mk.tt("dve", hidT[:, fc, :], sl[fc % 2].all(), pv(b3, 0, 128, 0, NSL), ALU.mult)
        cut(C, 22)
        for si, s in enumerate(seqs):
            for half in range(2):
                sl0 = (si * 2 + half) * 128
                y = ye[yi % 2]
                yi += 1
                gv = gate_s[s][:, half, e:e + 1]
                for h2 in range(2):
                    bank = P[4 + h2]
                    for fc in range(8):
                        mk.mm(pv(bank, 0, 128, 0, 512), hidT[:, fc, sl0:sl0 + 128], w2b[:, fc, h2 * 512:(h2 + 1) * 512], start=(fc == 0), stop=(fc == 7))
                    if h2 == 0:
                        mk.act(y[:, 0:512], pv(bank, 0, 128, 0, 512), AF.Copy, scale=gv)
                    else:
                        mk.ts("dve", y[:, 512:1024], pv(bank, 0, 128, 0, 512), gv, None, ALU.mult)
                idxv = idx_s[s][:, half, e:e + 1]
                dst = C.acc[L][s]
                mk.dma("pool", dst.all(), y.all(), reads=[y.all(), idxv], writes=[dst.all()],
                       indirect=lambda eng, y=y, dst=dst, idxv=idxv: eng.indirect_dma_start(
                           out=dst.h[:, :], out_offset=bass.IndirectOffsetOnAxis(ap=idxv.ap, axis=0),
                           in_=y.all().ap, in_offset=None, compute_op=ALU.add, oob_is_err=True))
                cut(C, 23)
        cut(C, 24)
    stk.close()
    mk.barrier()


def odd_mixer(mk, C, s, xT, mixT):
    stk = ExitStack()
    P = C.P
    wi = C.w_in_odd.h
    wk = mk.sb("gq_wk", [128, 8, 4, 2, 64], BF16, stk)
    wks = mk.sb("gq_wks", [128, 8, 4, 2, 64], BF16, stk)
    wv = mk.sb("gq_wv", [128, 8, 256], BF16, stk)
    gn = mk.sb("gq_gn", [128, 4], F32, stk)
    rope = mk.sb("gq_rope", [128, 2, S], F32, stk)
    for dup in range(2):
        for kc in range(8):
            mk.dma("pool", wk[:, kc, :, dup, :], None, reads=[],
                   in_ap=wi[kc * 128:(kc + 1) * 128, 1024:1280].rearrange("p (k e) -> p k e", k=4))
            mk.dma("pool", wks[:, kc, :, dup, :], None, reads=[],
                   in_ap=C.w_k_sw.h[kc * 128:(kc + 1) * 128, :].rearrange("p (k e) -> p k e", k=4))
    mk.dma("pool", wv.all(), None, reads=[], in_ap=wsrc(wi, 0, D, 1280, 1536))
    mk.dma("sp", gn.all(), None, reads=[], in_ap=C.gqa_n.h[:, :])
    mk.dma("sp", rope.all(), None, reads=[], in_ap=C.rope_c.h[:, :, :].rearrange("k p n -> p k n"))
    KT = mk.sb("gq_KT", [128, 4, S], BF16, stk)
    VA = mk.sb("gq_VA", [128, NT, 4, 128], BF16, stk)
    QT = mk.sb("gq_QT", [128, 4, S], BF16, stk)
    mk.memset("pool", VA.all(), 1.0)

    def norm_rope(tb, lhsA, lhsB, g0, dst, tmps, it):
        t0, t1 = tb * 512, (tb + 1) * 512
        sq, sqh, rs, ta, tb_ = tmps
        A = P[0] if it % 2 == 0 else P[3]
        B = P[1] if it % 2 == 0 else P[4]
        Sb = P[2] if it % 2 == 0 else P[5]
        for kc in range(8):
            mk.mm(pv(A, 0, 128, 0, 512), lhsA(kc), xT[:, kc, t0:t1], start=(kc == 0), stop=(kc == 7))
        for kc in range(8):
            mk.mm(pv(B, 0, 128, 0, 512), lhsB(kc), xT[:, kc, t0:t1], start=(kc == 0), stop=(kc == 7))
        mk.act(sq.all(), pv(A, 0, 128, 0, 512), AF.Square)
        mk.copy("pool", sqh[:, 0, :], sq.all())
        mk.tt("pool", sqh[:, 1, :], sq.all(), sqh[:, 0, :], ALU.subtract)
        for hl in range(2):
            mk.mm(pv(Sb, 0, 128, 0, 512), C.bd_onesb, sqh[:, hl, :], start=(hl == 0), stop=(hl == 1))
        mk.act(rs.all(), pv(Sb, 0, 128, 0, 512), AF.Sqrt, bias=C.eps[:, 0:1], scale=1.0 / 64)
        mk.recip(rs.all(), rs.all())
        mk.stt("dve", ta.all(), pv(A, 0, 128, 0, 512), gn[:, g0:g0 + 1], rope[:, 0, t0:t1], ALU.mult, ALU.mult)
        mk.stt("dve", tb_.all(), pv(B, 0, 128, 0, 512), gn[:, g0 + 1:g0 + 2], rope[:, 1, t0:t1], ALU.mult, ALU.mult)
        mk.tt("pool", ta.all(), ta.all(), tb_.all(), ALU.add)
        mk.tt("dve", dst, ta.all(), rs.all(), ALU.mult)

    st2 = ExitStack()
    tmps = [(mk.sb("gq_sq%d" % i, [128, 512], F32, st2), mk.sb("gq_sqh%d" % i, [128, 2, 512], BF16, st2),
             mk.sb("gq_rs%d" % i, [128, 512], F32, st2), mk.sb("gq_ta%d" % i, [128, 512], F32, st2),
             mk.sb("gq_tb%d" % i, [128, 512], F32, st2)) for i in range(2)]
    it = 0
    for tb in range(4):
        for kv in range(4):
            norm_rope(tb, lambda kc, kv=kv: V(wk, wk.h[:, kc, kv, :, :].rearrange("p a e -> p (a e)"), 0, wk.size),
                      lambda kc, kv=kv: V(wks, wks.h[:, kc, kv, :, :].rearrange("p a e -> p (a e)"), 0, wks.size),
                      2, KT[:, kv, tb * 512:(tb + 1) * 512], tmps[it % 2], it)
            it += 1
        for j in range(4):
            tt = tb * 4 + j
            for kc in range(8):
                mk.mm(pv(P[6], 0, 128, 0, 256), xT[:, kc, tt * 128:(tt + 1) * 128], wv[:, kc, :], start=(kc == 0), stop=(kc == 7))
            mk.copy("act", VA[:, tt, :, 0:64], pv(P[6], 0, 128, 0, 256, ("p (h e) -> p h e", dict(h=4))))
    for hg in range(2):
        st3 = ExitStack()
        wq = mk.sb("gq_wq", [128, 8, 512], BF16, st3)
        wqs = mk.sb("gq_wqs", [128, 8, 512], BF16, st3)
        mk.dma("pool", wq.all(), None, reads=[], in_ap=wsrc(wi, 0, D, hg * 512, (hg + 1) * 512))
        mk.dma("pool", wqs.all(), None, reads=[], in_ap=wsrc(C.w_q_sw.h, 0, D, hg * 512, (hg + 1) * 512))
        for tb in range(4):
            for c in range(4):
                norm_rope(tb, lambda kc, c=c: wq[:, kc, c * 128:(c + 1) * 128], lambda kc, c=c: wqs[:, kc, c * 128:(c + 1) * 128],
                          0, QT[:, c, tb * 512:(tb + 1) * 512], tmps[it % 2], it)
                it += 1
        if C.dbg and s == 0:
            dump(C, "gQT%d" % hg, QT.all(), [128, 4, S], BF16)
            if hg == 0:
                dump(C, "gKT", KT.all(), [128, 4, S], BF16)
                dump(C, "gVA", VA.all(), [128, NT, 4, 128], BF16)
        attention(mk, C, QT, KT, VA, mixT, 8, 64, 64.0 ** -0.5,
                  lambda hl: ((hl % 2) * 64, hl // 2, (hg * 8 + hl) // 4, (hg * 8 + hl) // 4, (hg * 8 + hl) // 2, (hl % 2) * 64), st3)
        st3.close()
        mk.barrier()
    st2.close()
    stk.close()
    mk.barrier()


def phase_final(mk, C):
    stk = ExitStack()
    lnb = mk.sb("fin_lnb", [128, 2, D], F32, stk)
    mk.dma("sp", lnb.all(), None, reads=[], in_ap=C.lnbc.h[1, 2:4].rearrange("k p n -> p k n"))
    xts = [mk.sb("fin_x%d" % i, [128, D], F32, stk) for i in range(3)]
    tmp = mk.sb("fin_tmp", [128, D], F32, stk)
    sts = [mk.sb("fin_st%d" % i, [128, 16], F32, stk) for i in range(2)]
    i = 0
    for s in range(2):
        for tt in range(NT):
            xt = xts[i % 3]
            mk.dma("sp", xt.all(), C.acc[1][s][tt * 128:(tt + 1) * 128, :])
            layer_norm_tile(mk, C, xt, lnb[:, 0, :], lnb[:, 1, :], tmp, sts[i % 2])
            mk.dma("sp", C.out[s, tt * 128:(tt + 1) * 128, :], xt.all())
            i += 1
    stk.close()
    mk.barrier()


def build_program(nc, dbg=False, layers=(0, 1)):
    mk = MK(nc)
    C = setup(mk, dbg=dbg)
    idx_s = [mk.sb("idx_s%d" % i, [128, 2, 16], I32) for i in range(2)]
    gate_s = [mk.sb("gate_s%d" % i, [128, 2, 16], F32) for i in range(2)]
    for L in layers:
        for s in range(2):
            stk = ExitStack()
            xT = mk.sb("xT", [128, 8, S], BF16, stk)
            mixT = mk.sb("mixT", [128, 8, S], BF16, stk)
            st = ExitStack()
            if L == 1:
                lnb2 = mk.sb("pro_lnb", [128, 2, D], F32, st)
                mk.dma("sp", lnb2.all(), None, reads=[], in_ap=C.lnbc.h[0, 2:4].rearrange("k p n -> p k n"))
                phase_prologue(mk, C, L, s, xT, lnb2[:, 0, :], lnb2[:, 1, :], stk=st)
            else:
                phase_prologue(mk, C, L, s, xT, stk=st)
            st.close()
            mk.barrier()
            if L == 0:
                even_mla(mk, C, s, xT, mixT)
                even_gla(mk, C, s, xT, mixT)
            else:
                odd_mixer(mk, C, s, xT, mixT)
            phase_post(mk, C, L, s, mixT, idx_s[s], gate_s[s])
            stk.close()
            mk.barrier()
        phase_experts(mk, C, L, [0, 1], idx_s, gate_s)
    if 1 in layers:
        phase_final(mk, C)
    mk.finish()
    return mk, C


def _axial_rope_np(seq, rot_dim):
    rows = seq // 64
    row = np.repeat(np.arange(rows, dtype=np.float32), 64)
    col = np.tile(np.arange(64, dtype=np.float32), rows)
    axis_dim = rot_dim // 2
    inv = (np.float32(10000.0) ** (-np.arange(0, axis_dim, 2, dtype=np.float32) / np.float32(axis_dim))).astype(np.float32)
    ang = np.concatenate([row[:, None] * inv, col[:, None] * inv], axis=-1).astype(np.float32)
    return np.cos(ang).astype(np.float32), np.sin(ang).astype(np.float32)


def host_consts():
    f = np.float32
    idx = np.arange(128)
    same = (idx[:, None] // 64) == (idx[None, :] // 64)
    ident = np.eye(128, dtype=f)
    ones = np.ones((128, 128), f)
    bd = same.astype(f)
    s_, t_ = idx[:, None], idx[None, :]
    triF = (same & (s_ <= t_)).astype(f)
    triS = (same & (s_ > t_)).astype(f)
    triB = (same & (s_ >= t_)).astype(f)
    triSB = (same & (s_ < t_)).astype(f)
    cmat = np.stack([ident, ones, bd, triF, triS, triB, triSB]).astype(f)
    cmask = np.stack([np.tile(triF, (1, 4)), np.tile(triB, (1, 4))]).astype(f)
    ca, sa = _axial_rope_np(S, 32)
    rope_a = np.zeros((2, 128, S), f)
    rope_a[0, 64:96] = np.concatenate([ca, ca], 1).T
    rope_a[1, 64:96] = np.concatenate([-sa, sa], 1).T
    cc, sc = _axial_rope_np(S, 64)
    CC = np.concatenate([cc, cc], 1).T
    SS = np.concatenate([-sc, sc], 1).T
    rope_c = np.stack([np.concatenate([CC, CC], 0), np.concatenate([SS, SS], 0)]).astype(f)
    return dict(cmat=cmat, cmask=cmask, rope_a=rope_a, rope_c=rope_c)


def host_shared(I):
    f = np.float32
    g = lambda k: np.asarray(I[k], dtype=f)
    sh = dict(host_consts())
    we = g("w_in_even")[0]
    sh["w_in_even"] = np.ascontiguousarray(we)
    perm32 = (np.arange(32) + 16) % 32
    sh["w_kpe_sw"] = np.ascontiguousarray(we[:, 384:416][:, perm32])
    wuq = g("w_uq")[0]
    sh["w_uq"] = np.ascontiguousarray(wuq)
    sh["w_uq_sw"] = np.ascontiguousarray(wuq.reshape(256, 8, 96)[:, :, 64:][:, :, perm32].reshape(256, 256))
    sh["w_ukv"] = np.ascontiguousarray(g("w_ukv")[0])
    sh["mla_qn"] = np.ascontiguousarray(g("mla_q_norm")[0].reshape(2, 128).T)
    sh["mla_kvn"] = np.ascontiguousarray(g("mla_kv_norm")[0].reshape(128, 1))
    gw = np.zeros((2, 17, 256), f)
    gw[0, :16] = g("gla_gate_w_fwd")[0]
    gw[0, 16] = g("gla_gate_b_fwd")[0]
    gw[1, :16] = g("gla_gate_w_bwd")[0]
    gw[1, 16] = g("gla_gate_b_bwd")[0]
    sh["gla_gw"] = gw
    sh["gla_norm_bc"] = np.ascontiguousarray(np.broadcast_to(np.tile(g("gla_norm")[0], 4)[None, :], (128, 512)))
    wo_ = g("w_in_odd")[0]
    sh["w_in_odd"] = np.ascontiguousarray(wo_)
    perm64 = (np.arange(64) + 32) % 64
    sh["w_q_sw"] = np.ascontiguousarray(wo_[:, :1024].reshape(D, 16, 64)[:, :, perm64].reshape(D, 1024))
    sh["w_k_sw"] = np.ascontiguousarray(wo_[:, 1024:1280].reshape(D, 4, 64)[:, :, perm64].reshape(D, 256))
    qn, kn = g("gqa_q_norm")[0], g("gqa_k_norm")[0]
    sh["gqa_n"] = np.ascontiguousarray(np.stack([np.tile(qn, 2), np.tile(qn[perm64], 2), np.tile(kn, 2), np.tile(kn[perm64], 2)], 1))
    sh["w_o"] = g("w_o")
    lnbc = np.zeros((2, 5, 128, D), f)
    for L in range(2):
        for j, k in enumerate(["ln1_g", "ln1_b", "ln2_g", "ln2_b", "ple_gate_b"]):
            lnbc[L, j] = np.broadcast_to(g(k)[L][None, :], (128, D))
    sh["lnbc"] = lnbc
    for k in ["router_w", "w1", "w3", "w2", "ple_gate_w", "ple_w"]:
        sh[k] = g(k)
    return sh


def host_percore(I, c):
    f = np.float32
    x = np.asarray(I["x"], dtype=f)
    p = np.asarray(I["p"], dtype=f)
    return dict(x_in=np.ascontiguousarray(x[2 * c:2 * c + 2]),
                pT=np.ascontiguousarray(p[:, 2 * c:2 * c + 2].transpose(0, 1, 3, 2)))


def kernel(**inputs):
    sh = host_shared(inputs)
    in_maps = []
    for c in range(8):
        m = dict(sh)
        m.update(host_percore(inputs, c))
        in_maps.append(m)
    nc = bass.Bass("TRN2", target_bir_lowering=False)
    build_program(nc)
    res = run_bass_kernel_spmd(nc, in_maps, core_ids=list(range(8)))
    out = np.concatenate([np.asarray(r["out"]) for r in res.results], axis=0)
    return np.ascontiguousarray(out.astype(np.float32))
```

```python
from concourse.bass_utils import run_bass_kernel_spmd
import numpy as np
from contextlib import ExitStack
import concourse.bass as bass
import concourse.mybir as mybir

F32 = mybir.dt.float32
BF16 = mybir.dt.bfloat16
U32 = mybir.dt.uint32
I32 = mybir.dt.int32
ALU = mybir.AluOpType
AF = mybir.ActivationFunctionType
AX = mybir.AxisListType

EPOCH = 16000
N_DMA_SEMS = 20
SAME_ENGINE_SYNC = {"pe": False, "act": True, "dve": True, "pool": True, "sp": False}


class V:
    __slots__ = ("tile", "ap", "lo", "hi")

    def __init__(self, tile, ap, lo, hi):
        self.tile, self.ap, self.lo, self.hi = tile, ap, lo, hi


class T:
    def __init__(self, mk, name, handle, shape, kind):
        self.mk, self.name, self.h, self.shape, self.kind = mk, name, handle, list(shape), kind
        fd = self.shape[1:] if kind != "dram" else self.shape
        self.fshape = fd
        st = [1] * len(fd)
        for i in range(len(fd) - 2, -1, -1):
            st[i] = st[i + 1] * fd[i + 1]
        self.fstride = st
        self.size = int(np.prod(fd)) if fd else 1
        self.recs = {}

    def __getitem__(self, idx):
        if not isinstance(idx, tuple):
            idx = (idx,)
        idx = tuple(idx) + (slice(None),) * (len(self.shape) - len(idx))
        ap = self.h[idx]
        fidx = idx[1:] if self.kind != "dram" else idx
        if self.kind == "psum":
            return V(self, ap, 0, self.size)
        lo = 0
        hi = 0
        for i, ix in enumerate(fidx):
            n = self.fshape[i]
            if isinstance(ix, slice):
                a, b, stp = ix.indices(n)
                assert stp == 1 and b > a, (self.name, idx)
                lo += a * self.fstride[i]
                hi += (b - 1) * self.fstride[i]
            else:
                assert 0 <= ix < n, (self.name, idx)
                lo += ix * self.fstride[i]
                hi += ix * self.fstride[i]
        return V(self, ap, lo, hi + 1)

    def all(self):
        return self[tuple(slice(None) for _ in self.shape)]


class MK:
    def __init__(self, nc):
        self.nc = nc
        self.es = ExitStack()
        self.eng = {"pe": nc.tensor, "act": nc.scalar, "dve": nc.vector, "pool": nc.gpsimd, "sp": nc.sync}
        self.count = {e: 0 for e in self.eng}
        self.esems = {e: [] for e in self.eng}
        self.seen = {e: {} for e in self.eng}
        self.semh = {}
        self.dma_sems = []
        self.dma_val = []
        self.dma_rr = 0
        self.n_wait = 0
        self.n_inst = 0
        for i in range(N_DMA_SEMS):
            s = self.es.enter_context(nc.semaphore("dq%d" % i))
            key = ("dma", i)
            self.semh[key] = s
            self.dma_sems.append(key)
            self.dma_val.append(0)
        self.phase_stack = []

    def sb(self, name, shape, dtype, stack=None):
        self.uid = getattr(self, "uid", 0) + 1
        name = "sb%d_%s" % (self.uid, name)
        h = (stack or self.es).enter_context(self.nc.sbuf_tensor(name, list(shape), dtype))
        return T(self, name, h, shape, "sbuf")

    def ps(self, name, shape, dtype, stack=None):
        h = (stack or self.es).enter_context(self.nc.psum_tensor(name, list(shape), dtype))
        return T(self, name, h, shape, "psum")

    def dram(self, name, shape, dtype, kind="Internal"):
        h = self.nc.dram_tensor(name, list(shape), dtype, kind=kind)
        return T(self, name, h, shape, "dram")

    def _eng_token(self, e):
        c = self.count[e]
        ep = c // EPOCH
        while len(self.esems[e]) <= ep:
            s = self.es.enter_context(self.nc.semaphore("e_%s_%d" % (e, len(self.esems[e]))))
            key = ("eng", e, len(self.esems[e]))
            self.semh[key] = s
            self.esems[e].append(key)
        return self.esems[e][ep], (c % EPOCH) + 1

    def _wait(self, e, key, val):
        if self.seen[e].get(key, 0) >= val:
            return
        self.eng[e].wait_ge(self.semh[key], val)
        self.seen[e][key] = val
        self.n_wait += 1

    def _deps(self, e, reads, writes, strict=()):
        deps = {}
        sdeps = {}
        for v in strict:
            for (k, key, lo, hi), val in v.tile.recs.items():
                if k == "w" and lo < v.hi and v.lo < hi:
                    if sdeps.get(key, 0) < val:
                        sdeps[key] = val
        for key, val in sdeps.items():
            self._wait(e, key, val)

        def add(key, val):
            if deps.get(key, 0) < val:
                deps[key] = val

        for v in reads:
            for (k, key, lo, hi), val in v.tile.recs.items():
                if k == "w" and lo < v.hi and v.lo < hi:
                    add(key, val)
        for v in writes:
            for (k, key, lo, hi), val in v.tile.recs.items():
                if lo < v.hi and v.lo < hi:
                    add(key, val)
        for key, val in deps.items():
            if key[0] == "eng" and key[1] == e and not SAME_ENGINE_SYNC[e]:
                continue
            self._wait(e, key, val)

    def _record(self, key, val, reads, writes):
        for v in reads:
            v.tile.recs[("r", key, v.lo, v.hi)] = val
        for v in writes:
            recs = v.tile.recs
            dead = [r for r in recs if v.lo <= r[2] and r[3] <= v.hi]
            for r in dead:
                del recs[r]
            recs[("w", key, v.lo, v.hi)] = val

    def op(self, e, build, reads=(), writes=(), strict=()):
        reads = [r for r in reads if r is not None]
        writes = list(writes) + [r for r in reads if r.tile.kind == "psum"]
        reads = [r for r in reads if r.tile.kind != "psum"]
        self._deps(e, reads, writes, strict)
        key, val = self._eng_token(e)
        ins = build(self.eng[e])
        ins.then_inc(self.semh[key], 1)
        self.count[e] += 1
        self.n_inst += 1
        self._record(key, val, reads, writes)
        return ins

    def dma(self, q, out, in_, reads=None, writes=None, indirect=None, in_ap=None, out_ap=None, **kw):
        reads = [in_] if reads is None else reads
        writes = [out] if writes is None else writes
        in_ap = in_.ap if in_ap is None else in_ap
        out_ap = out.ap if out_ap is None else out_ap
        i = self.dma_rr
        self.dma_rr = (self.dma_rr + 1) % N_DMA_SEMS
        key = self.dma_sems[i]
        self._wait(q, key, self.dma_val[i])
        self._deps(q, reads, writes)
        self.dma_val[i] += 16
        val = self.dma_val[i]
        if indirect is not None:
            ins = indirect(self.eng[q])
        else:
            ins = self.eng[q].dma_start(out=out_ap, in_=in_ap, **kw)
        ins.then_inc(self.semh[key], 16)
        self.n_inst += 1
        self._record(key, val, reads, writes)
        return ins

    def barrier(self):
        toks = []
        for e in self.eng:
            c = self.count[e]
            if c == 0:
                continue
            ep = (c - 1) // EPOCH
            toks.append((self.esems[e][ep], ((c - 1) % EPOCH) + 1))
        for i, key in enumerate(self.dma_sems):
            if self.dma_val[i]:
                toks.append((key, self.dma_val[i]))
        for e in self.eng:
            for key, val in toks:
                if key[0] == "eng" and key[1] == e:
                    continue
                self._wait(e, key, val)

    def finish(self):
        self.barrier()
        self.es.close()

    def mm(self, out, lhsT, rhs, start=True, stop=True, **kw):
        return self.op("pe", lambda g: g.matmul(out.ap, lhsT.ap, rhs.ap, start=start, stop=stop, **kw),
                       reads=[lhsT, rhs], writes=[out])

    def transpose(self, out, in_, ident):
        return self.op("pe", lambda g: g.transpose(out.ap, in_.ap, ident.ap), reads=[in_, ident], writes=[out])

    def act(self, out, in_, func, bias=None, scale=None, accum=None, e="act"):
        kw = {}
        rd = [in_]
        wr = [out]
        sr = []
        if bias is not None:
            if isinstance(bias, V):
                kw["bias"] = bias.ap
                rd.append(bias)
                sr.append(bias)
            else:
                kw["bias"] = bias
        if scale is not None:
            if isinstance(scale, V):
                kw["scale"] = scale.ap
                rd.append(scale)
                sr.append(scale)
            else:
                kw["scale"] = scale
        if accum is not None:
            kw["accum_out"] = accum.ap
            wr.append(accum)
        return self.op(e, lambda g: g.activation(out.ap, in_.ap, func, **kw), reads=rd, writes=wr, strict=sr)

    def tt(self, e, out, a, b, op):
        return self.op(e, lambda g: g.tensor_tensor(out.ap, a.ap, b.ap, op), reads=[a, b], writes=[out])

    def ts(self, e, out, a, s1, s2, op0, op1=None, accum=None):
        rd = [a]
        wr = [out]
        sr = []
        s1a = s1
        s2a = s2
        if isinstance(s1, V):
            rd.append(s1)
            sr.append(s1)
            s1a = s1.ap
        if isinstance(s2, V):
            rd.append(s2)
            sr.append(s2)
            s2a = s2.ap
        kw = {}
        if op1 is not None:
            kw["op1"] = op1
        if accum is not None:
            kw["accum_out"] = accum.ap
            wr.append(accum)
        return self.op(e, lambda g: g.tensor_scalar(out.ap, a.ap, s1a, s2a, op0, **kw), reads=rd, writes=wr, strict=sr)

    def stt(self, e, out, a, s, b, op0, op1):
        rd = [a, b]
        sr = []
        sa = s
        if isinstance(s, V):
            rd.append(s)
            sr.append(s)
            sa = s.ap
        return self.op(e, lambda g: g.scalar_tensor_tensor(out.ap, a.ap, sa, b.ap, op0, op1), reads=rd, writes=[out], strict=sr)

    def copy(self, e, out, in_):
        if e == "act":
            return self.op(e, lambda g: g.copy(out.ap, in_.ap), reads=[in_], writes=[out])
        return self.op(e, lambda g: g.tensor_copy(out.ap, in_.ap), reads=[in_], writes=[out])

    def memset(self, e, out, val):
        return self.op(e, lambda g: g.memset(out.ap, val), reads=[], writes=[out])

    def recip(self, out, in_):
        return self.op("dve", lambda g: g.reciprocal(out.ap, in_.ap), reads=[in_], writes=[out])


import numpy as np
import math
from contextlib import ExitStack

S = 2048
D = 1024
NT = 16
ALPHA = (2.0 * 2) ** 0.25
EPS = 1e-6


class Rot:
    def __init__(self, items):
        self.items, self.i = list(items), 0

    def __call__(self):
        x = self.items[self.i % len(self.items)]
        self.i += 1
        return x


def pv(bank, p0, p1, c0, c1, shape=None):
    ap = bank.h[p0:p1, c0:c1]
    if shape is not None:
        ap = ap.rearrange(shape[0], **shape[1])
    return V(bank, ap, 0, bank.size)


def wsrc(h, r0, r1, c0, c1):
    return h[r0:r1, c0:c1].rearrange("(c p) n -> p c n", p=128)


class Ctx:
    pass


class Cut(Exception):
    pass


def cut(C, n):
    if getattr(C, "cut", None) == n:
        raise Cut()


def setup(mk, dbg=False):
    C = Ctx()
    C.mk = mk
    C.dbg = dbg
    C.dumps = {}
    d = lambda n, s, t=F32: mk.dram(n, s, t, kind="ExternalInput")
    C.x_in = d("x_in", [2, S, D])
    C.pT = d("pT", [2, 2, 256, S])
    C.w_in_even = d("w_in_even", [D, 1984])
    C.w_kpe_sw = d("w_kpe_sw", [D, 32])
    C.w_uq = d("w_uq", [256, 768])
    C.w_uq_sw = d("w_uq_sw", [256, 256])
    C.w_ukv = d("w_ukv", [128, 1024])
    C.mla_qn = d("mla_qn", [128, 2])
    C.mla_kvn = d("mla_kvn", [128, 1])
    C.gla_gw = d("gla_gw", [2, 17, 256])
    C.gla_norm_bc = d("gla_norm_bc", [128, 512])
    C.w_in_odd = d("w_in_odd", [D, 1536])
    C.w_q_sw = d("w_q_sw", [D, 1024])
    C.w_k_sw = d("w_k_sw", [D, 256])
    C.gqa_n = d("gqa_n", [128, 4])
    C.w_o = d("w_o", [2, D, D])
    C.lnbc = d("lnbc", [2, 5, 128, D])
    C.router_w = d("router_w", [2, D, 16])
    C.w1 = d("w1", [2, 16, D, D])
    C.w3 = d("w3", [2, 16, D, D])
    C.w2 = d("w2", [2, 16, D, D])
    C.ple_gate_w = d("ple_gate_w", [2, D, D])
    C.ple_w = d("ple_w", [2, 256, D])
    C.cmat = d("cmat", [7, 128, 128])
    C.cmask = d("cmask", [2, 128, 512])
    C.rope_a = d("rope_a", [2, 128, S])
    C.rope_c = d("rope_c", [2, 128, S])
    C.out = mk.dram("out", [2, S, D], F32, kind="ExternalOutput")
    C.acc = [[mk.dram("acc_%d_%d" % (L, s), [S, D], F32) for s in range(2)] for L in range(2)]
    C.xrows = [[mk.dram("xrows_%d_%d" % (L, s), [S, D], BF16) for s in range(2)] for L in range(2)]
    C.xln = [mk.dram("xln_%d" % s, [S, D], F32) for s in range(2)]
    C.P = [mk.ps("pb%d" % i, [128, 512], F32) for i in range(7)]
    C.PH = mk.ps("pbh", [128, 1024], BF16)
    names = ["ident", "ones", "bd_ones", "triF", "triS", "triB", "triSB"]
    C.cm = mk.sb("cm", [128, 7, 128], F32)
    mk.dma("sp", C.cm.all(), None, reads=[], in_ap=C.cmat.h[:, :, :].rearrange("k p n -> p k n"))
    for i, n in enumerate(names):
        setattr(C, n, C.cm[:, i, :])
    C.cmb = mk.sb("cmb", [128, 7, 128], BF16)
    mk.copy("dve", C.cmb.all(), C.cm.all())
    for i, n in enumerate(names):
        setattr(C, n + "b", C.cmb[:, i, :])
    C.maskt = mk.sb("maskt", [128, 2, 512], BF16)
    mk.dma("pool", C.maskt.all(), None, reads=[], in_ap=C.cmask.h[:, :, :].rearrange("k p n -> p k n"))
    C.eps = mk.sb("eps", [128, 1], F32)
    mk.memset("dve", C.eps.all(), EPS)
    return C


def dump(C, name, view, shape, dtype):
    if not C.dbg:
        return
    mk = C.mk
    t = mk.dram("dbg_" + name, shape, dtype, kind="ExternalOutput")
    mk.dma("sp", t.all(), view)
    C.dumps[name] = "dbg_" + name


def layer_norm_tile(mk, C, xt, g, b, tmp, st):
    FM = 512
    for j in range(2):
        mk.op("dve", lambda e, j=j: e.bn_stats(st[:, j * 6:(j + 1) * 6].ap, xt[:, j * FM:(j + 1) * FM].ap),
              reads=[xt[:, j * FM:(j + 1) * FM]], writes=[st[:, j * 6:(j + 1) * 6]])
    mk.op("dve", lambda e: e.bn_aggr(st[:, 12:14].ap, st.h[:, 0:12].rearrange("p (n k) -> p n k", k=6)),
          reads=[st[:, 0:12]], writes=[st[:, 12:14]])
    mk.act(st[:, 14:15], st[:, 13:14], AF.Sqrt, bias=C.eps[:, 0:1])
    mk.recip(st[:, 15:16], st[:, 14:15])
    mk.ts("dve", tmp.all(), xt.all(), st[:, 12:13], st[:, 15:16], ALU.subtract, ALU.mult)
    mk.tt("pool", tmp.all(), tmp.all(), g, ALU.mult)
    mk.tt("dve", xt.all(), tmp.all(), b, ALU.add)


def transpose_tile_to_xT(mk, C, xt, xT, tt, banks, ei):
    for g in range(2):
        bank = banks()
        for j in range(4):
            dc = g * 4 + j
            mk.transpose(pv(bank, 0, 128, j * 128, (j + 1) * 128), xt[:, dc * 128:(dc + 1) * 128], C.ident)
        src = pv(bank, 0, 128, 0, 512, ("p (c n) -> p c n", dict(c=4)))
        dst = xT[:, g * 4:(g + 1) * 4, tt * 128:(tt + 1) * 128]
        mk.copy(ei(), dst, src)


def phase_prologue(mk, C, L, s, xT, ln_g=None, ln_b=None, stk=None):
    src = C.x_in if L == 0 else C.acc[0][s]
    xts = [mk.sb("pro_x%d" % i, [128, D], F32, stk) for i in range(3)]
    tmp = mk.sb("pro_tmp", [128, D], F32, stk)
    sts = [mk.sb("pro_st%d" % i, [128, 16], F32, stk) for i in range(2)]
    banks = Rot([C.P[0], C.P[1]])
    ei = Rot(["act", "dve"])
    for tt in range(NT):
        xt = xts[tt % 3]
        if L == 0:
            mk.dma("sp", xt.all(), src[s, tt * 128:(tt + 1) * 128, :], reads=[])
        else:
            mk.dma("sp", xt.all(), src[tt * 128:(tt + 1) * 128, :])
            layer_norm_tile(mk, C, xt, ln_g, ln_b, tmp, sts[tt % 2])
            mk.dma("sp", C.xln[s][tt * 128:(tt + 1) * 128, :], xt.all())
        transpose_tile_to_xT(mk, C, xt, xT, tt, banks, ei)


def proj_fm(mk, out_ps, w, c0, c1, xT, t0, t1, nk=8):
    for kc in range(nk):
        mk.mm(out_ps, w[:, kc, c0:c1], xT[:, kc, t0:t1], start=(kc == 0), stop=(kc == nk - 1))


def rope_evac(mk, C, psA, psB, rope, p0, p1, t0, t1, dst, tmpa, tmpb):
    mk.tt("dve", tmpa[p0:p1, 0:t1 - t0], psA, rope[p0:p1, 0, t0:t1], ALU.mult)
    mk.tt("dve", tmpb[p0:p1, 0:t1 - t0], psB, rope[p0:p1, 1, t0:t1], ALU.mult)
    mk.tt("pool", dst, tmpa[p0:p1, 0:t1 - t0], tmpb[p0:p1, 0:t1 - t0], ALU.add)


def attention(mk, C, QT, KT, VA, mixT, nheads, kdim, scale, kmap, stk):
    pts = [mk.sb("att_pt%d" % i, [128, 512], BF16, stk) for i in range(3)]
    rec = [mk.sb("att_rec%d" % i, [128, 512], F32, stk) for i in range(2)]
    sbk = [C.P[0], C.P[1], C.P[2]]
    obk = [C.P[3], C.P[4]]
    steps = [(h, qb, kt) for h in range(nheads) for qb in range(4) for kt in range(NT)]

    def qk(i):
        h, qb, kt = steps[i]
        qb0, qs, ks, vs, oc, ob0 = kmap(h)
        mk.mm(pv(sbk[i % 3], 0, 128, 0, 512), KT[qb0:qb0 + kdim, ks, kt * 128:(kt + 1) * 128],
              QT[qb0:qb0 + kdim, qs, qb * 512:(qb + 1) * 512])

    qk(0)
    for i, (h, qb, kt) in enumerate(steps):
        qb0, qs, ks, vs, oc, ob0 = kmap(h)
        if i + 1 < len(steps):
            qk(i + 1)
        obank = obk[(h * 4 + qb) % 2]
        pt = pts[i % 3]
        mk.act(pt.all(), pv(sbk[i % 3], 0, 128, 0, 512), AF.Exp, scale=scale)
        mk.mm(pv(obank, 0, 128, 0, 512), VA[:, kt, vs, :], pt.all(), start=(kt == 0), stop=(kt == NT - 1))
        if kt == NT - 1:
            r = rec[(h * 4 + qb) % 2]
            mk.recip(r[ob0:ob0 + 64, :], pv(obank, 64, 128, 0, 512))
            mk.tt("dve", mixT[ob0:ob0 + 64, oc, qb * 512:(qb + 1) * 512], pv(obank, 0, 64, 0, 512),
                  r[ob0:ob0 + 64, :], ALU.mult)


def attention_pairs(mk, C, QT, KT, VA, mixT, npairs, scale, pmap, stk):
    pts = [mk.sb("atp_pt%d" % i, [128, 512], BF16, stk) for i in range(4)]
    rec = [mk.sb("atp_rec%d" % i, [128, 512], F32, stk) for i in range(2)]
    SA = [C.P[0], C.P[1]]
    SB = [C.P[2], C.P[3]]
    OB = [C.P[4], C.P[5]]
    steps = [(p, qb, kt) for p in range(npairs) for qb in range(4) for kt in range(NT)]

    def qk(i):
        p, qb, kt = steps[i]
        qs, ks, vs, oc = pmap(p)
        ksl = slice(kt * 128, (kt + 1) * 128)
        qsl = slice(qb * 512, (qb + 1) * 512)
        mk.mm(pv(SA[i % 2], 0, 128, 0, 512), KT[0:64, ks, ksl], QT[0:64, qs, qsl])
        mk.mm(pv(SB[i % 2], 0, 128, 0, 512), KT[64:128, ks, ksl], QT[64:128, qs, qsl])

    qk(0)
    for i, (p, qb, kt) in enumerate(steps):
        qs, ks, vs, oc = pmap(p)
        if i + 1 < len(steps):
            qk(i + 1)
        ptA, ptB = pts[(2 * i) % 4], pts[(2 * i + 1) % 4]
        mk.act(ptA.all(), pv(SA[i % 2], 0, 128, 0, 512), AF.Exp, scale=scale)
        mk.act(ptB.all(), pv(SB[i % 2], 0, 128, 0, 512), AF.Exp, scale=scale)
        mk.mm(pv(OB[0], 0, 128, 0, 512), VA[:, kt, vs, :], ptA.all(), start=(kt == 0), stop=(kt == NT - 1))
        mk.mm(pv(OB[1], 0, 128, 0, 512), VA[:, kt, vs, :], ptB.all(), start=(kt == 0), stop=(kt == NT - 1))
        if kt == NT - 1:
            for j, ob0 in enumerate([0, 64]):
                r = rec[j]
                mk.recip(r[ob0:ob0 + 64, :], pv(OB[j], 64, 128, 0, 512))
                mk.tt("dve", mixT[ob0:ob0 + 64, oc, qb * 512:(qb + 1) * 512], pv(OB[j], 0, 64, 0, 512),
                      r[ob0:ob0 + 64, :], ALU.mult)


def even_mla(mk, C, s, xT, mixT):
    stk = ExitStack()
    we = C.w_in_even.h
    wq = mk.sb("mla_wq", [128, 8, 256], BF16, stk)
    wkv = mk.sb("mla_wkv", [128, 8, 128], BF16, stk)
    wkpe = mk.sb("mla_wkpe", [128, 8, 2, 96], BF16, stk)
    wuq = mk.sb("mla_wuq", [128, 2, 768], BF16, stk)
    wuqs = mk.sb("mla_wuqs", [128, 2, 8, 96], BF16, stk)
    wukv = mk.sb("mla_wukv", [128, 1024], BF16, stk)
    qn = mk.sb("mla_qn", [128, 2], F32, stk)
    kvn = mk.sb("mla_kvn", [128, 1], F32, stk)
    rope = mk.sb("mla_rope", [128, 2, S], F32, stk)
    mk.dma("pool", wq.all(), None, reads=[], in_ap=wsrc(we, 0, D, 0, 256))
    mk.dma("pool", wkv.all(), None, reads=[], in_ap=wsrc(we, 0, D, 256, 384))
    mk.memset("pool", wkpe.all(), 0.0)
    mk.memset("pool", wuqs.all(), 0.0)
    mk.dma("pool", wkpe[:, :, 0, 64:96], None, reads=[], in_ap=wsrc(we, 0, D, 384, 416))
    mk.dma("pool", wkpe[:, :, 1, 64:96], None, reads=[], in_ap=wsrc(C.w_kpe_sw.h, 0, D, 0, 32))
    mk.dma("pool", wuq.all(), None, reads=[], in_ap=wsrc(C.w_uq.h, 0, 256, 0, 768))
    for kc in range(2):
        mk.dma("pool", wuqs[:, kc, :, 64:96], None, reads=[],
               in_ap=C.w_uq_sw.h[kc * 128:(kc + 1) * 128, :].rearrange("p (h e) -> p h e", h=8))
    mk.dma("pool", wukv.all(), None, reads=[], in_ap=C.w_ukv.h[:, :])
    mk.dma("sp", qn.all(), None, reads=[], in_ap=C.mla_qn.h[:, :])
    mk.dma("sp", kvn.all(), None, reads=[], in_ap=C.mla_kvn.h[:, :])
    mk.dma("sp", rope[64:96, :, :], None, reads=[], in_ap=C.rope_a.h[:, 64:96, :].rearrange("k p n -> p k n"))

    cqn = mk.sb("mla_cqn", [128, 2, S], BF16, stk)
    ckvn = mk.sb("mla_ckvn", [128, S], BF16, stk)
    kper = mk.sb("mla_kper", [128, S], BF16, stk)
    QT = mk.sb("mla_QT", [128, 4, S], BF16, stk)
    KT = mk.sb("mla_KT", [128, 4, S], BF16, stk)
    VA = mk.sb("mla_VA", [128, NT, 4, 128], BF16, stk)
    st2 = ExitStack()
    cqf = [mk.sb("mla_cqf%d" % i, [128, 512], F32, st2) for i in range(3)]
    sq = [mk.sb("mla_sq%d" % i, [128, 512], F32, st2) for i in range(3)]
    rs = [mk.sb("mla_rs%d" % i, [128, 512], F32, st2) for i in range(2)]
    sqh = [mk.sb("mla_sqh%d" % i, [128, 2, 512], BF16, st2) for i in range(3)]
    tmpa = mk.sb("mla_tmpa", [128, 512], F32, st2)
    tmpb = mk.sb("mla_tmpb", [128, 512], F32, st2)
    P = C.P
    cut(C, 1)
    for tb in range(4):
        t0, t1 = tb * 512, (tb + 1) * 512
        groups = [(wq, 0, 128, qn[:, 0:1], cqn[:, 0, t0:t1]),
                  (wq, 128, 256, qn[:, 1:2], cqn[:, 1, t0:t1]),
                  (wkv, 0, 128, kvn[:, 0:1], ckvn[:, t0:t1])]
        for gi, (w, c0, c1, gain, dst) in enumerate(groups):
            bank = P[gi]
            proj_fm(mk, pv(bank, 0, 128, 0, 512), w, c0, c1, xT, t0, t1)
            mk.act(sq[gi].all(), pv(bank, 0, 128, 0, 512), AF.Square)
            mk.copy("dve", cqf[gi].all(), pv(bank, 0, 128, 0, 512))
        cut(C, 2)
        for gi in range(3):
            mk.copy("pool", sqh[gi][:, 0, :], sq[gi].all())
            mk.tt("pool", sqh[gi][:, 1, :], sq[gi].all(), sqh[gi][:, 0, :], ALU.subtract)
        for j, (gi, hl) in enumerate([(0, 0), (0, 1), (1, 0), (1, 1)]):
            mk.mm(pv(P[3], 0, 128, 0, 512), C.onesb, sqh[gi][:, hl, :], start=(j == 0), stop=(j == 3))
        for hl in range(2):
            mk.mm(pv(P[4], 0, 128, 0, 512), C.onesb, sqh[2][:, hl, :], start=(hl == 0), stop=(hl == 1))
        cut(C, 3)
        mk.act(rs[0].all(), pv(P[3], 0, 128, 0, 512), AF.Sqrt, bias=C.eps[:, 0:1], scale=1.0 / 256)
        mk.recip(rs[0].all(), rs[0].all())
        mk.act(rs[1].all(), pv(P[4], 0, 128, 0, 512), AF.Sqrt, bias=C.eps[:, 0:1], scale=1.0 / 128)
        mk.recip(rs[1].all(), rs[1].all())
        for gi, (w, c0, c1, gain, dst) in enumerate(groups):
            mk.stt("dve", dst, cqf[gi].all(), gain, rs[0 if gi < 2 else 1].all(), ALU.mult, ALU.mult)
        cut(C, 4)
        for kc in range(8):
            mk.mm(pv(P[5], 0, 96, 0, 512), wkpe[:, kc, 0, :], xT[:, kc, t0:t1], start=(kc == 0), stop=(kc == 7))
        for kc in range(8):
            mk.mm(pv(P[6], 0, 96, 0, 512), wkpe[:, kc, 1, :], xT[:, kc, t0:t1], start=(kc == 0), stop=(kc == 7))
        cut(C, 5)
        mk.tt("dve", tmpa[64:96, :], pv(P[5], 64, 96, 0, 512), rope[64:96, 0, t0:t1], ALU.mult)
        mk.tt("dve", tmpb[64:96, :], pv(P[6], 64, 96, 0, 512), rope[64:96, 1, t0:t1], ALU.mult)
        mk.tt("pool", kper[64:96, t0:t1], tmpa[64:96, :], tmpb[64:96, :], ALU.add)
        cut(C, 6)
    stop = getattr(C, "stop", 99)
    if stop <= 1:
        dump(C, "cqn", cqn.all(), [128, 2, S], BF16)
        dump(C, "kper", kper[64:96, :], [32, S], BF16)
    for hg in range(2 if stop > 1 else 0):
        mk.memset("pool", VA.all(), 1.0)
        for tb in range(4):
            t0, t1 = tb * 512, (tb + 1) * 512
            ab = Rot([P[0], P[1]])
            bb = Rot([P[2], P[5]])
            kb = Rot([P[3], P[4]])
            for hl in range(4):
                h = hg * 4 + hl
                A = ab()
                B = bb()
                K = kb()
                for kc in range(2):
                    mk.mm(pv(A, 0, 96, 0, 512), wuq[:, kc, h * 96:(h + 1) * 96], cqn[:, kc, t0:t1], start=(kc == 0), stop=(kc == 1))
                for kc in range(2):
                    mk.mm(pv(B, 0, 96, 0, 512), wuqs[:, kc, h, :], cqn[:, kc, t0:t1], start=(kc == 0), stop=(kc == 1))
                mk.mm(pv(K, 0, 64, 0, 512), wukv[:, h * 128:h * 128 + 64], ckvn[:, t0:t1])
                mk.copy("act", QT[0:64, hl, t0:t1], pv(A, 0, 64, 0, 512))
                ta, tb_ = (tmpa, tmpb) if hl % 2 == 0 else (sq[0], sq[1])
                mk.tt("dve", ta[64:96, :], pv(A, 64, 96, 0, 512), rope[64:96, 0, t0:t1], ALU.mult)
                mk.tt("dve", tb_[64:96, :], pv(B, 64, 96, 0, 512), rope[64:96, 1, t0:t1], ALU.mult)
                mk.tt("pool", QT[64:96, hl, t0:t1], ta[64:96, :], tb_[64:96, :], ALU.add)
                mk.copy("act", KT[0:64, hl, t0:t1], pv(K, 0, 64, 0, 512))
                mk.copy("pool", KT[64:96, hl, t0:t1], kper[64:96, t0:t1])
            for j in range(4):
                tt = tb * 4 + j
                bank = P[6]
                rhs_ap = wukv.h[:, hg * 512:(hg + 1) * 512].rearrange("p (h e) -> p h e", h=4)[:, :, 64:128]
                rhs = V(wukv, rhs_ap, hg * 512, (hg + 1) * 512)
                mk.mm(pv(bank, 0, 128, 0, 256, ("p (h e) -> p h e", dict(h=4))), ckvn[:, tt * 128:(tt + 1) * 128], rhs)
                mk.copy("act" if j % 2 else "dve", VA[:, tt, :, 0:64], pv(bank, 0, 128, 0, 256, ("p (h e) -> p h e", dict(h=4))))
        if C.dbg and s == 0:
            dump(C, "QT%d" % hg, QT.all(), [128, 4, S], BF16)
            dump(C, "KT%d" % hg, KT.all(), [128, 4, S], BF16)
            dump(C, "VA%d" % hg, VA.all(), [128, NT, 4, 128], BF16)
        if stop <= 2:
            continue
        st3 = ExitStack()
        attention(mk, C, QT, KT, VA, mixT, 4, 96, 96.0 ** -0.5,
                  lambda hl: (0, hl, hl, hl, (hg * 4 + hl) // 2, (hl % 2) * 64), st3)
        st3.close()
    st2.close()
    stk.close()
    mk.barrier()


def even_gla(mk, C, s, xT, mixT):
    stk = ExitStack()
    we = C.w_in_even.h
    P = C.P
    wfm = mk.sb("gla_wfm", [128, 8, 512], BF16, stk)
    wtm = mk.sb("gla_wtm", [128, 8, 1280], BF16, stk)
    wlr = mk.sb("gla_wlr", [128, 8, 32], BF16, stk)
    gw = mk.sb("gla_gw", [17, 2, 256], BF16, stk)
    gnorm = mk.sb("gla_gnorm", [128, 512], F32, stk)
    one1 = mk.sb("gla_one1", [128, 1], F32, stk)
    mk.memset("dve", one1.all(), 1.0)
    mk.dma("pool", wfm.all(), None, reads=[], in_ap=wsrc(we, 0, D, 416, 928))
    mk.dma("pool", wtm[:, :, 0:768], None, reads=[], in_ap=wsrc(we, 0, D, 672, 1440))
    mk.dma("pool", wtm[:, :, 768:1280], None, reads=[], in_ap=wsrc(we, 0, D, 1472, 1984))
    mk.dma("pool", wlr.all(), None, reads=[], in_ap=wsrc(we, 0, D, 1440, 1472))
    mk.dma("pool", gw.all(), None, reads=[], in_ap=C.gla_gw.h[:, :, :].rearrange("k r n -> r k n"))
    mk.dma("sp", gnorm.all(), None, reads=[], in_ap=C.gla_norm_bc.h[:, :])
    gqT = mk.sb("gla_gqT", [128, 2, S], BF16, stk)
    gkT = mk.sb("gla_gkT", [128, 2, S], BF16, stk)
    gk_tok = mk.sb("gla_gk_tok", [128, NT, 256], BF16, stk)
    gv_tok = mk.sb("gla_gv_tok", [128, NT, 512], BF16, stk)
    o_f = mk.sb("gla_of", [128, NT, 512], F32, stk)
    ei = Rot(["act", "dve"])
    bk = Rot([P[0], P[1], P[2]])
    for tb in range(4):
        t0, t1 = tb * 512, (tb + 1) * 512
        for mc in range(4):
            bank = bk()
            proj_fm(mk, pv(bank, 0, 128, 0, 512), wfm, mc * 128, (mc + 1) * 128, xT, t0, t1)
            dst = (gqT if mc < 2 else gkT)[:, mc % 2, t0:t1]
            mk.copy(ei(), dst, pv(bank, 0, 128, 0, 512))
    for tt in range(NT):
        tk = slice(tt * 128, (tt + 1) * 128)
        b1, b2 = bk(), bk()
        for kc in range(8):
            mk.mm(pv(b1, 0, 128, 0, 256), xT[:, kc, tk], wtm[:, kc, 0:256], start=(kc == 0), stop=(kc == 7))
        for kc in range(8):
            mk.mm(pv(b2, 0, 128, 0, 512), xT[:, kc, tk], wtm[:, kc, 256:768], start=(kc == 0), stop=(kc == 7))
        mk.copy("act", gk_tok[:, tt, :], pv(b1, 0, 128, 0, 256))
        mk.copy("dve", gv_tok[:, tt, :], pv(b2, 0, 128, 0, 512))
    cut(C, 11)
    st2 = ExitStack()
    lrT = [mk.sb("gla_lrT%d" % i, [17, 128], BF16, st2) for i in range(2)]
    for t in lrT:
        mk.memset("dve", t.all(), 1.0)
    ez = [mk.sb("gla_ez%d" % i, [128, 256], F32, st2) for i in range(2)]
    nla = [mk.sb("gla_nla%d" % i, [128, 256], F32, st2) for i in range(2)]
    nlah = [mk.sb("gla_nlah%d" % i, [128, 2, 256], BF16, st2) for i in range(2)]
    E1 = [mk.sb("gla_E1%d" % i, [128, 2, 128], F32, st2) for i in range(3)]
    E2 = [mk.sb("gla_E2%d" % i, [128, 2, 128], F32, st2) for i in range(2)]
    E3 = [mk.sb("gla_E3%d" % i, [128, 256], F32, st2) for i in range(2)]
    ke = [mk.sb("gla_ke%d" % i, [128, 256], BF16, st2) for i in range(2)]
    qgT = [mk.sb("gla_qgT%d" % i, [128, 2, 128], BF16, st2) for i in range(2)]
    kgT = [mk.sb("gla_kgT%d" % i, [128, 2, 128], BF16, st2) for i in range(2)]
    attm = [mk.sb("gla_attm%d" % i, [128, 2, 2, 128], BF16, st2) for i in range(2)]
    Sf = mk.sb("gla_Sf", [128, 2, 128], F32, st2)
    Sb = [mk.sb("gla_Sb%d" % i, [128, 2, 128], BF16, st2) for i in range(4)]
    osum = [mk.sb("gla_osum%d" % i, [128, 512], F32, st2) for i in range(1)] * 2
    osq = mk.sb("gla_osq", [128, 512], F32, st2)
    sg = [mk.sb("gla_sg%d" % i, [128, 512], F32, st2) for i in range(1)] * 2
    og = [mk.sb("gla_og%d" % i, [128, 512], BF16, st2) for i in range(1)] * 2
    stt_ = [mk.sb("gla_st%d" % i, [128, 12], F32, st2) for i in range(2)]
    it = 0
    sbi = 0
    for d in range(2):
        tri = C.triFb if d == 0 else C.triBb
        tris = C.triSb if d == 0 else C.triSBb
        mk.memset("dve", Sf.all(), 0.0)
        mk.memset("dve", Sb[sbi % 4].all(), 0.0)
        tiles = range(NT) if d == 0 else range(NT - 1, -1, -1)
        order = [0, 1] if d == 0 else [1, 0]
        for tt in tiles:
            i2 = it % 2
            it += 1
            tk = slice(tt * 128, (tt + 1) * 128)
            for kc in range(8):
                mk.mm(pv(P[0], 0, 16, 0, 128), wlr[:, kc, d * 16:(d + 1) * 16], xT[:, kc, tk], start=(kc == 0), stop=(kc == 7))
            mk.copy("act", lrT[i2][0:16, :], pv(P[0], 0, 16, 0, 128))
            mk.mm(pv(P[0], 0, 128, 128, 384), lrT[i2][0:17, :], gw[0:17, d, :])
            mk.act(ez[i2].all(), pv(P[0], 0, 128, 128, 384), AF.Exp, scale=-1.0)
            mk.act(nla[i2].all(), ez[i2].all(), AF.Ln, bias=one1[:, 0:1])
            mk.copy("pool", nlah[i2][:, 0, :], nla[i2].all())
            mk.tt("pool", nlah[i2][:, 1, :], nla[i2].all(), nlah[i2][:, 0, :], ALU.subtract)
            cut(C, 12)
            for pc in range(2):
                for hl in range(2):
                    mk.mm(pv(P[1], 0, 128, pc * 128, (pc + 1) * 128), nlah[i2][:, hl, pc * 128:(pc + 1) * 128], tri,
                          start=(hl == 0), stop=(hl == 1))
            for hl in range(2):
                mk.mm(pv(P[1], 0, 128, 256, 512), tris, nlah[i2][:, hl, :], start=(hl == 0), stop=(hl == 1))
            e1 = E1[it % 3]
            cumv = pv(P[1], 0, 128, 0, 256, ("p (c n) -> p c n", dict(c=2)))
            mk.act(e1.all(), cumv, AF.Exp, scale=-1.0 / 16)
            mk.act(E2[i2].all(), cumv, AF.Exp, scale=1.0 / 16)
            mk.act(E3[i2].all(), pv(P[1], 0, 128, 256, 512), AF.Exp, scale=-1.0 / 16)
            mk.tt("pool", ke[i2].all(), gk_tok[:, tt, :], E3[i2].all(), ALU.mult)
            mk.stt("dve", qgT[i2].all(), gqT[:, :, tk], 0.125, e1.all(), ALU.mult, ALU.mult)
            mk.tt("dve", kgT[i2].all(), gkT[:, :, tk], E2[i2].all(), ALU.mult)
            cut(C, 13)
            for h in range(4):
                pc, b0 = h // 2, (h % 2) * 64
                bank = P[2] if h % 2 == 0 else P[5]
                mk.mm(pv(bank, 0, 128, pc * 128, (pc + 1) * 128), kgT[i2][b0:b0 + 64, pc, :], qgT[i2][b0:b0 + 64, pc, :])
            for hp in range(2):
                bank = P[2] if hp == 0 else P[5]
                mk.tt("dve", attm[i2][:, :, hp, :], pv(bank, 0, 128, 0, 256, ("p (a n) -> p a n", dict(a=2))),
                      V(C.maskt, C.maskt.h[:, d, 0:256].rearrange("p (a n) -> p a n", a=2), d * 512, d * 512 + 256), ALU.mult)
            cut(C, 14)
            for c in range(2):
                ubank = P[3] if c == 0 else P[6]
                for h in range(4):
                    pc, j = h // 2, h % 2
                    mk.mm(pv(ubank, j * 64, j * 64 + 64, pc * 128, (pc + 1) * 128), ke[i2][c * 64:(c + 1) * 64, h * 64:(h + 1) * 64],
                          gv_tok[c * 64:(c + 1) * 64, tt, h * 128:(h + 1) * 128])
            cut(C, 15)
            sb_for = {}
            for c in order:
                sb_for[c] = Sb[sbi % 4]
                dcol = c * 64 + (63 if d == 0 else 0)
                ubank = P[3] if c == 0 else P[6]
                for pc in range(2):
                    mk.stt("dve", Sf[:, pc, :], Sf[:, pc, :], e1[:, pc, dcol:dcol + 1], pv(ubank, 0, 128, pc * 128, (pc + 1) * 128), ALU.mult, ALU.add)
                sbi += 1
                mk.copy("pool", Sb[sbi % 4].all(), Sf.all())
            cut(C, 16)
            for h in range(4):
                pc, b0 = h // 2, (h % 2) * 64
                obank = P[4] if h % 2 == 0 else P[1]
                mk.mm(pv(obank, 0, 128, pc * 128, (pc + 1) * 128), attm[i2][:, pc, h % 2, :], gv_tok[:, tt, h * 128:(h + 1) * 128],
                      start=True, stop=False, skip_group_check=True)
                for ci, c in enumerate(order):
                    mk.mm(pv(obank, c * 64, c * 64 + 64, pc * 128, (pc + 1) * 128), qgT[i2][b0:b0 + 64, pc, c * 64:(c + 1) * 64],
                          sb_for[c][b0:b0 + 64, pc, :], start=False, stop=(ci == 1), skip_group_check=True)
            of4 = o_f.h[:, tt, :].rearrange("p (a b e) -> p a b e", a=2, b=2)
            if d == 0:
                for hp in range(2):
                    obank = P[4] if hp == 0 else P[1]
                    mk.copy("act", V(o_f, of4[:, :, hp, :], tt * 512, (tt + 1) * 512),
                            pv(obank, 0, 128, 0, 256, ("p (a e) -> p a e", dict(a=2))))
                cut(C, 17)
                continue
            os_ = osum[i2]
            os4 = os_.h[:, :].rearrange("p (a b e) -> p a b e", a=2, b=2)
            for hp in range(2):
                obank = P[4] if hp == 0 else P[1]
                mk.tt("dve", V(os_, os4[:, :, hp, :], 0, 512), V(o_f, of4[:, :, hp, :], tt * 512, (tt + 1) * 512),
                      pv(obank, 0, 128, 0, 256, ("p (a e) -> p a e", dict(a=2))), ALU.add)
            if C.dbg and s == 0 and getattr(C, "dbg_osum_on", False):
                if "osum" not in C.dumps:
                    C.dbg_osum = mk.dram("dbg_osum", [128, NT, 512], F32, kind="ExternalOutput")
                    C.dumps["osum"] = 1
                mk.dma("sp", C.dbg_osum[:, tt, :], os_.all())
            mk.tt("pool", osq.all(), os_.all(), os_.all(), ALU.mult)
            st = stt_[i2]
            osq3 = V(osq, osq.h[:, :].rearrange("p (h e) -> p h e", h=4), 0, 512)
            mk.op("dve", lambda e, st=st, osq3=osq3: e.tensor_reduce(st[:, 0:4].ap, osq3.ap, AX.X, ALU.add), reads=[osq3], writes=[st[:, 0:4]])
            mk.act(st[:, 4:8], st[:, 0:4], AF.Sqrt, bias=C.eps[:, 0:1], scale=1.0 / 128)
            mk.recip(st[:, 8:12], st[:, 4:8])
            for kc in range(8):
                mk.mm(pv(P[0], 0, 128, 0, 512), xT[:, kc, tk], wtm[:, kc, 768:1280], start=(kc == 0), stop=(kc == 7))
            mk.act(sg[i2].all(), pv(P[0], 0, 128, 0, 512), AF.Silu)
            for h in range(4):
                mk.ts("dve" if h % 2 else "pool", os_[:, h * 128:(h + 1) * 128], os_[:, h * 128:(h + 1) * 128], st[:, 8 + h:9 + h], None, ALU.mult)
            mk.tt("pool", os_.all(), os_.all(), gnorm.all(), ALU.mult)
            mk.tt("dve", og[i2].all(), os_.all(), sg[i2].all(), ALU.mult)
            for h in range(4):
                mk.transpose(pv(C.PH, 0, 128, h * 128, (h + 1) * 128), og[i2][:, h * 128:(h + 1) * 128], C.identb)
            mk.copy("act", mixT[:, 4:8, tk], pv(C.PH, 0, 128, 0, 512, ("p (c n) -> p c n", dict(c=4))))
    if C.dbg and s == 0:
        dump(C, "o_f", o_f.all(), [128, NT, 512], F32)
    st2.close()
    stk.close()
    mk.barrier()


def phase_post(mk, C, L, s, mixT, idx_s, gate_s):
    stk = ExitStack()
    P = C.P
    wo = mk.sb("po_wo", [128, 8, D], BF16, stk)
    wg = mk.sb("po_wg", [128, 8, D], BF16, stk)
    wp = mk.sb("po_wp", [128, 2, D], BF16, stk)
    pT = mk.sb("po_pT", [128, 2, S], BF16, stk)
    rwf = mk.sb("po_rwf", [128, 8, 16], F32, stk)
    rw = mk.sb("po_rw", [128, 8, 2, 16], BF16, stk)
    lnb = mk.sb("po_lnb", [128, 5, D], F32, stk)
    affT = mk.sb("po_affT", [16, S], F32, stk)
    mk.dma("pool", wo.all(), None, reads=[], in_ap=wsrc(C.w_o.h[L], 0, D, 0, D))
    mk.dma("pool", wg.all(), None, reads=[], in_ap=wsrc(C.ple_gate_w.h[L], 0, D, 0, D))
    mk.dma("pool", wp.all(), None, reads=[], in_ap=wsrc(C.ple_w.h[L], 0, 256, 0, D))
    mk.dma("pool", pT.all(), None, reads=[], in_ap=C.pT.h[L, s].rearrange("(c p) n -> p c n", p=128))
    mk.dma("sp", rwf.all(), None, reads=[], in_ap=wsrc(C.router_w.h[L], 0, D, 0, 16))
    mk.dma("sp", lnb.all(), None, reads=[], in_ap=C.lnbc.h[L].rearrange("k p n -> p k n"))
    mk.copy("dve", rw[:, :, 0, :], rwf.all())
    mk.tt("dve", rw[:, :, 1, :], rwf.all(), rw[:, :, 0, :], ALU.subtract)
    stt_ = ExitStack()
    xts = [mk.sb("po_xt%d" % i, [128, D], F32, stt_) for i in range(2)]
    rts = [mk.sb("po_r%d" % i, [128, D], F32, stt_) for i in range(3)]
    ras = [mk.sb("po_ra%d" % i, [128, D], F32, stt_) for i in range(2)]
    x1bs = [mk.sb("po_x1b%d" % i, [128, D], BF16, stt_) for i in range(2)]
    x1Th = [mk.sb("po_x1Th%d" % i, [128, 8, 128], BF16, stt_) for i in range(3)]
    x1Tl = [mk.sb("po_x1Tl%d" % i, [128, 8, 128], BF16, stt_) for i in range(2)]
    accs = [mk.sb("po_acc%d" % i, [128, D], F32, stt_) for i in range(2)]
    tmp = mk.sb("po_tmp", [128, D], F32, stt_)
    gbs = [mk.sb("po_gb%d" % i, [128, 512], F32, stt_) for i in range(2)]
    sgs = [mk.sb("po_sg%d" % i, [128, 512], F32, stt_) for i in range(2)]
    sts = [mk.sb("po_st%d" % i, [128, 20], F32, stt_) for i in range(2)]
    sm = [mk.sb("po_sm%d" % i, [128, 40], F32, stt_) for i in range(2)]
    xsrc = C.x_in if L == 0 else C.xln[s]

    def stage_a(tt):
        tk = slice(tt * 128, (tt + 1) * 128)
        xt, r, st = xts[tt % 2], rts[tt % 3], sts[tt % 2]
        if L == 0:
            mk.dma("sp", xt.all(), xsrc[s, tk, :], reads=[])
        else:
            mk.dma("sp", xt.all(), xsrc[tk, :])
        for half in range(2):
            for kc in range(8):
                mk.mm(pv(P[half], 0, 128, 0, 512), mixT[:, kc, tk], wo[:, kc, half * 512:(half + 1) * 512], start=(kc == 0), stop=(kc == 7))
        for half in range(2):
            hs = slice(half * 512, (half + 1) * 512)
            mk.stt("dve", r[:, hs], xt[:, hs], ALPHA, pv(P[half], 0, 128, 0, 512), ALU.mult, ALU.add)
        for j in range(2):
            mk.op("dve", lambda e, j=j: e.bn_stats(st[:, j * 6:(j + 1) * 6].ap, r[:, j * 512:(j + 1) * 512].ap),
                  reads=[r[:, j * 512:(j + 1) * 512]], writes=[st[:, j * 6:(j + 1) * 6]])
        mk.op("dve", lambda e: e.bn_aggr(st[:, 12:14].ap, st[:, 0:12].ap), reads=[st[:, 0:12]], writes=[st[:, 12:14]])
        mk.act(st[:, 14:15], st[:, 13:14], AF.Sqrt, bias=C.eps[:, 0:1])
        mk.recip(st[:, 15:16], st[:, 14:15])
        mk.stt("dve", st[:, 16:17], st[:, 12:13], -1.0, st[:, 15:16], ALU.mult, ALU.mult)
        mk.act(tmp.all(), r.all(), AF.Identity, bias=st[:, 16:17], scale=st[:, 15:16])
        mk.tt("pool", tmp.all(), tmp.all(), lnb[:, 0, :], ALU.mult)
        mk.tt("pool", r.all(), tmp.all(), lnb[:, 1, :], ALU.add)

    def stage_b(tt):
        tk = slice(tt * 128, (tt + 1) * 128)
        r, x1b, xh, xl, m = rts[tt % 3], x1bs[tt % 2], x1Th[tt % 3], x1Tl[tt % 2], sm[tt % 2]
        mk.copy("act", x1b.all(), r.all())
        mk.dma("sp", C.xrows[L][s][tk, :], x1b.all())
        mk.act(ras[tt % 2].all(), r.all(), AF.Copy, scale=ALPHA)
        for g in range(2):
            bank = P[2 + g]
            for j in range(4):
                dc = g * 4 + j
                mk.transpose(pv(bank, 0, 128, j * 128, (j + 1) * 128), r[:, dc * 128:(dc + 1) * 128], C.ident)
            src = pv(bank, 0, 128, 0, 512, ("p (c n) -> p c n", dict(c=4)))
            mk.copy("act", xh[:, g * 4:(g + 1) * 4, :], src)
            mk.tt("dve", xl[:, g * 4:(g + 1) * 4, :], src, xh[:, g * 4:(g + 1) * 4, :], ALU.subtract)
        n = 0
        for kc in range(8):
            for (xa, wa) in [(xh, 0), (xl, 0), (xh, 1)]:
                mk.mm(pv(P[4], 0, 128, 0, 16), xa[:, kc, :], rw[:, kc, wa, :], start=(n == 0), stop=(n == 23))
                n += 1
        mk.op("dve", lambda e: e.tensor_reduce(m[:, 0:1].ap, P[4].h[:, 0:16], AX.X, ALU.max), reads=[pv(P[4], 0, 128, 0, 16)], writes=[m[:, 0:1]])
        mk.ts("dve", m[:, 1:2], m[:, 0:1], -1.0, None, ALU.mult)
        mk.act(m[:, 8:24], pv(P[4], 0, 128, 0, 16), AF.Exp, bias=m[:, 1:2], accum=m[:, 2:3])
        mk.recip(m[:, 3:4], m[:, 2:3])
        mk.ts("dve", m[:, 24:40], m[:, 8:24], m[:, 3:4], None, ALU.mult)
        mk.transpose(pv(P[4], 0, 16, 128, 256), m[:, 24:40], C.ident)
        mk.copy("act", affT[0:16, tk], pv(P[4], 0, 16, 128, 256))

    def stage_c(tt):
        tk = slice(tt * 128, (tt + 1) * 128)
        xh, acc, ra = x1Th[tt % 3], accs[tt % 2], ras[tt % 2]
        for half in range(2):
            hs = slice(half * 512, (half + 1) * 512)
            for kc in range(8):
                mk.mm(pv(P[5], 0, 128, 0, 512), xh[:, kc, :], wg[:, kc, hs], start=(kc == 0), stop=(kc == 7))
            for kc in range(2):
                mk.mm(pv(P[6], 0, 128, 0, 512), pT[:, kc, tk], wp[:, kc, hs], start=(kc == 0), stop=(kc == 1))
            mk.tt("dve", gbs[half].all(), pv(P[5], 0, 128, 0, 512), lnb[:, 4, hs], ALU.add)
            mk.act(sgs[half].all(), gbs[half].all(), AF.Sigmoid)
            mk.tt("dve", gbs[half].all(), sgs[half].all(), pv(P[6], 0, 128, 0, 512), ALU.mult)
            mk.tt("pool", acc[:, hs], ra[:, hs], gbs[half].all(), ALU.add)
        mk.dma("sp", C.acc[L][s][tk, :], acc.all())

    SK1, SK2 = getattr(C, "skew", (1, 2))
    for i in range(NT + SK2):
        if i < NT:
            stage_a(i)
        if 0 <= i - SK1 < NT:
            stage_b(i - SK1)
        if 0 <= i - SK2 < NT:
            stage_c(i - SK2)
    stt_.close()
    mk.barrier()
    if C.dbg and s == 0:
        dump(C, "affT%d" % L, affT.all(), [16, S], F32)
    work = mk.sb("po_work", [16, S], F32, stk)
    gat = mk.sb("po_gat", [16, 256], F32, stk)
    idxu = mk.sb("po_idxu", [16, 256], U32, stk)
    idxf = mk.sb("po_idxf", [16, 256], F32, stk)
    mk.copy("dve", work.all(), affT.all())
    for r_ in range(32):
        g8 = gat[:, r_ * 8:(r_ + 1) * 8]
        i8 = idxu[:, r_ * 8:(r_ + 1) * 8]
        mk.op("dve", lambda e, g8=g8: e.max(out=g8.ap, in_=work.all().ap), reads=[work.all()], writes=[g8])
        mk.op("dve", lambda e, g8=g8, i8=i8: e.max_index(out=i8.ap, in_max=g8.ap, in_values=work.all().ap),
              reads=[g8, work.all()], writes=[i8], strict=[g8])
        mk.op("dve", lambda e, g8=g8: e.match_replace(out=work.all().ap, in_to_replace=g8.ap, in_values=work.all().ap, imm_value=-1.0),
              reads=[g8, work.all()], writes=[work.all()], strict=[g8])
    mk.copy("dve", idxf.all(), idxu.all())
    for half in range(2):
        cs = slice(half * 128, (half + 1) * 128)
        mk.transpose(pv(P[0], 0, 128, half * 16, (half + 1) * 16), idxf[0:16, cs], C.cm[0:16, 0, 0:16])
        mk.transpose(pv(P[0], 0, 128, 32 + half * 16, 32 + (half + 1) * 16), gat[0:16, cs], C.cm[0:16, 0, 0:16])
    mk.copy("dve", idx_s.all(), pv(P[0], 0, 128, 0, 32, ("p (h e) -> p h e", dict(h=2))))
    mk.copy("dve", gate_s.all(), pv(P[0], 0, 128, 32, 64, ("p (h e) -> p h e", dict(h=2))))
    if C.dbg and s == 0:
        dump(C, "idx%d" % L, idx_s.all(), [128, 2, 16], I32)
        dump(C, "gate%d" % L, gate_s.all(), [128, 2, 16], F32)
    stk.close()
    mk.barrier()


def phase_experts(mk, C, L, seqs, idx_s, gate_s):
    stk = ExitStack()
    P = C.P
    ns = len(seqs)
    NSL = ns * 256
    NWB = 3
    wb = [[mk.sb("ex_w%d_%d" % (j, i), [128, 8, D], BF16, stk) for j in range(3)] for i in range(NWB)]
    xg = [mk.sb("ex_xg%d" % i, [128, D], BF16, stk) for i in range(4)]
    xgT = [mk.sb("ex_xgT%d" % i, [128, 8, NSL], BF16, stk) for i in range(1)] * 2
    hidT = mk.sb("ex_hidT", [128, 8, NSL], BF16, stk)
    sl = [mk.sb("ex_sl%d" % i, [128, NSL], F32, stk) for i in range(2)]
    ye = [mk.sb("ex_ye%d" % i, [128, D], F32, stk) for i in range(2)]
    wsrcs = [C.w1, C.w3, C.w2]

    def load_w(e):
        for j in range(3):
            mk.dma("pool", wb[e % NWB][j].all(), None, reads=[], in_ap=wsrc(wsrcs[j].h[L, e], 0, D, 0, D))

    load_w(0)
    load_w(1)
    gi = 0
    yi = 0
    for e in range(16):
        w1b, w3b, w2b = wb[e % NWB]
        xt_ = xgT[e % 2]
        for si, s in enumerate(seqs):
            for half in range(2):
                g = xg[gi % 4]
                gi += 1
                idxv = idx_s[s][:, half, e:e + 1]
                src = C.xrows[L][s]
                mk.dma("pool", g.all(), src.all(), reads=[src.all(), idxv],
                       indirect=lambda eng, g=g, src=src, idxv=idxv: eng.indirect_dma_start(
                           out=g.all().ap, out_offset=None, in_=src.h[:, :],
                           in_offset=bass.IndirectOffsetOnAxis(ap=idxv.ap, axis=0)))
                for dc in range(8):
                    mk.transpose(pv(C.PH, 0, 128, dc * 128, (dc + 1) * 128), g[:, dc * 128:(dc + 1) * 128], C.identb)
                sl0 = (si * 2 + half) * 128
                mk.copy("act" if half else "dve", xt_[:, :, sl0:sl0 + 128],
                        pv(C.PH, 0, 128, 0, 1024, ("p (c n) -> p c n", dict(c=8))))
        cut(C, 21)
        if e + 2 < 16:
            load_w(e + 2)
        for fc in range(8):
            fs = slice(fc * 128, (fc + 1) * 128)
            b1, b3 = (P[0], P[1]) if fc % 2 == 0 else (P[2], P[3])
            for kc in range(8):
                mk.mm(pv(b1, 0, 128, 0, NSL), w1b[:, kc, fs], xt_[:, kc, :], start=(kc == 0), stop=(kc == 7))
            for kc in range(8):
                mk.mm(pv(b3, 0, 128, 0, NSL), w3b[:, kc, fs], xt_[:, kc, :], start=(kc == 0), stop=(kc == 7))
            mk.act(sl[fc % 2].all(), pv(b1, 0, 128, 0, NSL), AF.Silu)
            mk.tt("dve", hidT[:, fc, :], sl[fc % 2].all(), pv(b3, 0, 128, 0, NSL), ALU.mult)
        cut(C, 22)
        for si, s in enumerate(seqs):
            for half in range(2):
                sl0 = (si * 2 + half) * 128
                y = ye[yi % 2]
                yi += 1
                gv = gate_s[s][:, half, e:e + 1]
                for h2 in range(2):
                    bank = P[4 + h2]
                    for fc in range(8):
                        mk.mm(pv(bank, 0, 128, 0, 512), hidT[:, fc, sl0:sl0 + 128], w2b[:, fc, h2 * 512:(h2 + 1) * 512], start=(fc == 0), stop=(fc == 7))
                    if h2 == 0:
                        mk.act(y[:, 0:512], pv(bank, 0, 128, 0, 512), AF.Copy, scale=gv)
                    else:
                        mk.ts("dve", y[:, 512:1024], pv(bank, 0, 128, 0, 512), gv, None, ALU.mult)
                idxv = idx_s[s][:, half, e:e + 1]
                dst = C.acc[L][s]
                mk.dma("pool", dst.all(), y.all(), reads=[y.all(), idxv], writes=[dst.all()],
                       indirect=lambda eng, y=y, dst=dst, idxv=idxv: eng.indirect_dma_start(
                           out=dst.h[:, :], out_offset=bass.IndirectOffsetOnAxis(ap=idxv.ap, axis=0),
                           in_=y.all().ap, in_offset=None, compute_op=ALU.add, oob_is_err=True))
                cut(C, 23)
        cut(C, 24)
    stk.close()
    mk.barrier()


def odd_mixer(mk, C, s, xT, mixT):
    stk = ExitStack()
    P = C.P
    wi = C.w_in_odd.h
    wk = mk.sb("gq_wk", [128, 8, 4, 2, 64], BF16, stk)
    wks = mk.sb("gq_wks", [128, 8, 4, 2, 64], BF16, stk)
    wv = mk.sb("gq_wv", [128, 8, 256], BF16, stk)
    gn = mk.sb("gq_gn", [128, 4], F32, stk)
    rope = mk.sb("gq_rope", [128, 2, S], F32, stk)
    for dup in range(2):
        for kc in range(8):
            mk.dma("pool", wk[:, kc, :, dup, :], None, reads=[],
                   in_ap=wi[kc * 128:(kc + 1) * 128, 1024:1280].rearrange("p (k e) -> p k e", k=4))
            mk.dma("pool", wks[:, kc, :, dup, :], None, reads=[],
                   in_ap=C.w_k_sw.h[kc * 128:(kc + 1) * 128, :].rearrange("p (k e) -> p k e", k=4))
    mk.dma("pool", wv.all(), None, reads=[], in_ap=wsrc(wi, 0, D, 1280, 1536))
    mk.dma("sp", gn.all(), None, reads=[], in_ap=C.gqa_n.h[:, :])
    mk.dma("sp", rope.all(), None, reads=[], in_ap=C.rope_c.h[:, :, :].rearrange("k p n -> p k n"))
    KT = mk.sb("gq_KT", [128, 4, S], BF16, stk)
    VA = mk.sb("gq_VA", [128, NT, 4, 128], BF16, stk)
    QT = mk.sb("gq_QT", [128, 4, S], BF16, stk)
    mk.memset("pool", VA.all(), 1.0)

    def norm_rope(tb, lhsA, lhsB, g0, dst, tmps, it):
        t0, t1 = tb * 512, (tb + 1) * 512
        sq, sqh, rs, ta, tb_ = tmps
        A = P[0] if it % 2 == 0 else P[3]
        B = P[1] if it % 2 == 0 else P[4]
        Sb = P[2] if it % 2 == 0 else P[5]
        for kc in range(8):
            mk.mm(pv(A, 0, 128, 0, 512), lhsA(kc), xT[:, kc, t0:t1], start=(kc == 0), stop=(kc == 7))
        for kc in range(8):
            mk.mm(pv(B, 0, 128, 0, 512), lhsB(kc), xT[:, kc, t0:t1], start=(kc == 0), stop=(kc == 7))
        mk.act(sq.all(), pv(A, 0, 128, 0, 512), AF.Square)
        mk.copy("pool", sqh[:, 0, :], sq.all())
        mk.tt("pool", sqh[:, 1, :], sq.all(), sqh[:, 0, :], ALU.subtract)
        for hl in range(2):
            mk.mm(pv(Sb, 0, 128, 0, 512), C.bd_onesb, sqh[:, hl, :], start=(hl == 0), stop=(hl == 1))
        mk.act(rs.all(), pv(Sb, 0, 128, 0, 512), AF.Sqrt, bias=C.eps[:, 0:1], scale=1.0 / 64)
        mk.recip(rs.all(), rs.all())
        mk.stt("dve", ta.all(), pv(A, 0, 128, 0, 512), gn[:, g0:g0 + 1], rope[:, 0, t0:t1], ALU.mult, ALU.mult)
        mk.stt("dve", tb_.all(), pv(B, 0, 128, 0, 512), gn[:, g0 + 1:g0 + 2], rope[:, 1, t0:t1], ALU.mult, ALU.mult)
        mk.tt("pool", ta.all(), ta.all(), tb_.all(), ALU.add)
        mk.tt("dve", dst, ta.all(), rs.all(), ALU.mult)

    st2 = ExitStack()
    tmps = [(mk.sb("gq_sq%d" % i, [128, 512], F32, st2), mk.sb("gq_sqh%d" % i, [128, 2, 512], BF16, st2),
             mk.sb("gq_rs%d" % i, [128, 512], F32, st2), mk.sb("gq_ta%d" % i, [128, 512], F32, st2),
             mk.sb("gq_tb%d" % i, [128, 512], F32, st2)) for i in range(2)]
    it = 0
    for tb in range(4):
        for kv in range(4):
            norm_rope(tb, lambda kc, kv=kv: V(wk, wk.h[:, kc, kv, :, :].rearrange("p a e -> p (a e)"), 0, wk.size),
                      lambda kc, kv=kv: V(wks, wks.h[:, kc, kv, :, :].rearrange("p a e -> p (a e)"), 0, wks.size),
                      2, KT[:, kv, tb * 512:(tb + 1) * 512], tmps[it % 2], it)
            it += 1
        for j in range(4):
            tt = tb * 4 + j
            for kc in range(8):
                mk.mm(pv(P[6], 0, 128, 0, 256), xT[:, kc, tt * 128:(tt + 1) * 128], wv[:, kc, :], start=(kc == 0), stop=(kc == 7))
            mk.copy("act", VA[:, tt, :, 0:64], pv(P[6], 0, 128, 0, 256, ("p (h e) -> p h e", dict(h=4))))
    for hg in range(2):
        st3 = ExitStack()
        wq = mk.sb("gq_wq", [128, 8, 512], BF16, st3)
        wqs = mk.sb("gq_wqs", [128, 8, 512], BF16, st3)
        mk.dma("pool", wq.all(), None, reads=[], in_ap=wsrc(wi, 0, D, hg * 512, (hg + 1) * 512))
        mk.dma("pool", wqs.all(), None, reads=[], in_ap=wsrc(C.w_q_sw.h, 0, D, hg * 512, (hg + 1) * 512))
        for tb in range(4):
            for c in range(4):
                norm_rope(tb, lambda kc, c=c: wq[:, kc, c * 128:(c + 1) * 128], lambda kc, c=c: wqs[:, kc, c * 128:(c + 1) * 128],
                          0, QT[:, c, tb * 512:(tb + 1) * 512], tmps[it % 2], it)
                it += 1
        if C.dbg and s == 0:
            dump(C, "gQT%d" % hg, QT.all(), [128, 4, S], BF16)
            if hg == 0:
                dump(C, "gKT", KT.all(), [128, 4, S], BF16)
                dump(C, "gVA", VA.all(), [128, NT, 4, 128], BF16)
        attention_pairs(mk, C, QT, KT, VA, mixT, 4, 64.0 ** -0.5,
                        lambda p: (p, (hg * 8 + 2 * p) // 4, (hg * 8 + 2 * p) // 4, (hg * 8 + 2 * p) // 2), st3)
        st3.close()
        mk.barrier()
    st2.close()
    stk.close()
    mk.barrier()


def phase_final(mk, C):
    stk = ExitStack()
    lnb = mk.sb("fin_lnb", [128, 2, D], F32, stk)
    mk.dma("sp", lnb.all(), None, reads=[], in_ap=C.lnbc.h[1, 2:4].rearrange("k p n -> p k n"))
    xts = [mk.sb("fin_x%d" % i, [128, D], F32, stk) for i in range(3)]
    tmp = mk.sb("fin_tmp", [128, D], F32, stk)
    sts = [mk.sb("fin_st%d" % i, [128, 16], F32, stk) for i in range(2)]
    i = 0
    for s in range(2):
        for tt in range(NT):
            xt = xts[i % 3]
            mk.dma("sp", xt.all(), C.acc[1][s][tt * 128:(tt + 1) * 128, :])
            layer_norm_tile(mk, C, xt, lnb[:, 0, :], lnb[:, 1, :], tmp, sts[i % 2])
            mk.dma("sp", C.out[s, tt * 128:(tt + 1) * 128, :], xt.all())
            i += 1
    stk.close()
    mk.barrier()


def build_program(nc, dbg=False, layers=(0, 1)):
    mk = MK(nc)
    C = setup(mk, dbg=dbg)
    idx_s = [mk.sb("idx_s%d" % i, [128, 2, 16], I32) for i in range(2)]
    gate_s = [mk.sb("gate_s%d" % i, [128, 2, 16], F32) for i in range(2)]
    for L in layers:
        for s in range(2):
            stk = ExitStack()
            xT = mk.sb("xT", [128, 8, S], BF16, stk)
            mixT = mk.sb("mixT", [128, 8, S], BF16, stk)
            st = ExitStack()
            if L == 1:
                lnb2 = mk.sb("pro_lnb", [128, 2, D], F32, st)
                mk.dma("sp", lnb2.all(), None, reads=[], in_ap=C.lnbc.h[0, 2:4].rearrange("k p n -> p k n"))
                phase_prologue(mk, C, L, s, xT, lnb2[:, 0, :], lnb2[:, 1, :], stk=st)
            else:
                phase_prologue(mk, C, L, s, xT, stk=st)
            st.close()
            mk.barrier()
            if L == 0:
                even_mla(mk, C, s, xT, mixT)
                even_gla(mk, C, s, xT, mixT)
            else:
                odd_mixer(mk, C, s, xT, mixT)
            phase_post(mk, C, L, s, mixT, idx_s[s], gate_s[s])
            stk.close()
            mk.barrier()
        phase_experts(mk, C, L, [0, 1], idx_s, gate_s)
    if 1 in layers:
        phase_final(mk, C)
    mk.finish()
    return mk, C


def _axial_rope_np(seq, rot_dim):
    rows = seq // 64
    row = np.repeat(np.arange(rows, dtype=np.float32), 64)
    col = np.tile(np.arange(64, dtype=np.float32), rows)
    axis_dim = rot_dim // 2
    inv = (np.float32(10000.0) ** (-np.arange(0, axis_dim, 2, dtype=np.float32) / np.float32(axis_dim))).astype(np.float32)
    ang = np.concatenate([row[:, None] * inv, col[:, None] * inv], axis=-1).astype(np.float32)
    return np.cos(ang).astype(np.float32), np.sin(ang).astype(np.float32)


def host_consts():
    f = np.float32
    idx = np.arange(128)
    same = (idx[:, None] // 64) == (idx[None, :] // 64)
    ident = np.eye(128, dtype=f)
    ones = np.ones((128, 128), f)
    bd = same.astype(f)
    s_, t_ = idx[:, None], idx[None, :]
    triF = (same & (s_ <= t_)).astype(f)
    triS = (same & (s_ > t_)).astype(f)
    triB = (same & (s_ >= t_)).astype(f)
    triSB = (same & (s_ < t_)).astype(f)
    cmat = np.stack([ident, ones, bd, triF, triS, triB, triSB]).astype(f)
    cmask = np.stack([np.tile(triF, (1, 4)), np.tile(triB, (1, 4))]).astype(f)
    ca, sa = _axial_rope_np(S, 32)
    rope_a = np.zeros((2, 128, S), f)
    rope_a[0, 64:96] = np.concatenate([ca, ca], 1).T
    rope_a[1, 64:96] = np.concatenate([-sa, sa], 1).T
    cc, sc = _axial_rope_np(S, 64)
    CC = np.concatenate([cc, cc], 1).T
    SS = np.concatenate([-sc, sc], 1).T
    rope_c = np.stack([np.concatenate([CC, CC], 0), np.concatenate([SS, SS], 0)]).astype(f)
    return dict(cmat=cmat, cmask=cmask, rope_a=rope_a, rope_c=rope_c)


def host_shared(I):
    f = np.float32
    g = lambda k: np.asarray(I[k], dtype=f)
    sh = dict(host_consts())
    we = g("w_in_even")[0]
    sh["w_in_even"] = np.ascontiguousarray(we)
    perm32 = (np.arange(32) + 16) % 32
    sh["w_kpe_sw"] = np.ascontiguousarray(we[:, 384:416][:, perm32])
    wuq = g("w_uq")[0]
    sh["w_uq"] = np.ascontiguousarray(wuq)
    sh["w_uq_sw"] = np.ascontiguousarray(wuq.reshape(256, 8, 96)[:, :, 64:][:, :, perm32].reshape(256, 256))
    sh["w_ukv"] = np.ascontiguousarray(g("w_ukv")[0])
    sh["mla_qn"] = np.ascontiguousarray(g("mla_q_norm")[0].reshape(2, 128).T)
    sh["mla_kvn"] = np.ascontiguousarray(g("mla_kv_norm")[0].reshape(128, 1))
    gw = np.zeros((2, 17, 256), f)
    gw[0, :16] = g("gla_gate_w_fwd")[0]
    gw[0, 16] = g("gla_gate_b_fwd")[0]
    gw[1, :16] = g("gla_gate_w_bwd")[0]
    gw[1, 16] = g("gla_gate_b_bwd")[0]
    sh["gla_gw"] = gw
    sh["gla_norm_bc"] = np.ascontiguousarray(np.broadcast_to(np.tile(g("gla_norm")[0], 4)[None, :], (128, 512)))
    wo_ = g("w_in_odd")[0]
    sh["w_in_odd"] = np.ascontiguousarray(wo_)
    perm64 = (np.arange(64) + 32) % 64
    sh["w_q_sw"] = np.ascontiguousarray(wo_[:, :1024].reshape(D, 16, 64)[:, :, perm64].reshape(D, 1024))
    sh["w_k_sw"] = np.ascontiguousarray(wo_[:, 1024:1280].reshape(D, 4, 64)[:, :, perm64].reshape(D, 256))
    qn, kn = g("gqa_q_norm")[0], g("gqa_k_norm")[0]
    sh["gqa_n"] = np.ascontiguousarray(np.stack([np.tile(qn, 2), np.tile(qn[perm64], 2), np.tile(kn, 2), np.tile(kn[perm64], 2)], 1))
    sh["w_o"] = g("w_o")
    lnbc = np.zeros((2, 5, 128, D), f)
    for L in range(2):
        for j, k in enumerate(["ln1_g", "ln1_b", "ln2_g", "ln2_b", "ple_gate_b"]):
            lnbc[L, j] = np.broadcast_to(g(k)[L][None, :], (128, D))
    sh["lnbc"] = lnbc
    for k in ["router_w", "w1", "w3", "w2", "ple_gate_w", "ple_w"]:
        sh[k] = g(k)
    return sh


def host_percore(I, c):
    f = np.float32
    x = np.asarray(I["x"], dtype=f)
    p = np.asarray(I["p"], dtype=f)
    return dict(x_in=np.ascontiguousarray(x[2 * c:2 * c + 2]),
                pT=np.ascontiguousarray(p[:, 2 * c:2 * c + 2].transpose(0, 1, 3, 2)))


def kernel(**inputs):
    sh = host_shared(inputs)
    in_maps = []
    for c in range(8):
        m = dict(sh)
        m.update(host_percore(inputs, c))
        in_maps.append(m)
    nc = bass.Bass("TRN2", target_bir_lowering=False)
    build_program(nc)
    res = run_bass_kernel_spmd(nc, in_maps, core_ids=list(range(8)))
    out = np.concatenate([np.asarray(r["out"]) for r in res.results], axis=0)
    return np.ascontiguousarray(out.astype(np.float32))
```

```python
from concourse.bass_utils import run_bass_kernel_spmd
import numpy as np
from contextlib import ExitStack
import concourse.bass as bass
import concourse.mybir as mybir

F32 = mybir.dt.float32
BF16 = mybir.dt.bfloat16
U32 = mybir.dt.uint32
I32 = mybir.dt.int32
ALU = mybir.AluOpType
AF = mybir.ActivationFunctionType
AX = mybir.AxisListType

EPOCH = 16000
N_DMA_SEMS = 20
SAME_ENGINE_SYNC = {"pe": False, "act": True, "dve": True, "pool": True, "sp": False}


class V:
    __slots__ = ("tile", "ap", "lo", "hi")

    def __init__(self, tile, ap, lo, hi):
        self.tile, self.ap, self.lo, self.hi = tile, ap, lo, hi


class T:
    def __init__(self, mk, name, handle, shape, kind):
        self.mk, self.name, self.h, self.shape, self.kind = mk, name, handle, list(shape), kind
        fd = self.shape[1:] if kind != "dram" else self.shape
        self.fshape = fd
        st = [1] * len(fd)
        for i in range(len(fd) - 2, -1, -1):
            st[i] = st[i + 1] * fd[i + 1]
        self.fstride = st
        self.size = int(np.prod(fd)) if fd else 1
        self.recs = {}

    def __getitem__(self, idx):
        if not isinstance(idx, tuple):
            idx = (idx,)
        idx = tuple(idx) + (slice(None),) * (len(self.shape) - len(idx))
        ap = self.h[idx]
        fidx = idx[1:] if self.kind != "dram" else idx
        if self.kind == "psum":
            return V(self, ap, 0, self.size)
        lo = 0
        hi = 0
        for i, ix in enumerate(fidx):
            n = self.fshape[i]
            if isinstance(ix, slice):
                a, b, stp = ix.indices(n)
                assert stp == 1 and b > a, (self.name, idx)
                lo += a * self.fstride[i]
                hi += (b - 1) * self.fstride[i]
            else:
                assert 0 <= ix < n, (self.name, idx)
                lo += ix * self.fstride[i]
                hi += ix * self.fstride[i]
        return V(self, ap, lo, hi + 1)

    def all(self):
        return self[tuple(slice(None) for _ in self.shape)]


class MK:
    def __init__(self, nc):
        self.nc = nc
        self.es = ExitStack()
        self.eng = {"pe": nc.tensor, "act": nc.scalar, "dve": nc.vector, "pool": nc.gpsimd, "sp": nc.sync}
        self.count = {e: 0 for e in self.eng}
        self.esems = {e: [] for e in self.eng}
        self.seen = {e: {} for e in self.eng}
        self.semh = {}
        self.dma_sems = []
        self.dma_val = []
        self.dma_rr = 0
        self.n_wait = 0
        self.n_inst = 0
        for i in range(N_DMA_SEMS):
            s = self.es.enter_context(nc.semaphore("dq%d" % i))
            key = ("dma", i)
            self.semh[key] = s
            self.dma_sems.append(key)
            self.dma_val.append(0)
        self.phase_stack = []

    def sb(self, name, shape, dtype, stack=None):
        self.uid = getattr(self, "uid", 0) + 1
        name = "sb%d_%s" % (self.uid, name)
        h = (stack or self.es).enter_context(self.nc.sbuf_tensor(name, list(shape), dtype))
        return T(self, name, h, shape, "sbuf")

    def ps(self, name, shape, dtype, stack=None):
        h = (stack or self.es).enter_context(self.nc.psum_tensor(name, list(shape), dtype))
        return T(self, name, h, shape, "psum")

    def dram(self, name, shape, dtype, kind="Internal"):
        h = self.nc.dram_tensor(name, list(shape), dtype, kind=kind)
        return T(self, name, h, shape, "dram")

    def _eng_token(self, e):
        c = self.count[e]
        ep = c // EPOCH
        while len(self.esems[e]) <= ep:
            s = self.es.enter_context(self.nc.semaphore("e_%s_%d" % (e, len(self.esems[e]))))
            key = ("eng", e, len(self.esems[e]))
            self.semh[key] = s
            self.esems[e].append(key)
        return self.esems[e][ep], (c % EPOCH) + 1

    def _wait(self, e, key, val):
        if self.seen[e].get(key, 0) >= val:
            return
        self.eng[e].wait_ge(self.semh[key], val)
        self.seen[e][key] = val
        self.n_wait += 1

    def _deps(self, e, reads, writes, strict=()):
        deps = {}
        sdeps = {}
        for v in strict:
            for (k, key, lo, hi), val in v.tile.recs.items():
                if k == "w" and lo < v.hi and v.lo < hi:
                    if sdeps.get(key, 0) < val:
                        sdeps[key] = val
        for key, val in sdeps.items():
            self._wait(e, key, val)

        def add(key, val):
            if deps.get(key, 0) < val:
                deps[key] = val

        for v in reads:
            for (k, key, lo, hi), val in v.tile.recs.items():
                if k == "w" and lo < v.hi and v.lo < hi:
                    add(key, val)
        for v in writes:
            for (k, key, lo, hi), val in v.tile.recs.items():
                if lo < v.hi and v.lo < hi:
                    add(key, val)
        for key, val in deps.items():
            if key[0] == "eng" and key[1] == e and not SAME_ENGINE_SYNC[e]:
                continue
            self._wait(e, key, val)

    def _record(self, key, val, reads, writes):
        for v in reads:
            v.tile.recs[("r", key, v.lo, v.hi)] = val
        for v in writes:
            recs = v.tile.recs
            dead = [r for r in recs if v.lo <= r[2] and r[3] <= v.hi]
            for r in dead:
                del recs[r]
            recs[("w", key, v.lo, v.hi)] = val

    def op(self, e, build, reads=(), writes=(), strict=()):
        reads = [r for r in reads if r is not None]
        writes = list(writes) + [r for r in reads if r.tile.kind == "psum"]
        reads = [r for r in reads if r.tile.kind != "psum"]
        self._deps(e, reads, writes, strict)
        key, val = self._eng_token(e)
        ins = build(self.eng[e])
        ins.then_inc(self.semh[key], 1)
        self.count[e] += 1
        self.n_inst += 1
        self._record(key, val, reads, writes)
        return ins

    def dma(self, q, out, in_, reads=None, writes=None, indirect=None, in_ap=None, out_ap=None, **kw):
        reads = [in_] if reads is None else reads
        writes = [out] if writes is None else writes
        in_ap = in_.ap if in_ap is None else in_ap
        out_ap = out.ap if out_ap is None else out_ap
        i = self.dma_rr
        self.dma_rr = (self.dma_rr + 1) % N_DMA_SEMS
        key = self.dma_sems[i]
        self._wait(q, key, self.dma_val[i])
        self._deps(q, reads, writes)
        self.dma_val[i] += 16
        val = self.dma_val[i]
        if indirect is not None:
            ins = indirect(self.eng[q])
        else:
            ins = self.eng[q].dma_start(out=out_ap, in_=in_ap, **kw)
        ins.then_inc(self.semh[key], 16)
        self.n_inst += 1
        self._record(key, val, reads, writes)
        return ins

    def barrier(self):
        toks = []
        for e in self.eng:
            c = self.count[e]
            if c == 0:
                continue
            ep = (c - 1) // EPOCH
            toks.append((self.esems[e][ep], ((c - 1) % EPOCH) + 1))
        for i, key in enumerate(self.dma_sems):
            if self.dma_val[i]:
                toks.append((key, self.dma_val[i]))
        for e in self.eng:
            for key, val in toks:
                if key[0] == "eng" and key[1] == e:
                    continue
                self._wait(e, key, val)

    def finish(self):
        self.barrier()
        self.es.close()

    def mm(self, out, lhsT, rhs, start=True, stop=True, **kw):
        return self.op("pe", lambda g: g.matmul(out.ap, lhsT.ap, rhs.ap, start=start, stop=stop, **kw),
                       reads=[lhsT, rhs], writes=[out])

    def transpose(self, out, in_, ident):
        return self.op("pe", lambda g: g.transpose(out.ap, in_.ap, ident.ap), reads=[in_, ident], writes=[out])

    def act(self, out, in_, func, bias=None, scale=None, accum=None, e="act"):
        kw = {}
        rd = [in_]
        wr = [out]
        sr = []
        if bias is not None:
            if isinstance(bias, V):
                kw["bias"] = bias.ap
                rd.append(bias)
                sr.append(bias)
            else:
                kw["bias"] = bias
        if scale is not None:
            if isinstance(scale, V):
                kw["scale"] = scale.ap
                rd.append(scale)
                sr.append(scale)
            else:
                kw["scale"] = scale
        if accum is not None:
            kw["accum_out"] = accum.ap
            wr.append(accum)
        return self.op(e, lambda g: g.activation(out.ap, in_.ap, func, **kw), reads=rd, writes=wr, strict=sr)

    def tt(self, e, out, a, b, op):
        return self.op(e, lambda g: g.tensor_tensor(out.ap, a.ap, b.ap, op), reads=[a, b], writes=[out])

    def ts(self, e, out, a, s1, s2, op0, op1=None, accum=None):
        rd = [a]
        wr = [out]
        sr = []
        s1a = s1
        s2a = s2
        if isinstance(s1, V):
            rd.append(s1)
            sr.append(s1)
            s1a = s1.ap
        if isinstance(s2, V):
            rd.append(s2)
            sr.append(s2)
            s2a = s2.ap
        kw = {}
        if op1 is not None:
            kw["op1"] = op1
        if accum is not None:
            kw["accum_out"] = accum.ap
            wr.append(accum)
        return self.op(e, lambda g: g.tensor_scalar(out.ap, a.ap, s1a, s2a, op0, **kw), reads=rd, writes=wr, strict=sr)

    def stt(self, e, out, a, s, b, op0, op1):
        rd = [a, b]
        sr = []
        sa = s
        if isinstance(s, V):
            rd.append(s)
            sr.append(s)
            sa = s.ap
        return self.op(e, lambda g: g.scalar_tensor_tensor(out.ap, a.ap, sa, b.ap, op0, op1), reads=rd, writes=[out], strict=sr)

    def copy(self, e, out, in_):
        if e == "act":
            return self.op(e, lambda g: g.copy(out.ap, in_.ap), reads=[in_], writes=[out])
        return self.op(e, lambda g: g.tensor_copy(out.ap, in_.ap), reads=[in_], writes=[out])

    def memset(self, e, out, val):
        return self.op(e, lambda g: g.memset(out.ap, val), reads=[], writes=[out])

    def recip(self, out, in_):
        return self.op("dve", lambda g: g.reciprocal(out.ap, in_.ap), reads=[in_], writes=[out])


import numpy as np
import math
from contextlib import ExitStack

S = 2048
D = 1024
NT = 16
ALPHA = (2.0 * 2) ** 0.25
EPS = 1e-6


class Rot:
    def __init__(self, items):
        self.items, self.i = list(items), 0

    def __call__(self):
        x = self.items[self.i % len(self.items)]
        self.i += 1
        return x


def pv(bank, p0, p1, c0, c1, shape=None):
    ap = bank.h[p0:p1, c0:c1]
    if shape is not None:
        ap = ap.rearrange(shape[0], **shape[1])
    return V(bank, ap, 0, bank.size)


def wsrc(h, r0, r1, c0, c1):
    return h[r0:r1, c0:c1].rearrange("(c p) n -> p c n", p=128)


class Ctx:
    pass


class Cut(Exception):
    pass


def cut(C, n):
    if getattr(C, "cut", None) == n:
        raise Cut()


def setup(mk, dbg=False):
    C = Ctx()
    C.mk = mk
    C.dbg = dbg
    C.dumps = {}
    d = lambda n, s, t=F32: mk.dram(n, s, t, kind="ExternalInput")
    C.x_in = d("x_in", [2, S, D])
    C.pT = d("pT", [2, 2, 256, S])
    C.w_in_even = d("w_in_even", [D, 1984])
    C.w_kpe_sw = d("w_kpe_sw", [D, 32])
    C.w_uq = d("w_uq", [256, 768])
    C.w_uq_sw = d("w_uq_sw", [256, 256])
    C.w_ukv = d("w_ukv", [128, 1024])
    C.mla_qn = d("mla_qn", [128, 2])
    C.mla_kvn = d("mla_kvn", [128, 1])
    C.gla_gw = d("gla_gw", [2, 17, 256])
    C.gla_norm_bc = d("gla_norm_bc", [128, 512])
    C.w_in_odd = d("w_in_odd", [D, 1536])
    C.w_q_sw = d("w_q_sw", [D, 1024])
    C.w_k_sw = d("w_k_sw", [D, 256])
    C.gqa_n = d("gqa_n", [128, 4])
    C.w_o = d("w_o", [2, D, D])
    C.lnbc = d("lnbc", [2, 5, 128, D])
    C.router_w = d("router_w", [2, D, 16])
    C.w1 = d("w1", [2, 16, D, D])
    C.w3 = d("w3", [2, 16, D, D])
    C.w2 = d("w2", [2, 16, D, D])
    C.ple_gate_w = d("ple_gate_w", [2, D, D])
    C.ple_w = d("ple_w", [2, 256, D])
    C.cmat = d("cmat", [7, 128, 128])
    C.cmask = d("cmask", [2, 128, 512])
    C.rope_a = d("rope_a", [2, 128, S])
    C.rope_c = d("rope_c", [2, 128, S])
    C.out = mk.dram("out", [2, S, D], F32, kind="ExternalOutput")
    C.acc = [[mk.dram("acc_%d_%d" % (L, s), [S, D], F32) for s in range(2)] for L in range(2)]
    C.xrows = [[mk.dram("xrows_%d_%d" % (L, s), [S, D], BF16) for s in range(2)] for L in range(2)]
    C.xln = [mk.dram("xln_%d" % s, [S, D], F32) for s in range(2)]
    C.affd = mk.dram("affd", [2, 16, S], F32)
    C.P = [mk.ps("pb%d" % i, [128, 512], F32) for i in range(7)]
    C.PH = mk.ps("pbh", [128, 1024], BF16)
    names = ["ident", "ones", "bd_ones", "triF", "triS", "triB", "triSB"]
    C.cm = mk.sb("cm", [128, 7, 128], F32)
    mk.dma("sp", C.cm.all(), None, reads=[], in_ap=C.cmat.h[:, :, :].rearrange("k p n -> p k n"))
    for i, n in enumerate(names):
        setattr(C, n, C.cm[:, i, :])
    C.cmb = mk.sb("cmb", [128, 7, 128], BF16)
    mk.copy("dve", C.cmb.all(), C.cm.all())
    for i, n in enumerate(names):
        setattr(C, n + "b", C.cmb[:, i, :])
    C.maskt = mk.sb("maskt", [128, 2, 512], BF16)
    mk.dma("pool", C.maskt.all(), None, reads=[], in_ap=C.cmask.h[:, :, :].rearrange("k p n -> p k n"))
    C.eps = mk.sb("eps", [128, 1], F32)
    mk.memset("dve", C.eps.all(), EPS)
    return C


def dump(C, name, view, shape, dtype):
    if not C.dbg:
        return
    mk = C.mk
    t = mk.dram("dbg_" + name, shape, dtype, kind="ExternalOutput")
    mk.dma("sp", t.all(), view)
    C.dumps[name] = "dbg_" + name


def layer_norm_tile(mk, C, xt, g, b, tmp, st):
    FM = 512
    for j in range(2):
        mk.op("dve", lambda e, j=j: e.bn_stats(st[:, j * 6:(j + 1) * 6].ap, xt[:, j * FM:(j + 1) * FM].ap),
              reads=[xt[:, j * FM:(j + 1) * FM]], writes=[st[:, j * 6:(j + 1) * 6]])
    mk.op("dve", lambda e: e.bn_aggr(st[:, 12:14].ap, st.h[:, 0:12].rearrange("p (n k) -> p n k", k=6)),
          reads=[st[:, 0:12]], writes=[st[:, 12:14]])
    mk.act(st[:, 14:15], st[:, 13:14], AF.Sqrt, bias=C.eps[:, 0:1])
    mk.recip(st[:, 15:16], st[:, 14:15])
    mk.ts("dve", tmp.all(), xt.all(), st[:, 12:13], st[:, 15:16], ALU.subtract, ALU.mult)
    mk.tt("pool", tmp.all(), tmp.all(), g, ALU.mult)
    mk.tt("dve", xt.all(), tmp.all(), b, ALU.add)


def transpose_tile_to_xT(mk, C, xt, xT, tt, banks, ei):
    for g in range(2):
        bank = banks()
        for j in range(4):
            dc = g * 4 + j
            mk.transpose(pv(bank, 0, 128, j * 128, (j + 1) * 128), xt[:, dc * 128:(dc + 1) * 128], C.ident)
        src = pv(bank, 0, 128, 0, 512, ("p (c n) -> p c n", dict(c=4)))
        dst = xT[:, g * 4:(g + 1) * 4, tt * 128:(tt + 1) * 128]
        mk.copy(ei(), dst, src)


def phase_prologue(mk, C, L, s, xT, ln_g=None, ln_b=None, stk=None):
    src = C.x_in if L == 0 else C.acc[0][s]
    xts = [mk.sb("pro_x%d" % i, [128, D], F32, stk) for i in range(3)]
    tmp = mk.sb("pro_tmp", [128, D], F32, stk)
    sts = [mk.sb("pro_st%d" % i, [128, 16], F32, stk) for i in range(2)]
    banks = Rot([C.P[0], C.P[1]])
    ei = Rot(["act", "dve"])
    for tt in range(NT):
        xt = xts[tt % 3]
        if L == 0:
            mk.dma("sp", xt.all(), src[s, tt * 128:(tt + 1) * 128, :], reads=[])
        else:
            mk.dma("sp", xt.all(), src[tt * 128:(tt + 1) * 128, :])
            layer_norm_tile(mk, C, xt, ln_g, ln_b, tmp, sts[tt % 2])
            mk.dma("sp", C.xln[s][tt * 128:(tt + 1) * 128, :], xt.all())
        transpose_tile_to_xT(mk, C, xt, xT, tt, banks, ei)


def proj_fm(mk, out_ps, w, c0, c1, xT, t0, t1, nk=8):
    for kc in range(nk):
        mk.mm(out_ps, w[:, kc, c0:c1], xT[:, kc, t0:t1], start=(kc == 0), stop=(kc == nk - 1))


def rope_evac(mk, C, psA, psB, rope, p0, p1, t0, t1, dst, tmpa, tmpb):
    mk.tt("dve", tmpa[p0:p1, 0:t1 - t0], psA, rope[p0:p1, 0, t0:t1], ALU.mult)
    mk.tt("dve", tmpb[p0:p1, 0:t1 - t0], psB, rope[p0:p1, 1, t0:t1], ALU.mult)
    mk.tt("pool", dst, tmpa[p0:p1, 0:t1 - t0], tmpb[p0:p1, 0:t1 - t0], ALU.add)


def attention(mk, C, QT, KT, VA, mixT, nheads, kdim, scale, kmap, stk):
    pts = [mk.sb("att_pt%d" % i, [128, 512], BF16, stk) for i in range(3)]
    rec = [mk.sb("att_rec%d" % i, [128, 512], F32, stk) for i in range(2)]
    sbk = [C.P[0], C.P[1], C.P[2]]
    obk = [C.P[3], C.P[4]]
    steps = [(h, qb, kt) for h in range(nheads) for qb in range(4) for kt in range(NT)]

    def qk(i):
        h, qb, kt = steps[i]
        qb0, qs, ks, vs, oc, ob0 = kmap(h)
        mk.mm(pv(sbk[i % 3], 0, 128, 0, 512), KT[qb0:qb0 + kdim, ks, kt * 128:(kt + 1) * 128],
              QT[qb0:qb0 + kdim, qs, qb * 512:(qb + 1) * 512])

    qk(0)
    for i, (h, qb, kt) in enumerate(steps):
        qb0, qs, ks, vs, oc, ob0 = kmap(h)
        if i + 1 < len(steps):
            qk(i + 1)
        obank = obk[(h * 4 + qb) % 2]
        pt = pts[i % 3]
        mk.act(pt.all(), pv(sbk[i % 3], 0, 128, 0, 512), AF.Exp, scale=scale)
        mk.mm(pv(obank, 0, 128, 0, 512), VA[:, kt, vs, :], pt.all(), start=(kt == 0), stop=(kt == NT - 1))
        if kt == NT - 1:
            r = rec[(h * 4 + qb) % 2]
            mk.recip(r[ob0:ob0 + 64, :], pv(obank, 64, 128, 0, 512))
            mk.tt("dve", mixT[ob0:ob0 + 64, oc, qb * 512:(qb + 1) * 512], pv(obank, 0, 64, 0, 512),
                  r[ob0:ob0 + 64, :], ALU.mult)


def attention_pairs(mk, C, QT, KT, VA, mixT, npairs, scale, pmap, stk):
    pts = [mk.sb("atp_pt%d" % i, [128, 512], BF16, stk) for i in range(4)]
    rec = [mk.sb("atp_rec%d" % i, [128, 512], F32, stk) for i in range(2)]
    SA = [C.P[0], C.P[1]]
    SB = [C.P[2], C.P[3]]
    OB = [C.P[4], C.P[5]]
    steps = [(p, qb, kt) for p in range(npairs) for qb in range(4) for kt in range(NT)]

    def qk(i):
        p, qb, kt = steps[i]
        qs, ks, vs, oc = pmap(p)
        ksl = slice(kt * 128, (kt + 1) * 128)
        qsl = slice(qb * 512, (qb + 1) * 512)
        mk.mm(pv(SA[i % 2], 0, 128, 0, 512), KT[0:64, ks, ksl], QT[0:64, qs, qsl])
        mk.mm(pv(SB[i % 2], 0, 128, 0, 512), KT[64:128, ks, ksl], QT[64:128, qs, qsl])

    qk(0)
    for i, (p, qb, kt) in enumerate(steps):
        qs, ks, vs, oc = pmap(p)
        if i + 1 < len(steps):
            qk(i + 1)
        ptA, ptB = pts[(2 * i) % 4], pts[(2 * i + 1) % 4]
        mk.act(ptA.all(), pv(SA[i % 2], 0, 128, 0, 512), AF.Exp, scale=scale)
        mk.act(ptB.all(), pv(SB[i % 2], 0, 128, 0, 512), AF.Exp, scale=scale)
        mk.mm(pv(OB[0], 0, 128, 0, 512), VA[:, kt, vs, :], ptA.all(), start=(kt == 0), stop=(kt == NT - 1))
        mk.mm(pv(OB[1], 0, 128, 0, 512), VA[:, kt, vs, :], ptB.all(), start=(kt == 0), stop=(kt == NT - 1))
        if kt == NT - 1:
            for j, ob0 in enumerate([0, 64]):
                r = rec[j]
                mk.recip(r[ob0:ob0 + 64, :], pv(OB[j], 64, 128, 0, 512))
                mk.tt("dve", mixT[ob0:ob0 + 64, oc, qb * 512:(qb + 1) * 512], pv(OB[j], 0, 64, 0, 512),
                      r[ob0:ob0 + 64, :], ALU.mult)


def even_mla(mk, C, s, xT, mixT):
    stk = ExitStack()
    we = C.w_in_even.h
    wq = mk.sb("mla_wq", [128, 8, 256], BF16, stk)
    wkv = mk.sb("mla_wkv", [128, 8, 128], BF16, stk)
    wkpe = mk.sb("mla_wkpe", [128, 8, 2, 96], BF16, stk)
    wuq = mk.sb("mla_wuq", [128, 2, 768], BF16, stk)
    wuqs = mk.sb("mla_wuqs", [128, 2, 8, 96], BF16, stk)
    wukv = mk.sb("mla_wukv", [128, 1024], BF16, stk)
    qn = mk.sb("mla_qn", [128, 2], F32, stk)
    kvn = mk.sb("mla_kvn", [128, 1], F32, stk)
    rope = mk.sb("mla_rope", [128, 2, S], F32, stk)
    mk.dma("pool", wq.all(), None, reads=[], in_ap=wsrc(we, 0, D, 0, 256))
    mk.dma("pool", wkv.all(), None, reads=[], in_ap=wsrc(we, 0, D, 256, 384))
    mk.memset("pool", wkpe.all(), 0.0)
    mk.memset("pool", wuqs.all(), 0.0)
    mk.dma("pool", wkpe[:, :, 0, 64:96], None, reads=[], in_ap=wsrc(we, 0, D, 384, 416))
    mk.dma("pool", wkpe[:, :, 1, 64:96], None, reads=[], in_ap=wsrc(C.w_kpe_sw.h, 0, D, 0, 32))
    mk.dma("pool", wuq.all(), None, reads=[], in_ap=wsrc(C.w_uq.h, 0, 256, 0, 768))
    for kc in range(2):
        mk.dma("pool", wuqs[:, kc, :, 64:96], None, reads=[],
               in_ap=C.w_uq_sw.h[kc * 128:(kc + 1) * 128, :].rearrange("p (h e) -> p h e", h=8))
    mk.dma("pool", wukv.all(), None, reads=[], in_ap=C.w_ukv.h[:, :])
    mk.dma("sp", qn.all(), None, reads=[], in_ap=C.mla_qn.h[:, :])
    mk.dma("sp", kvn.all(), None, reads=[], in_ap=C.mla_kvn.h[:, :])
    mk.dma("sp", rope[64:96, :, :], None, reads=[], in_ap=C.rope_a.h[:, 64:96, :].rearrange("k p n -> p k n"))

    cqn = mk.sb("mla_cqn", [128, 2, S], BF16, stk)
    ckvn = mk.sb("mla_ckvn", [128, S], BF16, stk)
    kper = mk.sb("mla_kper", [128, S], BF16, stk)
    QT = mk.sb("mla_QT", [128, 4, S], BF16, stk)
    KT = mk.sb("mla_KT", [128, 4, S], BF16, stk)
    VA = mk.sb("mla_VA", [128, NT, 4, 128], BF16, stk)
    st2 = ExitStack()
    cqf = [mk.sb("mla_cqf%d" % i, [128, 512], F32, st2) for i in range(3)]
    sq = [mk.sb("mla_sq%d" % i, [128, 512], F32, st2) for i in range(3)]
    rs = [mk.sb("mla_rs%d" % i, [128, 512], F32, st2) for i in range(2)]
    sqh = [mk.sb("mla_sqh%d" % i, [128, 2, 512], BF16, st2) for i in range(3)]
    tmpa = mk.sb("mla_tmpa", [128, 512], F32, st2)
    tmpb = mk.sb("mla_tmpb", [128, 512], F32, st2)
    P = C.P
    cut(C, 1)
    for tb in range(4):
        t0, t1 = tb * 512, (tb + 1) * 512
        groups = [(wq, 0, 128, qn[:, 0:1], cqn[:, 0, t0:t1]),
                  (wq, 128, 256, qn[:, 1:2], cqn[:, 1, t0:t1]),
                  (wkv, 0, 128, kvn[:, 0:1], ckvn[:, t0:t1])]
        for gi, (w, c0, c1, gain, dst) in enumerate(groups):
            bank = P[gi]
            proj_fm(mk, pv(bank, 0, 128, 0, 512), w, c0, c1, xT, t0, t1)
            mk.act(sq[gi].all(), pv(bank, 0, 128, 0, 512), AF.Square)
            mk.copy("dve", cqf[gi].all(), pv(bank, 0, 128, 0, 512))
        cut(C, 2)
        for gi in range(3):
            mk.copy("pool", sqh[gi][:, 0, :], sq[gi].all())
            mk.tt("pool", sqh[gi][:, 1, :], sq[gi].all(), sqh[gi][:, 0, :], ALU.subtract)
        for j, (gi, hl) in enumerate([(0, 0), (0, 1), (1, 0), (1, 1)]):
            mk.mm(pv(P[3], 0, 128, 0, 512), C.onesb, sqh[gi][:, hl, :], start=(j == 0), stop=(j == 3))
        for hl in range(2):
            mk.mm(pv(P[4], 0, 128, 0, 512), C.onesb, sqh[2][:, hl, :], start=(hl == 0), stop=(hl == 1))
        cut(C, 3)
        mk.act(rs[0].all(), pv(P[3], 0, 128, 0, 512), AF.Sqrt, bias=C.eps[:, 0:1], scale=1.0 / 256)
        mk.recip(rs[0].all(), rs[0].all())
        mk.act(rs[1].all(), pv(P[4], 0, 128, 0, 512), AF.Sqrt, bias=C.eps[:, 0:1], scale=1.0 / 128)
        mk.recip(rs[1].all(), rs[1].all())
        for gi, (w, c0, c1, gain, dst) in enumerate(groups):
            mk.stt("dve", dst, cqf[gi].all(), gain, rs[0 if gi < 2 else 1].all(), ALU.mult, ALU.mult)
        cut(C, 4)
        for kc in range(8):
            mk.mm(pv(P[5], 0, 96, 0, 512), wkpe[:, kc, 0, :], xT[:, kc, t0:t1], start=(kc == 0), stop=(kc == 7))
        for kc in range(8):
            mk.mm(pv(P[6], 0, 96, 0, 512), wkpe[:, kc, 1, :], xT[:, kc, t0:t1], start=(kc == 0), stop=(kc == 7))
        cut(C, 5)
        mk.tt("dve", tmpa[64:96, :], pv(P[5], 64, 96, 0, 512), rope[64:96, 0, t0:t1], ALU.mult)
        mk.tt("dve", tmpb[64:96, :], pv(P[6], 64, 96, 0, 512), rope[64:96, 1, t0:t1], ALU.mult)
        mk.tt("pool", kper[64:96, t0:t1], tmpa[64:96, :], tmpb[64:96, :], ALU.add)
        cut(C, 6)
    st2.close()
    mk.barrier()
    st2 = ExitStack()
    tmpa = mk.sb("mla_tmpa2", [128, 512], F32, st2)
    tmpb = mk.sb("mla_tmpb2", [128, 512], F32, st2)
    tmpc = mk.sb("mla_tmpc2", [128, 512], F32, st2)
    tmpd = mk.sb("mla_tmpd2", [128, 512], F32, st2)
    stop = getattr(C, "stop", 99)
    if stop <= 1:
        dump(C, "cqn", cqn.all(), [128, 2, S], BF16)
        dump(C, "kper", kper[64:96, :], [32, S], BF16)
    for hg in range(2 if stop > 1 else 0):
        mk.memset("pool", VA.all(), 1.0)
        for tb in range(4):
            t0, t1 = tb * 512, (tb + 1) * 512
            ab = Rot([P[0], P[1]])
            bb = Rot([P[2], P[5]])
            kb = Rot([P[3], P[4]])
            for hl in range(4):
                h = hg * 4 + hl
                A = ab()
                B = bb()
                K = kb()
                for kc in range(2):
                    mk.mm(pv(A, 0, 96, 0, 512), wuq[:, kc, h * 96:(h + 1) * 96], cqn[:, kc, t0:t1], start=(kc == 0), stop=(kc == 1))
                for kc in range(2):
                    mk.mm(pv(B, 0, 96, 0, 512), wuqs[:, kc, h, :], cqn[:, kc, t0:t1], start=(kc == 0), stop=(kc == 1))
                mk.mm(pv(K, 0, 64, 0, 512), wukv[:, h * 128:h * 128 + 64], ckvn[:, t0:t1])
                mk.copy("act", QT[0:64, hl, t0:t1], pv(A, 0, 64, 0, 512))
                ta, tb_ = (tmpa, tmpb) if hl % 2 == 0 else (tmpc, tmpd)
                mk.tt("dve", ta[64:96, :], pv(A, 64, 96, 0, 512), rope[64:96, 0, t0:t1], ALU.mult)
                mk.tt("dve", tb_[64:96, :], pv(B, 64, 96, 0, 512), rope[64:96, 1, t0:t1], ALU.mult)
                mk.tt("pool", QT[64:96, hl, t0:t1], ta[64:96, :], tb_[64:96, :], ALU.add)
                mk.copy("act", KT[0:64, hl, t0:t1], pv(K, 0, 64, 0, 512))
                mk.copy("pool", KT[64:96, hl, t0:t1], kper[64:96, t0:t1])
            for j in range(4):
                tt = tb * 4 + j
                bank = P[6]
                rhs_ap = wukv.h[:, hg * 512:(hg + 1) * 512].rearrange("p (h e) -> p h e", h=4)[:, :, 64:128]
                rhs = V(wukv, rhs_ap, hg * 512, (hg + 1) * 512)
                mk.mm(pv(bank, 0, 128, 0, 256, ("p (h e) -> p h e", dict(h=4))), ckvn[:, tt * 128:(tt + 1) * 128], rhs)
                mk.copy("act" if j % 2 else "dve", VA[:, tt, :, 0:64], pv(bank, 0, 128, 0, 256, ("p (h e) -> p h e", dict(h=4))))
        if C.dbg and s == 0:
            dump(C, "QT%d" % hg, QT.all(), [128, 4, S], BF16)
            dump(C, "KT%d" % hg, KT.all(), [128, 4, S], BF16)
            dump(C, "VA%d" % hg, VA.all(), [128, NT, 4, 128], BF16)
        if stop <= 2:
            continue
        st3 = ExitStack()
        attention(mk, C, QT, KT, VA, mixT, 4, 96, 96.0 ** -0.5,
                  lambda hl: (0, hl, hl, hl, (hg * 4 + hl) // 2, (hl % 2) * 64), st3)
        st3.close()
    st2.close()
    stk.close()
    mk.barrier()


def even_gla(mk, C, s, xT, mixT):
    stk = ExitStack()
    we = C.w_in_even.h
    P = C.P
    wfm = mk.sb("gla_wfm", [128, 8, 512], BF16, stk)
    wtm = mk.sb("gla_wtm", [128, 8, 1280], BF16, stk)
    wlr = mk.sb("gla_wlr", [128, 8, 32], BF16, stk)
    gw = mk.sb("gla_gw", [17, 2, 256], BF16, stk)
    gnorm = mk.sb("gla_gnorm", [128, 512], F32, stk)
    one1 = mk.sb("gla_one1", [128, 1], F32, stk)
    mk.memset("dve", one1.all(), 1.0)
    mk.dma("pool", wfm.all(), None, reads=[], in_ap=wsrc(we, 0, D, 416, 928))
    mk.dma("pool", wtm[:, :, 0:768], None, reads=[], in_ap=wsrc(we, 0, D, 672, 1440))
    mk.dma("pool", wtm[:, :, 768:1280], None, reads=[], in_ap=wsrc(we, 0, D, 1472, 1984))
    mk.dma("pool", wlr.all(), None, reads=[], in_ap=wsrc(we, 0, D, 1440, 1472))
    mk.dma("pool", gw.all(), None, reads=[], in_ap=C.gla_gw.h[:, :, :].rearrange("k r n -> r k n"))
    mk.dma("sp", gnorm.all(), None, reads=[], in_ap=C.gla_norm_bc.h[:, :])
    gqT = mk.sb("gla_gqT", [128, 2, S], BF16, stk)
    gkT = mk.sb("gla_gkT", [128, 2, S], BF16, stk)
    gk_tok = mk.sb("gla_gk_tok", [128, NT, 256], BF16, stk)
    gv_tok = mk.sb("gla_gv_tok", [128, NT, 512], BF16, stk)
    o_f = mk.sb("gla_of", [128, NT, 512], F32, stk)
    ei = Rot(["act", "dve"])
    bk = Rot([P[0], P[1], P[2]])
    for tb in range(4):
        t0, t1 = tb * 512, (tb + 1) * 512
        for mc in range(4):
            bank = bk()
            proj_fm(mk, pv(bank, 0, 128, 0, 512), wfm, mc * 128, (mc + 1) * 128, xT, t0, t1)
            dst = (gqT if mc < 2 else gkT)[:, mc % 2, t0:t1]
            mk.copy(ei(), dst, pv(bank, 0, 128, 0, 512))
    for tt in range(NT):
        tk = slice(tt * 128, (tt + 1) * 128)
        b1, b2 = bk(), bk()
        for kc in range(8):
            mk.mm(pv(b1, 0, 128, 0, 256), xT[:, kc, tk], wtm[:, kc, 0:256], start=(kc == 0), stop=(kc == 7))
        for kc in range(8):
            mk.mm(pv(b2, 0, 128, 0, 512), xT[:, kc, tk], wtm[:, kc, 256:768], start=(kc == 0), stop=(kc == 7))
        mk.copy("act", gk_tok[:, tt, :], pv(b1, 0, 128, 0, 256))
        mk.copy("dve", gv_tok[:, tt, :], pv(b2, 0, 128, 0, 512))
    cut(C, 11)
    st2 = ExitStack()
    lrT = [mk.sb("gla_lrT%d" % i, [17, 128], BF16, st2) for i in range(2)]
    for t in lrT:
        mk.memset("dve", t.all(), 1.0)
    ez = [mk.sb("gla_ez%d" % i, [128, 256], F32, st2) for i in range(2)]
    nla = [mk.sb("gla_nla%d" % i, [128, 256], F32, st2) for i in range(2)]
    nlah = [mk.sb("gla_nlah%d" % i, [128, 2, 256], BF16, st2) for i in range(2)]
    E1 = [mk.sb("gla_E1%d" % i, [128, 2, 128], F32, st2) for i in range(3)]
    E2 = [mk.sb("gla_E2%d" % i, [128, 2, 128], F32, st2) for i in range(2)]
    E3 = [mk.sb("gla_E3%d" % i, [128, 256], F32, st2) for i in range(2)]
    ke = [mk.sb("gla_ke%d" % i, [128, 256], BF16, st2) for i in range(2)]
    qgT = [mk.sb("gla_qgT%d" % i, [128, 2, 128], BF16, st2) for i in range(2)]
    kgT = [mk.sb("gla_kgT%d" % i, [128, 2, 128], BF16, st2) for i in range(2)]
    attm = [mk.sb("gla_attm%d" % i, [128, 2, 2, 128], BF16, st2) for i in range(2)]
    Sf = mk.sb("gla_Sf", [128, 2, 128], F32, st2)
    Sb = [mk.sb("gla_Sb%d" % i, [128, 2, 128], BF16, st2) for i in range(4)]
    osum = [mk.sb("gla_osum%d" % i, [128, 512], F32, st2) for i in range(1)] * 2
    osq = mk.sb("gla_osq", [128, 512], F32, st2)
    sg = [mk.sb("gla_sg%d" % i, [128, 512], F32, st2) for i in range(1)] * 2
    og = [mk.sb("gla_og%d" % i, [128, 512], BF16, st2) for i in range(1)] * 2
    stt_ = [mk.sb("gla_st%d" % i, [128, 12], F32, st2) for i in range(2)]
    it = 0
    sbi = 0
    for d in range(2):
        tri = C.triFb if d == 0 else C.triBb
        tris = C.triSb if d == 0 else C.triSBb
        mk.memset("dve", Sf.all(), 0.0)
        mk.memset("dve", Sb[sbi % 4].all(), 0.0)
        tiles = range(NT) if d == 0 else range(NT - 1, -1, -1)
        order = [0, 1] if d == 0 else [1, 0]
        for tt in tiles:
            i2 = it % 2
            it += 1
            tk = slice(tt * 128, (tt + 1) * 128)
            for kc in range(8):
                mk.mm(pv(P[0], 0, 16, 0, 128), wlr[:, kc, d * 16:(d + 1) * 16], xT[:, kc, tk], start=(kc == 0), stop=(kc == 7))
            mk.copy("act", lrT[i2][0:16, :], pv(P[0], 0, 16, 0, 128))
            mk.mm(pv(P[0], 0, 128, 128, 384), lrT[i2][0:17, :], gw[0:17, d, :])
            mk.act(ez[i2].all(), pv(P[0], 0, 128, 128, 384), AF.Exp, scale=-1.0)
            mk.act(nla[i2].all(), ez[i2].all(), AF.Ln, bias=one1[:, 0:1])
            mk.copy("pool", nlah[i2][:, 0, :], nla[i2].all())
            mk.tt("pool", nlah[i2][:, 1, :], nla[i2].all(), nlah[i2][:, 0, :], ALU.subtract)
            cut(C, 12)
            for pc in range(2):
                for hl in range(2):
                    mk.mm(pv(P[1], 0, 128, pc * 128, (pc + 1) * 128), nlah[i2][:, hl, pc * 128:(pc + 1) * 128], tri,
                          start=(hl == 0), stop=(hl == 1))
            for hl in range(2):
                mk.mm(pv(P[1], 0, 128, 256, 512), tris, nlah[i2][:, hl, :], start=(hl == 0), stop=(hl == 1))
            e1 = E1[it % 3]
            cumv = pv(P[1], 0, 128, 0, 256, ("p (c n) -> p c n", dict(c=2)))
            mk.act(e1.all(), cumv, AF.Exp, scale=-1.0 / 16)
            mk.act(E2[i2].all(), cumv, AF.Exp, scale=1.0 / 16)
            mk.act(E3[i2].all(), pv(P[1], 0, 128, 256, 512), AF.Exp, scale=-1.0 / 16)
            mk.tt("pool", ke[i2].all(), gk_tok[:, tt, :], E3[i2].all(), ALU.mult)
            mk.stt("dve", qgT[i2].all(), gqT[:, :, tk], 0.125, e1.all(), ALU.mult, ALU.mult)
            mk.tt("dve", kgT[i2].all(), gkT[:, :, tk], E2[i2].all(), ALU.mult)
            cut(C, 13)
            for h in range(4):
                pc, b0 = h // 2, (h % 2) * 64
                bank = P[2] if h % 2 == 0 else P[5]
                mk.mm(pv(bank, 0, 128, pc * 128, (pc + 1) * 128), kgT[i2][b0:b0 + 64, pc, :], qgT[i2][b0:b0 + 64, pc, :])
            for hp in range(2):
                bank = P[2] if hp == 0 else P[5]
                mk.tt("dve", attm[i2][:, :, hp, :], pv(bank, 0, 128, 0, 256, ("p (a n) -> p a n", dict(a=2))),
                      V(C.maskt, C.maskt.h[:, d, 0:256].rearrange("p (a n) -> p a n", a=2), d * 512, d * 512 + 256), ALU.mult)
            cut(C, 14)
            for c in range(2):
                ubank = P[3] if c == 0 else P[6]
                for h in range(4):
                    pc, j = h // 2, h % 2
                    mk.mm(pv(ubank, j * 64, j * 64 + 64, pc * 128, (pc + 1) * 128), ke[i2][c * 64:(c + 1) * 64, h * 64:(h + 1) * 64],
                          gv_tok[c * 64:(c + 1) * 64, tt, h * 128:(h + 1) * 128])
            cut(C, 15)
            sb_for = {}
            for c in order:
                sb_for[c] = Sb[sbi % 4]
                dcol = c * 64 + (63 if d == 0 else 0)
                ubank = P[3] if c == 0 else P[6]
                for pc in range(2):
                    mk.stt("dve", Sf[:, pc, :], Sf[:, pc, :], e1[:, pc, dcol:dcol + 1], pv(ubank, 0, 128, pc * 128, (pc + 1) * 128), ALU.mult, ALU.add)
                sbi += 1
                mk.copy("pool", Sb[sbi % 4].all(), Sf.all())
            cut(C, 16)
            for h in range(4):
                pc, b0 = h // 2, (h % 2) * 64
                obank = P[4] if h % 2 == 0 else P[1]
                mk.mm(pv(obank, 0, 128, pc * 128, (pc + 1) * 128), attm[i2][:, pc, h % 2, :], gv_tok[:, tt, h * 128:(h + 1) * 128],
                      start=True, stop=False, skip_group_check=True)
                for ci, c in enumerate(order):
                    mk.mm(pv(obank, c * 64, c * 64 + 64, pc * 128, (pc + 1) * 128), qgT[i2][b0:b0 + 64, pc, c * 64:(c + 1) * 64],
                          sb_for[c][b0:b0 + 64, pc, :], start=False, stop=(ci == 1), skip_group_check=True)
            of4 = o_f.h[:, tt, :].rearrange("p (a b e) -> p a b e", a=2, b=2)
            if d == 0:
                for hp in range(2):
                    obank = P[4] if hp == 0 else P[1]
                    mk.copy("act", V(o_f, of4[:, :, hp, :], tt * 512, (tt + 1) * 512),
                            pv(obank, 0, 128, 0, 256, ("p (a e) -> p a e", dict(a=2))))
                cut(C, 17)
                continue
            os_ = osum[i2]
            os4 = os_.h[:, :].rearrange("p (a b e) -> p a b e", a=2, b=2)
            for hp in range(2):
                obank = P[4] if hp == 0 else P[1]
                mk.tt("dve", V(os_, os4[:, :, hp, :], 0, 512), V(o_f, of4[:, :, hp, :], tt * 512, (tt + 1) * 512),
                      pv(obank, 0, 128, 0, 256, ("p (a e) -> p a e", dict(a=2))), ALU.add)
            if C.dbg and s == 0 and getattr(C, "dbg_osum_on", False):
                if "osum" not in C.dumps:
                    C.dbg_osum = mk.dram("dbg_osum", [128, NT, 512], F32, kind="ExternalOutput")
                    C.dumps["osum"] = 1
                mk.dma("sp", C.dbg_osum[:, tt, :], os_.all())
            mk.tt("pool", osq.all(), os_.all(), os_.all(), ALU.mult)
            st = stt_[i2]
            osq3 = V(osq, osq.h[:, :].rearrange("p (h e) -> p h e", h=4), 0, 512)
            mk.op("dve", lambda e, st=st, osq3=osq3: e.tensor_reduce(st[:, 0:4].ap, osq3.ap, AX.X, ALU.add), reads=[osq3], writes=[st[:, 0:4]])
            mk.act(st[:, 4:8], st[:, 0:4], AF.Sqrt, bias=C.eps[:, 0:1], scale=1.0 / 128)
            mk.recip(st[:, 8:12], st[:, 4:8])
            for kc in range(8):
                mk.mm(pv(P[0], 0, 128, 0, 512), xT[:, kc, tk], wtm[:, kc, 768:1280], start=(kc == 0), stop=(kc == 7))
            mk.act(sg[i2].all(), pv(P[0], 0, 128, 0, 512), AF.Silu)
            for h in range(4):
                mk.ts("dve" if h % 2 else "pool", os_[:, h * 128:(h + 1) * 128], os_[:, h * 128:(h + 1) * 128], st[:, 8 + h:9 + h], None, ALU.mult)
            mk.tt("pool", os_.all(), os_.all(), gnorm.all(), ALU.mult)
            mk.tt("dve", og[i2].all(), os_.all(), sg[i2].all(), ALU.mult)
            for h in range(4):
                mk.transpose(pv(C.PH, 0, 128, h * 128, (h + 1) * 128), og[i2][:, h * 128:(h + 1) * 128], C.identb)
            mk.copy("act", mixT[:, 4:8, tk], pv(C.PH, 0, 128, 0, 512, ("p (c n) -> p c n", dict(c=4))))
    if C.dbg and s == 0:
        dump(C, "o_f", o_f.all(), [128, NT, 512], F32)
    st2.close()
    stk.close()
    mk.barrier()


def phase_post(mk, C, L, s, mixT):
    stk = ExitStack()
    P = C.P
    wo = mk.sb("po_wo", [128, 8, D], BF16, stk)
    wg = mk.sb("po_wg", [128, 8, D], BF16, stk)
    wp = mk.sb("po_wp", [128, 2, D], BF16, stk)
    pT = mk.sb("po_pT", [128, 2, S], BF16, stk)
    rwf = mk.sb("po_rwf", [128, 8, 16], F32, stk)
    rw = mk.sb("po_rw", [128, 8, 2, 16], BF16, stk)
    lnb = mk.sb("po_lnb", [128, 5, D], F32, stk)
    affT = mk.sb("po_affT", [16, S], F32, stk)
    ab = 0
    mk.dma("pool", wo.all(), None, reads=[], in_ap=wsrc(C.w_o.h[L], 0, D, 0, D))
    mk.dma("pool", wg.all(), None, reads=[], in_ap=wsrc(C.ple_gate_w.h[L], 0, D, 0, D))
    mk.dma("pool", wp.all(), None, reads=[], in_ap=wsrc(C.ple_w.h[L], 0, 256, 0, D))
    mk.dma("pool", pT.all(), None, reads=[], in_ap=C.pT.h[L, s].rearrange("(c p) n -> p c n", p=128))
    mk.dma("sp", rwf.all(), None, reads=[], in_ap=wsrc(C.router_w.h[L], 0, D, 0, 16))
    mk.dma("sp", lnb.all(), None, reads=[], in_ap=C.lnbc.h[L].rearrange("k p n -> p k n"))
    mk.copy("dve", rw[:, :, 0, :], rwf.all())
    mk.tt("dve", rw[:, :, 1, :], rwf.all(), rw[:, :, 0, :], ALU.subtract)
    stt_ = ExitStack()
    xts = [mk.sb("po_xt%d" % i, [128, D], F32, stt_) for i in range(2)]
    rts = [mk.sb("po_r%d" % i, [128, D], F32, stt_) for i in range(3)]
    ras = [mk.sb("po_ra%d" % i, [128, D], F32, stt_) for i in range(2)]
    x1bs = [mk.sb("po_x1b%d" % i, [128, D], BF16, stt_) for i in range(2)]
    x1Th = [mk.sb("po_x1Th%d" % i, [128, 8, 128], BF16, stt_) for i in range(3)]
    x1Tl = [mk.sb("po_x1Tl%d" % i, [128, 8, 128], BF16, stt_) for i in range(2)]
    accs = [mk.sb("po_acc%d" % i, [128, D], F32, stt_) for i in range(2)]
    tmp = mk.sb("po_tmp", [128, D], F32, stt_)
    gbs = [mk.sb("po_gb%d" % i, [128, 512], F32, stt_) for i in range(2)]
    sgs = [mk.sb("po_sg%d" % i, [128, 512], F32, stt_) for i in range(2)]
    sts = [mk.sb("po_st%d" % i, [128, 20], F32, stt_) for i in range(2)]
    sm = [mk.sb("po_sm%d" % i, [128, 40], F32, stt_) for i in range(2)]
    xsrc = C.x_in if L == 0 else C.xln[s]

    def stage_a(tt):
        tk = slice(tt * 128, (tt + 1) * 128)
        xt, r, st = xts[tt % 2], rts[tt % 3], sts[tt % 2]
        if L == 0:
            mk.dma("sp", xt.all(), xsrc[s, tk, :], reads=[])
        else:
            mk.dma("sp", xt.all(), xsrc[tk, :])
        for half in range(2):
            for kc in range(8):
                mk.mm(pv(P[half], 0, 128, 0, 512), mixT[:, kc, tk], wo[:, kc, half * 512:(half + 1) * 512], start=(kc == 0), stop=(kc == 7))
        for half in range(2):
            hs = slice(half * 512, (half + 1) * 512)
            mk.stt("dve", r[:, hs], xt[:, hs], ALPHA, pv(P[half], 0, 128, 0, 512), ALU.mult, ALU.add)
        for j in range(2):
            mk.op("dve", lambda e, j=j: e.bn_stats(st[:, j * 6:(j + 1) * 6].ap, r[:, j * 512:(j + 1) * 512].ap),
                  reads=[r[:, j * 512:(j + 1) * 512]], writes=[st[:, j * 6:(j + 1) * 6]])
        mk.op("dve", lambda e: e.bn_aggr(st[:, 12:14].ap, st[:, 0:12].ap), reads=[st[:, 0:12]], writes=[st[:, 12:14]])
        mk.act(st[:, 14:15], st[:, 13:14], AF.Sqrt, bias=C.eps[:, 0:1])
        mk.recip(st[:, 15:16], st[:, 14:15])
        mk.stt("dve", st[:, 16:17], st[:, 12:13], -1.0, st[:, 15:16], ALU.mult, ALU.mult)
        mk.act(tmp.all(), r.all(), AF.Identity, bias=st[:, 16:17], scale=st[:, 15:16])
        mk.tt("pool", tmp.all(), tmp.all(), lnb[:, 0, :], ALU.mult)
        mk.tt("pool", r.all(), tmp.all(), lnb[:, 1, :], ALU.add)

    def stage_b(tt):
        tk = slice(tt * 128, (tt + 1) * 128)
        r, x1b, xh, xl, m = rts[tt % 3], x1bs[tt % 2], x1Th[tt % 3], x1Tl[tt % 2], sm[tt % 2]
        mk.copy("act", x1b.all(), r.all())
        mk.dma("sp", C.xrows[L][s][tk, :], x1b.all())
        mk.act(ras[tt % 2].all(), r.all(), AF.Copy, scale=ALPHA)
        for g in range(2):
            bank = P[2 + g]
            for j in range(4):
                dc = g * 4 + j
                mk.transpose(pv(bank, 0, 128, j * 128, (j + 1) * 128), r[:, dc * 128:(dc + 1) * 128], C.ident)
            src = pv(bank, 0, 128, 0, 512, ("p (c n) -> p c n", dict(c=4)))
            mk.copy("act", xh[:, g * 4:(g + 1) * 4, :], src)
            mk.tt("dve", xl[:, g * 4:(g + 1) * 4, :], src, xh[:, g * 4:(g + 1) * 4, :], ALU.subtract)
        n = 0
        for kc in range(8):
            for (xa, wa) in [(xh, 0), (xl, 0), (xh, 1)]:
                mk.mm(pv(P[4], 0, 128, 0, 16), xa[:, kc, :], rw[:, kc, wa, :], start=(n == 0), stop=(n == 23))
                n += 1
        mk.op("dve", lambda e: e.tensor_reduce(m[:, 0:1].ap, P[4].h[:, 0:16], AX.X, ALU.max), reads=[pv(P[4], 0, 128, 0, 16)], writes=[m[:, 0:1]])
        mk.ts("dve", m[:, 1:2], m[:, 0:1], -1.0, None, ALU.mult)
        mk.act(m[:, 8:24], pv(P[4], 0, 128, 0, 16), AF.Exp, bias=m[:, 1:2], accum=m[:, 2:3])
        mk.recip(m[:, 3:4], m[:, 2:3])
        mk.ts("dve", m[:, 24:40], m[:, 8:24], m[:, 3:4], None, ALU.mult)
        mk.transpose(pv(P[4], 0, 16, 128, 256), m[:, 24:40], C.ident)
        mk.copy("act", affT[ab:ab + 16, tk], pv(P[4], 0, 16, 128, 256))

    def stage_c(tt):
        tk = slice(tt * 128, (tt + 1) * 128)
        xh, acc, ra = x1Th[tt % 3], accs[tt % 2], ras[tt % 2]
        for half in range(2):
            hs = slice(half * 512, (half + 1) * 512)
            for kc in range(8):
                mk.mm(pv(P[5], 0, 128, 0, 512), xh[:, kc, :], wg[:, kc, hs], start=(kc == 0), stop=(kc == 7))
            for kc in range(2):
                mk.mm(pv(P[6], 0, 128, 0, 512), pT[:, kc, tk], wp[:, kc, hs], start=(kc == 0), stop=(kc == 1))
            mk.tt("dve", gbs[half].all(), pv(P[5], 0, 128, 0, 512), lnb[:, 4, hs], ALU.add)
            mk.act(sgs[half].all(), gbs[half].all(), AF.Sigmoid)
            mk.tt("dve", gbs[half].all(), sgs[half].all(), pv(P[6], 0, 128, 0, 512), ALU.mult)
            mk.tt("pool", acc[:, hs], ra[:, hs], gbs[half].all(), ALU.add)
        mk.dma("sp", C.acc[L][s][tk, :], acc.all())

    SK1, SK2 = getattr(C, "skew", (1, 2))
    for i in range(NT + SK2):
        if i < NT:
            stage_a(i)
        if 0 <= i - SK1 < NT:
            stage_b(i - SK1)
        if 0 <= i - SK2 < NT:
            stage_c(i - SK2)
    stt_.close()
    mk.barrier()
    mk.dma("sp", C.affd[s], affT.all())
    stk.close()
    mk.barrier()


def phase_topk(mk, C, idx_s, gate_s):
    stk = ExitStack()
    P = C.P
    work = mk.sb("tk_work", [64, S], F32, stk)
    gat = mk.sb("tk_gat", [64, 256], F32, stk)
    idxu = mk.sb("tk_idxu", [64, 256], U32, stk)
    idxf = mk.sb("tk_idxf", [64, 256], F32, stk)
    mk.memset("dve", work.all(), 0.0)
    for s in range(2):
        mk.dma("sp", work[s * 32:s * 32 + 16, :], C.affd[s])
    for r_ in range(32):
        g8 = gat[:, r_ * 8:(r_ + 1) * 8]
        i8 = idxu[:, r_ * 8:(r_ + 1) * 8]
        mk.op("dve", lambda e, g8=g8: e.max(out=g8.ap, in_=work.all().ap), reads=[work.all()], writes=[g8])
        mk.op("dve", lambda e, g8=g8, i8=i8: e.max_index(out=i8.ap, in_max=g8.ap, in_values=work.all().ap),
              reads=[g8, work.all()], writes=[i8], strict=[g8])
        mk.op("dve", lambda e, g8=g8: e.match_replace(out=work.all().ap, in_to_replace=g8.ap, in_values=work.all().ap, imm_value=-1.0),
              reads=[g8, work.all()], writes=[work.all()], strict=[g8])
    mk.copy("dve", idxf.all(), idxu.all())
    for half in range(2):
        cs = slice(half * 128, (half + 1) * 128)
        mk.transpose(pv(P[0], 0, 128, half * 64, (half + 1) * 64), idxf[0:64, cs], C.cm[0:64, 0, 0:64])
        mk.transpose(pv(P[0], 0, 128, 128 + half * 64, 128 + (half + 1) * 64), gat[0:64, cs], C.cm[0:64, 0, 0:64])
    for s in range(2):
        iv = P[0].h[:, 0:128].rearrange("p (h e) -> p h e", h=2)[:, :, s * 32:s * 32 + 16]
        gv = P[0].h[:, 128:256].rearrange("p (h e) -> p h e", h=2)[:, :, s * 32:s * 32 + 16]
        mk.copy("dve", idx_s[s].all(), V(P[0], iv, 0, P[0].size))
        mk.copy("dve", gate_s[s].all(), V(P[0], gv, 0, P[0].size))
    stk.close()
    mk.barrier()


def phase_experts(mk, C, L, seqs, idx_s, gate_s):
    stk = ExitStack()
    P = C.P
    ns = len(seqs)
    NSL = ns * 256
    NWB = 3
    wb = [[mk.sb("ex_w%d_%d" % (j, i), [128, 8, D], BF16, stk) for j in range(3)] for i in range(NWB)]
    xg = [mk.sb("ex_xg%d" % i, [128, D], BF16, stk) for i in range(4)]
    xgT = [mk.sb("ex_xgT%d" % i, [128, 8, NSL], BF16, stk) for i in range(1)] * 2
    hidT = mk.sb("ex_hidT", [128, 8, NSL], BF16, stk)
    sl = [mk.sb("ex_sl%d" % i, [128, NSL], F32, stk) for i in range(2)]
    ye = [mk.sb("ex_ye%d" % i, [128, D], F32, stk) for i in range(2)]
    wsrcs = [C.w1, C.w3, C.w2]

    def load_w(e):
        for j in range(3):
            mk.dma("pool", wb[e % NWB][j].all(), None, reads=[], in_ap=wsrc(wsrcs[j].h[L, e], 0, D, 0, D))

    load_w(0)
    load_w(1)
    gi = 0
    yi = 0
    for e in range(16):
        w1b, w3b, w2b = wb[e % NWB]
        xt_ = xgT[e % 2]
        for si, s in enumerate(seqs):
            for half in range(2):
                g = xg[gi % 4]
                gi += 1
                idxv = idx_s[s][:, half, e:e + 1]
                src = C.xrows[L][s]
                mk.dma("pool", g.all(), src.all(), reads=[src.all(), idxv],
                       indirect=lambda eng, g=g, src=src, idxv=idxv: eng.indirect_dma_start(
                           out=g.all().ap, out_offset=None, in_=src.h[:, :],
                           in_offset=bass.IndirectOffsetOnAxis(ap=idxv.ap, axis=0)))
                for dc in range(8):
                    mk.transpose(pv(C.PH, 0, 128, dc * 128, (dc + 1) * 128), g[:, dc * 128:(dc + 1) * 128], C.identb)
                sl0 = (si * 2 + half) * 128
                mk.copy("act" if half else "dve", xt_[:, :, sl0:sl0 + 128],
                        pv(C.PH, 0, 128, 0, 1024, ("p (c n) -> p c n", dict(c=8))))
        cut(C, 21)
        if e + 2 < 16:
            load_w(e + 2)
        for fc in range(8):
            fs = slice(fc * 128, (fc + 1) * 128)
            b1, b3 = (P[0], P[1]) if fc % 2 == 0 else (P[2], P[3])
            for kc in range(8):
                mk.mm(pv(b1, 0, 128, 0, NSL), w1b[:, kc, fs], xt_[:, kc, :], start=(kc == 0), stop=(kc == 7))
            for kc in range(8):
                mk.mm(pv(b3, 0, 128, 0, NSL), w3b[:, kc, fs], xt_[:, kc, :], start=(kc == 0), stop=(kc == 7))
            mk.act(sl[fc % 2].all(), pv(b1, 0, 128, 0, NSL), AF.Silu)
            mk.tt("dve", hidT[:, fc, :], sl[fc % 2].all(), pv(b3, 0, 128, 0, NSL), ALU.mult)
        cut(C, 22)
        for si, s in enumerate(seqs):
            for half in range(2):
                sl0 = (si * 2 + half) * 128
                y = ye[yi % 2]
                yi += 1
                gv = gate_s[s][:, half, e:e + 1]
                for h2 in range(2):
                    bank = P[4 + h2]
                    for fc in range(8):
                        mk.mm(pv(bank, 0, 128, 0, 512), hidT[:, fc, sl0:sl0 + 128], w2b[:, fc, h2 * 512:(h2 + 1) * 512], start=(fc == 0), stop=(fc == 7))
                    if h2 == 0:
                        mk.act(y[:, 0:512], pv(bank, 0, 128, 0, 512), AF.Copy, scale=gv)
                    else:
                        mk.ts("dve", y[:, 512:1024], pv(bank, 0, 128, 0, 512), gv, None, ALU.mult)
                idxv = idx_s[s][:, half, e:e + 1]
                dst = C.acc[L][s]
                mk.dma("pool", dst.all(), y.all(), reads=[y.all(), idxv], writes=[dst.all()],
                       indirect=lambda eng, y=y, dst=dst, idxv=idxv: eng.indirect_dma_start(
                           out=dst.h[:, :], out_offset=bass.IndirectOffsetOnAxis(ap=idxv.ap, axis=0),
                           in_=y.all().ap, in_offset=None, compute_op=ALU.add, oob_is_err=True))
                cut(C, 23)
        cut(C, 24)
    stk.close()
    mk.barrier()


def odd_mixer(mk, C, s, xT, mixT):
    stk = ExitStack()
    P = C.P
    wi = C.w_in_odd.h
    wk = mk.sb("gq_wk", [128, 8, 4, 2, 64], BF16, stk)
    wks = mk.sb("gq_wks", [128, 8, 4, 2, 64], BF16, stk)
    wv = mk.sb("gq_wv", [128, 8, 256], BF16, stk)
    gn = mk.sb("gq_gn", [128, 4], F32, stk)
    rope = mk.sb("gq_rope", [128, 2, S], F32, stk)
    for dup in range(2):
        for kc in range(8):
            mk.dma("pool", wk[:, kc, :, dup, :], None, reads=[],
                   in_ap=wi[kc * 128:(kc + 1) * 128, 1024:1280].rearrange("p (k e) -> p k e", k=4))
            mk.dma("pool", wks[:, kc, :, dup, :], None, reads=[],
                   in_ap=C.w_k_sw.h[kc * 128:(kc + 1) * 128, :].rearrange("p (k e) -> p k e", k=4))
    mk.dma("pool", wv.all(), None, reads=[], in_ap=wsrc(wi, 0, D, 1280, 1536))
    mk.dma("sp", gn.all(), None, reads=[], in_ap=C.gqa_n.h[:, :])
    mk.dma("sp", rope.all(), None, reads=[], in_ap=C.rope_c.h[:, :, :].rearrange("k p n -> p k n"))
    KT = mk.sb("gq_KT", [128, 4, S], BF16, stk)
    VA = mk.sb("gq_VA", [128, NT, 4, 128], BF16, stk)
    QT = mk.sb("gq_QT", [128, 4, S], BF16, stk)
    mk.memset("pool", VA.all(), 1.0)

    def norm_rope(tb, lhsA, lhsB, g0, dst, tmps, it):
        t0, t1 = tb * 512, (tb + 1) * 512
        sq, sqh, rs, ta, tb_ = tmps
        A = P[0] if it % 2 == 0 else P[3]
        B = P[1] if it % 2 == 0 else P[4]
        Sb = P[2] if it % 2 == 0 else P[5]
        for kc in range(8):
            mk.mm(pv(A, 0, 128, 0, 512), lhsA(kc), xT[:, kc, t0:t1], start=(kc == 0), stop=(kc == 7))
        for kc in range(8):
            mk.mm(pv(B, 0, 128, 0, 512), lhsB(kc), xT[:, kc, t0:t1], start=(kc == 0), stop=(kc == 7))
        mk.act(sq.all(), pv(A, 0, 128, 0, 512), AF.Square)
        mk.copy("pool", sqh[:, 0, :], sq.all())
        mk.tt("pool", sqh[:, 1, :], sq.all(), sqh[:, 0, :], ALU.subtract)
        for hl in range(2):
            mk.mm(pv(Sb, 0, 128, 0, 512), C.bd_onesb, sqh[:, hl, :], start=(hl == 0), stop=(hl == 1))
        mk.act(rs.all(), pv(Sb, 0, 128, 0, 512), AF.Sqrt, bias=C.eps[:, 0:1], scale=1.0 / 64)
        mk.recip(rs.all(), rs.all())
        mk.stt("dve", ta.all(), pv(A, 0, 128, 0, 512), gn[:, g0:g0 + 1], rope[:, 0, t0:t1], ALU.mult, ALU.mult)
        mk.stt("dve", tb_.all(), pv(B, 0, 128, 0, 512), gn[:, g0 + 1:g0 + 2], rope[:, 1, t0:t1], ALU.mult, ALU.mult)
        mk.tt("pool", ta.all(), ta.all(), tb_.all(), ALU.add)
        mk.tt("dve", dst, ta.all(), rs.all(), ALU.mult)

    st2 = ExitStack()
    tmps = [(mk.sb("gq_sq%d" % i, [128, 512], F32, st2), mk.sb("gq_sqh%d" % i, [128, 2, 512], BF16, st2),
             mk.sb("gq_rs%d" % i, [128, 512], F32, st2), mk.sb("gq_ta%d" % i, [128, 512], F32, st2),
             mk.sb("gq_tb%d" % i, [128, 512], F32, st2)) for i in range(2)]
    it = 0
    for tb in range(4):
        for kv in range(4):
            norm_rope(tb, lambda kc, kv=kv: V(wk, wk.h[:, kc, kv, :, :].rearrange("p a e -> p (a e)"), 0, wk.size),
                      lambda kc, kv=kv: V(wks, wks.h[:, kc, kv, :, :].rearrange("p a e -> p (a e)"), 0, wks.size),
                      2, KT[:, kv, tb * 512:(tb + 1) * 512], tmps[it % 2], it)
            it += 1
        for j in range(4):
            tt = tb * 4 + j
            for kc in range(8):
                mk.mm(pv(P[6], 0, 128, 0, 256), xT[:, kc, tt * 128:(tt + 1) * 128], wv[:, kc, :], start=(kc == 0), stop=(kc == 7))
            mk.copy("act", VA[:, tt, :, 0:64], pv(P[6], 0, 128, 0, 256, ("p (h e) -> p h e", dict(h=4))))
    for hg in range(2):
        st3 = ExitStack()
        wq = mk.sb("gq_wq", [128, 8, 512], BF16, st3)
        wqs = mk.sb("gq_wqs", [128, 8, 512], BF16, st3)
        mk.dma("pool", wq.all(), None, reads=[], in_ap=wsrc(wi, 0, D, hg * 512, (hg + 1) * 512))
        mk.dma("pool", wqs.all(), None, reads=[], in_ap=wsrc(C.w_q_sw.h, 0, D, hg * 512, (hg + 1) * 512))
        for tb in range(4):
            for c in range(4):
                norm_rope(tb, lambda kc, c=c: wq[:, kc, c * 128:(c + 1) * 128], lambda kc, c=c: wqs[:, kc, c * 128:(c + 1) * 128],
                          0, QT[:, c, tb * 512:(tb + 1) * 512], tmps[it % 2], it)
                it += 1
        if C.dbg and s == 0:
            dump(C, "gQT%d" % hg, QT.all(), [128, 4, S], BF16)
            if hg == 0:
                dump(C, "gKT", KT.all(), [128, 4, S], BF16)
                dump(C, "gVA", VA.all(), [128, NT, 4, 128], BF16)
        attention_pairs(mk, C, QT, KT, VA, mixT, 4, 64.0 ** -0.5,
                        lambda p: (p, (hg * 8 + 2 * p) // 4, (hg * 8 + 2 * p) // 4, (hg * 8 + 2 * p) // 2), st3)
        st3.close()
        mk.barrier()
    st2.close()
    stk.close()
    mk.barrier()


def phase_final(mk, C):
    stk = ExitStack()
    lnb = mk.sb("fin_lnb", [128, 2, D], F32, stk)
    mk.dma("sp", lnb.all(), None, reads=[], in_ap=C.lnbc.h[1, 2:4].rearrange("k p n -> p k n"))
    xts = [mk.sb("fin_x%d" % i, [128, D], F32, stk) for i in range(3)]
    tmp = mk.sb("fin_tmp", [128, D], F32, stk)
    sts = [mk.sb("fin_st%d" % i, [128, 16], F32, stk) for i in range(2)]
    i = 0
    for s in range(2):
        for tt in range(NT):
            xt = xts[i % 3]
            mk.dma("sp", xt.all(), C.acc[1][s][tt * 128:(tt + 1) * 128, :])
            layer_norm_tile(mk, C, xt, lnb[:, 0, :], lnb[:, 1, :], tmp, sts[i % 2])
            mk.dma("sp", C.out[s, tt * 128:(tt + 1) * 128, :], xt.all())
            i += 1
    stk.close()
    mk.barrier()


def build_program(nc, dbg=False, layers=(0, 1)):
    mk = MK(nc)
    C = setup(mk, dbg=dbg)
    idx_s = [mk.sb("idx_s%d" % i, [128, 2, 16], I32) for i in range(2)]
    gate_s = [mk.sb("gate_s%d" % i, [128, 2, 16], F32) for i in range(2)]
    for L in layers:
        for s in range(2):
            stk = ExitStack()
            xT = mk.sb("xT", [128, 8, S], BF16, stk)
            mixT = mk.sb("mixT", [128, 8, S], BF16, stk)
            st = ExitStack()
            if L == 1:
                lnb2 = mk.sb("pro_lnb", [128, 2, D], F32, st)
                mk.dma("sp", lnb2.all(), None, reads=[], in_ap=C.lnbc.h[0, 2:4].rearrange("k p n -> p k n"))
                phase_prologue(mk, C, L, s, xT, lnb2[:, 0, :], lnb2[:, 1, :], stk=st)
            else:
                phase_prologue(mk, C, L, s, xT, stk=st)
            st.close()
            mk.barrier()
            if L == 0:
                even_mla(mk, C, s, xT, mixT)
                even_gla(mk, C, s, xT, mixT)
            else:
                odd_mixer(mk, C, s, xT, mixT)
            phase_post(mk, C, L, s, mixT)
            stk.close()
            mk.barrier()
        phase_topk(mk, C, idx_s, gate_s)
        phase_experts(mk, C, L, [0, 1], idx_s, gate_s)
    if 1 in layers:
        phase_final(mk, C)
    mk.finish()
    return mk, C


def _axial_rope_np(seq, rot_dim):
    rows = seq // 64
    row = np.repeat(np.arange(rows, dtype=np.float32), 64)
    col = np.tile(np.arange(64, dtype=np.float32), rows)
    axis_dim = rot_dim // 2
    inv = (np.float32(10000.0) ** (-np.arange(0, axis_dim, 2, dtype=np.float32) / np.float32(axis_dim))).astype(np.float32)
    ang = np.concatenate([row[:, None] * inv, col[:, None] * inv], axis=-1).astype(np.float32)
    return np.cos(ang).astype(np.float32), np.sin(ang).astype(np.float32)


def host_consts():
    f = np.float32
    idx = np.arange(128)
    same = (idx[:, None] // 64) == (idx[None, :] // 64)
    ident = np.eye(128, dtype=f)
    ones = np.ones((128, 128), f)
    bd = same.astype(f)
    s_, t_ = idx[:, None], idx[None, :]
    triF = (same & (s_ <= t_)).astype(f)
    triS = (same & (s_ > t_)).astype(f)
    triB = (same & (s_ >= t_)).astype(f)
    triSB = (same & (s_ < t_)).astype(f)
    cmat = np.stack([ident, ones, bd, triF, triS, triB, triSB]).astype(f)
    cmask = np.stack([np.tile(triF, (1, 4)), np.tile(triB, (1, 4))]).astype(f)
    ca, sa = _axial_rope_np(S, 32)
    rope_a = np.zeros((2, 128, S), f)
    rope_a[0, 64:96] = np.concatenate([ca, ca], 1).T
    rope_a[1, 64:96] = np.concatenate([-sa, sa], 1).T
    cc, sc = _axial_rope_np(S, 64)
    CC = np.concatenate([cc, cc], 1).T
    SS = np.concatenate([-sc, sc], 1).T
    rope_c = np.stack([np.concatenate([CC, CC], 0), np.concatenate([SS, SS], 0)]).astype(f)
    return dict(cmat=cmat, cmask=cmask, rope_a=rope_a, rope_c=rope_c)


def host_shared(I):
    f = np.float32
    g = lambda k: np.asarray(I[k], dtype=f)
    sh = dict(host_consts())
    we = g("w_in_even")[0]
    sh["w_in_even"] = np.ascontiguousarray(we)
    perm32 = (np.arange(32) + 16) % 32
    sh["w_kpe_sw"] = np.ascontiguousarray(we[:, 384:416][:, perm32])
    wuq = g("w_uq")[0]
    sh["w_uq"] = np.ascontiguousarray(wuq)
    sh["w_uq_sw"] = np.ascontiguousarray(wuq.reshape(256, 8, 96)[:, :, 64:][:, :, perm32].reshape(256, 256))
    sh["w_ukv"] = np.ascontiguousarray(g("w_ukv")[0])
    sh["mla_qn"] = np.ascontiguousarray(g("mla_q_norm")[0].reshape(2, 128).T)
    sh["mla_kvn"] = np.ascontiguousarray(g("mla_kv_norm")[0].reshape(128, 1))
    gw = np.zeros((2, 17, 256), f)
    gw[0, :16] = g("gla_gate_w_fwd")[0]
    gw[0, 16] = g("gla_gate_b_fwd")[0]
    gw[1, :16] = g("gla_gate_w_bwd")[0]
    gw[1, 16] = g("gla_gate_b_bwd")[0]
    sh["gla_gw"] = gw
    sh["gla_norm_bc"] = np.ascontiguousarray(np.broadcast_to(np.tile(g("gla_norm")[0], 4)[None, :], (128, 512)))
    wo_ = g("w_in_odd")[0]
    sh["w_in_odd"] = np.ascontiguousarray(wo_)
    perm64 = (np.arange(64) + 32) % 64
    sh["w_q_sw"] = np.ascontiguousarray(wo_[:, :1024].reshape(D, 16, 64)[:, :, perm64].reshape(D, 1024))
    sh["w_k_sw"] = np.ascontiguousarray(wo_[:, 1024:1280].reshape(D, 4, 64)[:, :, perm64].reshape(D, 256))
    qn, kn = g("gqa_q_norm")[0], g("gqa_k_norm")[0]
    sh["gqa_n"] = np.ascontiguousarray(np.stack([np.tile(qn, 2), np.tile(qn[perm64], 2), np.tile(kn, 2), np.tile(kn[perm64], 2)], 1))
    sh["w_o"] = g("w_o")
    lnbc = np.zeros((2, 5, 128, D), f)
    for L in range(2):
        for j, k in enumerate(["ln1_g", "ln1_b", "ln2_g", "ln2_b", "ple_gate_b"]):
            lnbc[L, j] = np.broadcast_to(g(k)[L][None, :], (128, D))
    sh["lnbc"] = lnbc
    for k in ["router_w", "w1", "w3", "w2", "ple_gate_w", "ple_w"]:
        sh[k] = g(k)
    return sh


def host_percore(I, c):
    f = np.float32
    x = np.asarray(I["x"], dtype=f)
    p = np.asarray(I["p"], dtype=f)
    return dict(x_in=np.ascontiguousarray(x[2 * c:2 * c + 2]),
                pT=np.ascontiguousarray(p[:, 2 * c:2 * c + 2].transpose(0, 1, 3, 2)))


def kernel(**inputs):
    sh = host_shared(inputs)
    in_maps = []
    for c in range(8):
        m = dict(sh)
        m.update(host_percore(inputs, c))
        in_maps.append(m)
    nc = bass.Bass("TRN2", target_bir_lowering=False)
    build_program(nc)
    res = run_bass_kernel_spmd(nc, in_maps, core_ids=list(range(8)))
    out = np.concatenate([np.asarray(r["out"]) for r in res.results], axis=0)
    return np.ascontiguousarray(out.astype(np.float32))
```

```python
from concourse.bass_utils import run_bass_kernel_spmd
import numpy as np
from contextlib import ExitStack
import concourse.bass as bass
import concourse.mybir as mybir

F32 = mybir.dt.float32
BF16 = mybir.dt.bfloat16
U32 = mybir.dt.uint32
I32 = mybir.dt.int32
ALU = mybir.AluOpType
AF = mybir.ActivationFunctionType
AX = mybir.AxisListType

EPOCH = 16000
N_DMA_SEMS = 20
SAME_ENGINE_SYNC = {"pe": False, "act": True, "dve": True, "pool": True, "sp": False}


class V:
    __slots__ = ("tile", "ap", "lo", "hi")

    def __init__(self, tile, ap, lo, hi):
        self.tile, self.ap, self.lo, self.hi = tile, ap, lo, hi


class T:
    def __init__(self, mk, name, handle, shape, kind):
        self.mk, self.name, self.h, self.shape, self.kind = mk, name, handle, list(shape), kind
        fd = self.shape[1:] if kind != "dram" else self.shape
        self.fshape = fd
        st = [1] * len(fd)
        for i in range(len(fd) - 2, -1, -1):
            st[i] = st[i + 1] * fd[i + 1]
        self.fstride = st
        self.size = int(np.prod(fd)) if fd else 1
        self.recs = {}

    def __getitem__(self, idx):
        if not isinstance(idx, tuple):
            idx = (idx,)
        idx = tuple(idx) + (slice(None),) * (len(self.shape) - len(idx))
        ap = self.h[idx]
        fidx = idx[1:] if self.kind != "dram" else idx
        if self.kind == "psum":
            return V(self, ap, 0, self.size)
        lo = 0
        hi = 0
        for i, ix in enumerate(fidx):
            n = self.fshape[i]
            if isinstance(ix, slice):
                a, b, stp = ix.indices(n)
                assert stp == 1 and b > a, (self.name, idx)
                lo += a * self.fstride[i]
                hi += (b - 1) * self.fstride[i]
            else:
                assert 0 <= ix < n, (self.name, idx)
                lo += ix * self.fstride[i]
                hi += ix * self.fstride[i]
        return V(self, ap, lo, hi + 1)

    def all(self):
        return self[tuple(slice(None) for _ in self.shape)]


class MK:
    def __init__(self, nc):
        self.nc = nc
        self.es = ExitStack()
        self.eng = {"pe": nc.tensor, "act": nc.scalar, "dve": nc.vector, "pool": nc.gpsimd, "sp": nc.sync}
        self.count = {e: 0 for e in self.eng}
        self.esems = {e: [] for e in self.eng}
        self.seen = {e: {} for e in self.eng}
        self.semh = {}
        self.dma_sems = []
        self.dma_val = []
        self.dma_rr = 0
        self.n_wait = 0
        self.n_inst = 0
        for i in range(N_DMA_SEMS):
            s = self.es.enter_context(nc.semaphore("dq%d" % i))
            key = ("dma", i)
            self.semh[key] = s
            self.dma_sems.append(key)
            self.dma_val.append(0)
        self.phase_stack = []

    def sb(self, name, shape, dtype, stack=None):
        self.uid = getattr(self, "uid", 0) + 1
        name = "sb%d_%s" % (self.uid, name)
        h = (stack or self.es).enter_context(self.nc.sbuf_tensor(name, list(shape), dtype))
        return T(self, name, h, shape, "sbuf")

    def ps(self, name, shape, dtype, stack=None):
        h = (stack or self.es).enter_context(self.nc.psum_tensor(name, list(shape), dtype))
        return T(self, name, h, shape, "psum")

    def dram(self, name, shape, dtype, kind="Internal"):
        h = self.nc.dram_tensor(name, list(shape), dtype, kind=kind)
        return T(self, name, h, shape, "dram")

    def _eng_token(self, e):
        c = self.count[e]
        ep = c // EPOCH
        while len(self.esems[e]) <= ep:
            s = self.es.enter_context(self.nc.semaphore("e_%s_%d" % (e, len(self.esems[e]))))
            key = ("eng", e, len(self.esems[e]))
            self.semh[key] = s
            self.esems[e].append(key)
        return self.esems[e][ep], (c % EPOCH) + 1

    def _wait(self, e, key, val):
        if self.seen[e].get(key, 0) >= val:
            return
        self.eng[e].wait_ge(self.semh[key], val)
        self.seen[e][key] = val
        self.n_wait += 1

    def _deps(self, e, reads, writes, strict=()):
        deps = {}
        sdeps = {}
        for v in strict:
            for (k, key, lo, hi), val in v.tile.recs.items():
                if k == "w" and lo < v.hi and v.lo < hi:
                    if sdeps.get(key, 0) < val:
                        sdeps[key] = val
        for key, val in sdeps.items():
            self._wait(e, key, val)

        def add(key, val):
            if deps.get(key, 0) < val:
                deps[key] = val

        for v in reads:
            for (k, key, lo, hi), val in v.tile.recs.items():
                if k == "w" and lo < v.hi and v.lo < hi:
                    add(key, val)
        for v in writes:
            for (k, key, lo, hi), val in v.tile.recs.items():
                if lo < v.hi and v.lo < hi:
                    add(key, val)
        for key, val in deps.items():
            if key[0] == "eng" and key[1] == e and not SAME_ENGINE_SYNC[e]:
                continue
            self._wait(e, key, val)

    def _record(self, key, val, reads, writes):
        for v in reads:
            v.tile.recs[("r", key, v.lo, v.hi)] = val
        for v in writes:
            recs = v.tile.recs
            dead = [r for r in recs if v.lo <= r[2] and r[3] <= v.hi]
            for r in dead:
                del recs[r]
            recs[("w", key, v.lo, v.hi)] = val

    def op(self, e, build, reads=(), writes=(), strict=()):
        reads = [r for r in reads if r is not None]
        writes = list(writes) + [r for r in reads if r.tile.kind == "psum"]
        reads = [r for r in reads if r.tile.kind != "psum"]
        self._deps(e, reads, writes, strict)
        key, val = self._eng_token(e)
        ins = build(self.eng[e])
        ins.then_inc(self.semh[key], 1)
        self.count[e] += 1
        self.n_inst += 1
        self._record(key, val, reads, writes)
        return ins

    def dma(self, q, out, in_, reads=None, writes=None, indirect=None, in_ap=None, out_ap=None, **kw):
        reads = [in_] if reads is None else reads
        writes = [out] if writes is None else writes
        in_ap = in_.ap if in_ap is None else in_ap
        out_ap = out.ap if out_ap is None else out_ap
        i = self.dma_rr
        self.dma_rr = (self.dma_rr + 1) % N_DMA_SEMS
        key = self.dma_sems[i]
        self._wait(q, key, self.dma_val[i])
        self._deps(q, reads, writes)
        self.dma_val[i] += 16
        val = self.dma_val[i]
        if indirect is not None:
            ins = indirect(self.eng[q])
        else:
            ins = self.eng[q].dma_start(out=out_ap, in_=in_ap, **kw)
        ins.then_inc(self.semh[key], 16)
        self.n_inst += 1
        self._record(key, val, reads, writes)
        return ins

    def barrier(self):
        toks = []
        for e in self.eng:
            c = self.count[e]
            if c == 0:
                continue
            ep = (c - 1) // EPOCH
            toks.append((self.esems[e][ep], ((c - 1) % EPOCH) + 1))
        for i, key in enumerate(self.dma_sems):
            if self.dma_val[i]:
                toks.append((key, self.dma_val[i]))
        for e in self.eng:
            for key, val in toks:
                if key[0] == "eng" and key[1] == e:
                    continue
                self._wait(e, key, val)

    def finish(self):
        self.barrier()
        self.es.close()

    def mm(self, out, lhsT, rhs, start=True, stop=True, **kw):
        return self.op("pe", lambda g: g.matmul(out.ap, lhsT.ap, rhs.ap, start=start, stop=stop, **kw),
                       reads=[lhsT, rhs], writes=[out])

    def transpose(self, out, in_, ident):
        return self.op("pe", lambda g: g.transpose(out.ap, in_.ap, ident.ap), reads=[in_, ident], writes=[out])

    def act(self, out, in_, func, bias=None, scale=None, accum=None, e="act"):
        kw = {}
        rd = [in_]
        wr = [out]
        sr = []
        if bias is not None:
            if isinstance(bias, V):
                kw["bias"] = bias.ap
                rd.append(bias)
                sr.append(bias)
            else:
                kw["bias"] = bias
        if scale is not None:
            if isinstance(scale, V):
                kw["scale"] = scale.ap
                rd.append(scale)
                sr.append(scale)
            else:
                kw["scale"] = scale
        if accum is not None:
            kw["accum_out"] = accum.ap
            wr.append(accum)
        return self.op(e, lambda g: g.activation(out.ap, in_.ap, func, **kw), reads=rd, writes=wr, strict=sr)

    def tt(self, e, out, a, b, op):
        return self.op(e, lambda g: g.tensor_tensor(out.ap, a.ap, b.ap, op), reads=[a, b], writes=[out])

    def ts(self, e, out, a, s1, s2, op0, op1=None, accum=None):
        rd = [a]
        wr = [out]
        sr = []
        s1a = s1
        s2a = s2
        if isinstance(s1, V):
            rd.append(s1)
            sr.append(s1)
            s1a = s1.ap
        if isinstance(s2, V):
            rd.append(s2)
            sr.append(s2)
            s2a = s2.ap
        kw = {}
        if op1 is not None:
            kw["op1"] = op1
        if accum is not None:
            kw["accum_out"] = accum.ap
            wr.append(accum)
        return self.op(e, lambda g: g.tensor_scalar(out.ap, a.ap, s1a, s2a, op0, **kw), reads=rd, writes=wr, strict=sr)

    def stt(self, e, out, a, s, b, op0, op1):
        rd = [a, b]
        sr = []
        sa = s
        if isinstance(s, V):
            rd.append(s)
            sr.append(s)
            sa = s.ap
        return self.op(e, lambda g: g.scalar_tensor_tensor(out.ap, a.ap, sa, b.ap, op0, op1), reads=rd, writes=[out], strict=sr)

    def copy(self, e, out, in_):
        if e == "act":
            return self.op(e, lambda g: g.copy(out.ap, in_.ap), reads=[in_], writes=[out])
        return self.op(e, lambda g: g.tensor_copy(out.ap, in_.ap), reads=[in_], writes=[out])

    def memset(self, e, out, val):
        return self.op(e, lambda g: g.memset(out.ap, val), reads=[], writes=[out])

    def recip(self, out, in_):
        return self.op("dve", lambda g: g.reciprocal(out.ap, in_.ap), reads=[in_], writes=[out])


import numpy as np
import math
from contextlib import ExitStack

S = 2048
D = 1024
NT = 16
ALPHA = (2.0 * 2) ** 0.25
EPS = 1e-6


class Rot:
    def __init__(self, items):
        self.items, self.i = list(items), 0

    def __call__(self):
        x = self.items[self.i % len(self.items)]
        self.i += 1
        return x


def pv(bank, p0, p1, c0, c1, shape=None):
    ap = bank.h[p0:p1, c0:c1]
    if shape is not None:
        ap = ap.rearrange(shape[0], **shape[1])
    return V(bank, ap, 0, bank.size)


def wsrc(h, r0, r1, c0, c1):
    return h[r0:r1, c0:c1].rearrange("(c p) n -> p c n", p=128)


class Ctx:
    pass


class Cut(Exception):
    pass


def cut(C, n):
    if getattr(C, "cut", None) == n:
        raise Cut()


def setup(mk, dbg=False):
    C = Ctx()
    C.mk = mk
    C.dbg = dbg
    C.dumps = {}
    d = lambda n, s, t=F32: mk.dram(n, s, t, kind="ExternalInput")
    C.x_in = d("x_in", [2, S, D])
    C.pT = d("pT", [2, 2, 256, S])
    C.w_in_even = d("w_in_even", [D, 1984])
    C.w_kpe_sw = d("w_kpe_sw", [D, 32])
    C.w_uq = d("w_uq", [256, 768])
    C.w_uq_sw = d("w_uq_sw", [256, 256])
    C.w_ukv = d("w_ukv", [128, 1024])
    C.mla_qn = d("mla_qn", [128, 2])
    C.mla_kvn = d("mla_kvn", [128, 1])
    C.gla_gw = d("gla_gw", [2, 17, 256])
    C.gla_norm_bc = d("gla_norm_bc", [128, 512])
    C.w_in_odd = d("w_in_odd", [D, 1536])
    C.w_q_sw = d("w_q_sw", [D, 1024])
    C.w_k_sw = d("w_k_sw", [D, 256])
    C.gqa_n = d("gqa_n", [128, 4])
    C.w_o = d("w_o", [2, D, D])
    C.lnbc = d("lnbc", [2, 5, 128, D])
    C.router_w = d("router_w", [2, D, 16])
    C.w1 = d("w1", [2, 16, D, D])
    C.w3 = d("w3", [2, 16, D, D])
    C.w2 = d("w2", [2, 16, D, D])
    C.ple_gate_w = d("ple_gate_w", [2, D, D])
    C.ple_w = d("ple_w", [2, 256, D])
    C.cmat = d("cmat", [7, 128, 128])
    C.cmask = d("cmask", [2, 128, 512])
    C.rope_a = d("rope_a", [2, 128, S])
    C.rope_c = d("rope_c", [2, 128, S])
    C.out = mk.dram("out", [2, S, D], F32, kind="ExternalOutput")
    C.acc = [[mk.dram("acc_%d_%d" % (L, s), [S, D], F32) for s in range(2)] for L in range(2)]
    C.xrows = [[mk.dram("xrows_%d_%d" % (L, s), [S, D], BF16) for s in range(2)] for L in range(2)]
    C.xln = [mk.dram("xln_%d" % s, [S, D], F32) for s in range(2)]
    C.affd = mk.dram("affd", [2, 16, S], F32)
    C.P = [mk.ps("pb%d" % i, [128, 512], F32) for i in range(7)]
    C.PH = mk.ps("pbh", [128, 1024], BF16)
    names = ["ident", "ones", "bd_ones", "triF", "triS", "triB", "triSB"]
    C.cm = mk.sb("cm", [128, 7, 128], F32)
    mk.dma("sp", C.cm.all(), None, reads=[], in_ap=C.cmat.h[:, :, :].rearrange("k p n -> p k n"))
    for i, n in enumerate(names):
        setattr(C, n, C.cm[:, i, :])
    C.cmb = mk.sb("cmb", [128, 7, 128], BF16)
    mk.copy("dve", C.cmb.all(), C.cm.all())
    for i, n in enumerate(names):
        setattr(C, n + "b", C.cmb[:, i, :])
    C.maskt = mk.sb("maskt", [128, 2, 512], BF16)
    mk.dma("pool", C.maskt.all(), None, reads=[], in_ap=C.cmask.h[:, :, :].rearrange("k p n -> p k n"))
    C.eps = mk.sb("eps", [128, 1], F32)
    mk.memset("dve", C.eps.all(), EPS)
    return C


def dump(C, name, view, shape, dtype):
    if not C.dbg:
        return
    mk = C.mk
    t = mk.dram("dbg_" + name, shape, dtype, kind="ExternalOutput")
    mk.dma("sp", t.all(), view)
    C.dumps[name] = "dbg_" + name


def layer_norm_tile(mk, C, xt, g, b, tmp, st):
    FM = 512
    for j in range(2):
        mk.op("dve", lambda e, j=j: e.bn_stats(st[:, j * 6:(j + 1) * 6].ap, xt[:, j * FM:(j + 1) * FM].ap),
              reads=[xt[:, j * FM:(j + 1) * FM]], writes=[st[:, j * 6:(j + 1) * 6]])
    mk.op("dve", lambda e: e.bn_aggr(st[:, 12:14].ap, st.h[:, 0:12].rearrange("p (n k) -> p n k", k=6)),
          reads=[st[:, 0:12]], writes=[st[:, 12:14]])
    mk.act(st[:, 14:15], st[:, 13:14], AF.Sqrt, bias=C.eps[:, 0:1])
    mk.recip(st[:, 15:16], st[:, 14:15])
    mk.ts("dve", tmp.all(), xt.all(), st[:, 12:13], st[:, 15:16], ALU.subtract, ALU.mult)
    mk.tt("pool", tmp.all(), tmp.all(), g, ALU.mult)
    mk.tt("dve", xt.all(), tmp.all(), b, ALU.add)


def transpose_tile_to_xT(mk, C, xt, xT, tt, banks, ei):
    for g in range(2):
        bank = banks()
        for j in range(4):
            dc = g * 4 + j
            mk.transpose(pv(bank, 0, 128, j * 128, (j + 1) * 128), xt[:, dc * 128:(dc + 1) * 128], C.ident)
        src = pv(bank, 0, 128, 0, 512, ("p (c n) -> p c n", dict(c=4)))
        dst = xT[:, g * 4:(g + 1) * 4, tt * 128:(tt + 1) * 128]
        mk.copy(ei(), dst, src)


def phase_prologue(mk, C, L, s, xT, ln_g=None, ln_b=None, stk=None):
    src = C.x_in if L == 0 else C.acc[0][s]
    xts = [mk.sb("pro_x%d" % i, [128, D], F32, stk) for i in range(3)]
    tmp = mk.sb("pro_tmp", [128, D], F32, stk)
    sts = [mk.sb("pro_st%d" % i, [128, 16], F32, stk) for i in range(2)]
    banks = Rot([C.P[0], C.P[1]])
    ei = Rot(["act", "dve"])
    for tt in range(NT):
        xt = xts[tt % 3]
        if L == 0:
            mk.dma("sp", xt.all(), src[s, tt * 128:(tt + 1) * 128, :], reads=[])
        else:
            mk.dma("sp", xt.all(), src[tt * 128:(tt + 1) * 128, :])
            layer_norm_tile(mk, C, xt, ln_g, ln_b, tmp, sts[tt % 2])
            mk.dma("sp", C.xln[s][tt * 128:(tt + 1) * 128, :], xt.all())
        transpose_tile_to_xT(mk, C, xt, xT, tt, banks, ei)


def proj_fm(mk, out_ps, w, c0, c1, xT, t0, t1, nk=8):
    for kc in range(nk):
        mk.mm(out_ps, w[:, kc, c0:c1], xT[:, kc, t0:t1], start=(kc == 0), stop=(kc == nk - 1))


def rope_evac(mk, C, psA, psB, rope, p0, p1, t0, t1, dst, tmpa, tmpb):
    mk.tt("dve", tmpa[p0:p1, 0:t1 - t0], psA, rope[p0:p1, 0, t0:t1], ALU.mult)
    mk.tt("dve", tmpb[p0:p1, 0:t1 - t0], psB, rope[p0:p1, 1, t0:t1], ALU.mult)
    mk.tt("pool", dst, tmpa[p0:p1, 0:t1 - t0], tmpb[p0:p1, 0:t1 - t0], ALU.add)


def attention(mk, C, QT, KT, VA, mixT, nheads, kdim, scale, kmap, stk):
    pts = [mk.sb("att_pt%d" % i, [128, 512], BF16, stk) for i in range(3)]
    rec = [mk.sb("att_rec%d" % i, [128, 512], F32, stk) for i in range(2)]
    sbk = [C.P[0], C.P[1], C.P[2]]
    obk = [C.P[3], C.P[4]]
    steps = [(h, qb, kt) for h in range(nheads) for qb in range(4) for kt in range(NT)]

    def qk(i):
        h, qb, kt = steps[i]
        qb0, qs, ks, vs, oc, ob0 = kmap(h)
        mk.mm(pv(sbk[i % 3], 0, 128, 0, 512), KT[qb0:qb0 + kdim, ks, kt * 128:(kt + 1) * 128],
              QT[qb0:qb0 + kdim, qs, qb * 512:(qb + 1) * 512])

    qk(0)
    for i, (h, qb, kt) in enumerate(steps):
        qb0, qs, ks, vs, oc, ob0 = kmap(h)
        if i + 1 < len(steps):
            qk(i + 1)
        obank = obk[(h * 4 + qb) % 2]
        pt = pts[i % 3]
        mk.act(pt.all(), pv(sbk[i % 3], 0, 128, 0, 512), AF.Exp, scale=scale)
        mk.mm(pv(obank, 0, 128, 0, 512), VA[:, kt, vs, :], pt.all(), start=(kt == 0), stop=(kt == NT - 1))
        if kt == NT - 1:
            r = rec[(h * 4 + qb) % 2]
            mk.recip(r[ob0:ob0 + 64, :], pv(obank, 64, 128, 0, 512))
            mk.tt("dve", mixT[ob0:ob0 + 64, oc, qb * 512:(qb + 1) * 512], pv(obank, 0, 64, 0, 512),
                  r[ob0:ob0 + 64, :], ALU.mult)


def attention_pairs(mk, C, QT, KT, VA, mixT, npairs, scale, pmap, stk):
    pts = [mk.sb("atp_pt%d" % i, [128, 512], BF16, stk) for i in range(4)]
    rec = [mk.sb("atp_rec%d" % i, [128, 512], F32, stk) for i in range(2)]
    SA = [C.P[0], C.P[1]]
    SB = [C.P[2], C.P[3]]
    OB = [C.P[4], C.P[5]]
    steps = [(p, qb, kt) for p in range(npairs) for qb in range(4) for kt in range(NT)]

    def qk(i):
        p, qb, kt = steps[i]
        qs, ks, vs, oc = pmap(p)
        ksl = slice(kt * 128, (kt + 1) * 128)
        qsl = slice(qb * 512, (qb + 1) * 512)
        mk.mm(pv(SA[i % 2], 0, 128, 0, 512), KT[0:64, ks, ksl], QT[0:64, qs, qsl])
        mk.mm(pv(SB[i % 2], 0, 128, 0, 512), KT[64:128, ks, ksl], QT[64:128, qs, qsl])

    qk(0)
    for i, (p, qb, kt) in enumerate(steps):
        qs, ks, vs, oc = pmap(p)
        if i + 1 < len(steps):
            qk(i + 1)
        ptA, ptB = pts[(2 * i) % 4], pts[(2 * i + 1) % 4]
        mk.act(ptA.all(), pv(SA[i % 2], 0, 128, 0, 512), AF.Exp, scale=scale)
        mk.act(ptB.all(), pv(SB[i % 2], 0, 128, 0, 512), AF.Exp, scale=scale)
        mk.mm(pv(OB[0], 0, 128, 0, 512), VA[:, kt, vs, :], ptA.all(), start=(kt == 0), stop=(kt == NT - 1))
        mk.mm(pv(OB[1], 0, 128, 0, 512), VA[:, kt, vs, :], ptB.all(), start=(kt == 0), stop=(kt == NT - 1))
        if kt == NT - 1:
            for j, ob0 in enumerate([0, 64]):
                r = rec[j]
                mk.recip(r[ob0:ob0 + 64, :], pv(OB[j], 64, 128, 0, 512))
                mk.tt("dve", mixT[ob0:ob0 + 64, oc, qb * 512:(qb + 1) * 512], pv(OB[j], 0, 64, 0, 512),
                      r[ob0:ob0 + 64, :], ALU.mult)


def even_mla(mk, C, s, xT, mixT):
    stk = ExitStack()
    we = C.w_in_even.h
    wq = mk.sb("mla_wq", [128, 8, 256], BF16, stk)
    wkv = mk.sb("mla_wkv", [128, 8, 128], BF16, stk)
    wkpe = mk.sb("mla_wkpe", [128, 8, 2, 96], BF16, stk)
    wuq = mk.sb("mla_wuq", [128, 2, 768], BF16, stk)
    wuqs = mk.sb("mla_wuqs", [128, 2, 8, 96], BF16, stk)
    wukv = mk.sb("mla_wukv", [128, 1024], BF16, stk)
    qn = mk.sb("mla_qn", [128, 2], F32, stk)
    kvn = mk.sb("mla_kvn", [128, 1], F32, stk)
    rope = mk.sb("mla_rope", [128, 2, S], F32, stk)
    mk.dma("pool", wq.all(), None, reads=[], in_ap=wsrc(we, 0, D, 0, 256))
    mk.dma("pool", wkv.all(), None, reads=[], in_ap=wsrc(we, 0, D, 256, 384))
    mk.memset("pool", wkpe.all(), 0.0)
    mk.memset("pool", wuqs.all(), 0.0)
    mk.dma("pool", wkpe[:, :, 0, 64:96], None, reads=[], in_ap=wsrc(we, 0, D, 384, 416))
    mk.dma("pool", wkpe[:, :, 1, 64:96], None, reads=[], in_ap=wsrc(C.w_kpe_sw.h, 0, D, 0, 32))
    mk.dma("pool", wuq.all(), None, reads=[], in_ap=wsrc(C.w_uq.h, 0, 256, 0, 768))
    for kc in range(2):
        mk.dma("pool", wuqs[:, kc, :, 64:96], None, reads=[],
               in_ap=C.w_uq_sw.h[kc * 128:(kc + 1) * 128, :].rearrange("p (h e) -> p h e", h=8))
    mk.dma("pool", wukv.all(), None, reads=[], in_ap=C.w_ukv.h[:, :])
    mk.dma("sp", qn.all(), None, reads=[], in_ap=C.mla_qn.h[:, :])
    mk.dma("sp", kvn.all(), None, reads=[], in_ap=C.mla_kvn.h[:, :])
    mk.dma("sp", rope[64:96, :, :], None, reads=[], in_ap=C.rope_a.h[:, 64:96, :].rearrange("k p n -> p k n"))

    cqn = mk.sb("mla_cqn", [128, 2, S], BF16, stk)
    ckvn = mk.sb("mla_ckvn", [128, S], BF16, stk)
    kper = mk.sb("mla_kper", [128, S], BF16, stk)
    QT = mk.sb("mla_QT", [128, 4, S], BF16, stk)
    KT = mk.sb("mla_KT", [128, 4, S], BF16, stk)
    VA = mk.sb("mla_VA", [128, NT, 4, 128], BF16, stk)
    st2 = ExitStack()
    cqf = [mk.sb("mla_cqf%d" % i, [128, 512], F32, st2) for i in range(3)]
    sq = [mk.sb("mla_sq%d" % i, [128, 512], F32, st2) for i in range(3)]
    rs = [mk.sb("mla_rs%d" % i, [128, 512], F32, st2) for i in range(2)]
    sqh = [mk.sb("mla_sqh%d" % i, [128, 2, 512], BF16, st2) for i in range(3)]
    tmpa = mk.sb("mla_tmpa", [128, 512], F32, st2)
    tmpb = mk.sb("mla_tmpb", [128, 512], F32, st2)
    P = C.P
    cut(C, 1)
    for tb in range(4):
        t0, t1 = tb * 512, (tb + 1) * 512
        groups = [(wq, 0, 128, qn[:, 0:1], cqn[:, 0, t0:t1]),
                  (wq, 128, 256, qn[:, 1:2], cqn[:, 1, t0:t1]),
                  (wkv, 0, 128, kvn[:, 0:1], ckvn[:, t0:t1])]
        for gi, (w, c0, c1, gain, dst) in enumerate(groups):
            bank = P[gi]
            proj_fm(mk, pv(bank, 0, 128, 0, 512), w, c0, c1, xT, t0, t1)
            mk.act(sq[gi].all(), pv(bank, 0, 128, 0, 512), AF.Square)
            mk.copy("dve", cqf[gi].all(), pv(bank, 0, 128, 0, 512))
        cut(C, 2)
        for gi in range(3):
            mk.copy("pool", sqh[gi][:, 0, :], sq[gi].all())
            mk.tt("pool", sqh[gi][:, 1, :], sq[gi].all(), sqh[gi][:, 0, :], ALU.subtract)
        for j, (gi, hl) in enumerate([(0, 0), (0, 1), (1, 0), (1, 1)]):
            mk.mm(pv(P[3], 0, 128, 0, 512), C.onesb, sqh[gi][:, hl, :], start=(j == 0), stop=(j == 3))
        for hl in range(2):
            mk.mm(pv(P[4], 0, 128, 0, 512), C.onesb, sqh[2][:, hl, :], start=(hl == 0), stop=(hl == 1))
        cut(C, 3)
        mk.act(rs[0].all(), pv(P[3], 0, 128, 0, 512), AF.Sqrt, bias=C.eps[:, 0:1], scale=1.0 / 256)
        mk.recip(rs[0].all(), rs[0].all())
        mk.act(rs[1].all(), pv(P[4], 0, 128, 0, 512), AF.Sqrt, bias=C.eps[:, 0:1], scale=1.0 / 128)
        mk.recip(rs[1].all(), rs[1].all())
        for gi, (w, c0, c1, gain, dst) in enumerate(groups):
            mk.stt("dve", dst, cqf[gi].all(), gain, rs[0 if gi < 2 else 1].all(), ALU.mult, ALU.mult)
        cut(C, 4)
        for kc in range(8):
            mk.mm(pv(P[5], 0, 96, 0, 512), wkpe[:, kc, 0, :], xT[:, kc, t0:t1], start=(kc == 0), stop=(kc == 7))
        for kc in range(8):
            mk.mm(pv(P[6], 0, 96, 0, 512), wkpe[:, kc, 1, :], xT[:, kc, t0:t1], start=(kc == 0), stop=(kc == 7))
        cut(C, 5)
        mk.tt("dve", tmpa[64:96, :], pv(P[5], 64, 96, 0, 512), rope[64:96, 0, t0:t1], ALU.mult)
        mk.tt("dve", tmpb[64:96, :], pv(P[6], 64, 96, 0, 512), rope[64:96, 1, t0:t1], ALU.mult)
        mk.tt("pool", kper[64:96, t0:t1], tmpa[64:96, :], tmpb[64:96, :], ALU.add)
        cut(C, 6)
    st2.close()
    mk.barrier()
    st2 = ExitStack()
    tmpa = mk.sb("mla_tmpa2", [128, 512], F32, st2)
    tmpb = mk.sb("mla_tmpb2", [128, 512], F32, st2)
    tmpc = mk.sb("mla_tmpc2", [128, 512], F32, st2)
    tmpd = mk.sb("mla_tmpd2", [128, 512], F32, st2)
    stop = getattr(C, "stop", 99)
    if stop <= 1:
        dump(C, "cqn", cqn.all(), [128, 2, S], BF16)
        dump(C, "kper", kper[64:96, :], [32, S], BF16)
    for hg in range(2 if stop > 1 else 0):
        mk.memset("pool", VA.all(), 1.0)
        for tb in range(4):
            t0, t1 = tb * 512, (tb + 1) * 512
            ab = Rot([P[0], P[1]])
            bb = Rot([P[2], P[5]])
            kb = Rot([P[3], P[4]])
            for hl in range(4):
                h = hg * 4 + hl
                A = ab()
                B = bb()
                K = kb()
                for kc in range(2):
                    mk.mm(pv(A, 0, 96, 0, 512), wuq[:, kc, h * 96:(h + 1) * 96], cqn[:, kc, t0:t1], start=(kc == 0), stop=(kc == 1))
                for kc in range(2):
                    mk.mm(pv(B, 0, 96, 0, 512), wuqs[:, kc, h, :], cqn[:, kc, t0:t1], start=(kc == 0), stop=(kc == 1))
                mk.mm(pv(K, 0, 64, 0, 512), wukv[:, h * 128:h * 128 + 64], ckvn[:, t0:t1])
                mk.copy("act", QT[0:64, hl, t0:t1], pv(A, 0, 64, 0, 512))
                ta, tb_ = (tmpa, tmpb) if hl % 2 == 0 else (tmpc, tmpd)
                mk.tt("dve", ta[64:96, :], pv(A, 64, 96, 0, 512), rope[64:96, 0, t0:t1], ALU.mult)
                mk.tt("dve", tb_[64:96, :], pv(B, 64, 96, 0, 512), rope[64:96, 1, t0:t1], ALU.mult)
                mk.tt("pool", QT[64:96, hl, t0:t1], ta[64:96, :], tb_[64:96, :], ALU.add)
                mk.copy("act", KT[0:64, hl, t0:t1], pv(K, 0, 64, 0, 512))
                mk.copy("pool", KT[64:96, hl, t0:t1], kper[64:96, t0:t1])
            for j in range(4):
                tt = tb * 4 + j
                bank = P[6]
                rhs_ap = wukv.h[:, hg * 512:(hg + 1) * 512].rearrange("p (h e) -> p h e", h=4)[:, :, 64:128]
                rhs = V(wukv, rhs_ap, hg * 512, (hg + 1) * 512)
                mk.mm(pv(bank, 0, 128, 0, 256, ("p (h e) -> p h e", dict(h=4))), ckvn[:, tt * 128:(tt + 1) * 128], rhs)
                mk.copy("act" if j % 2 else "dve", VA[:, tt, :, 0:64], pv(bank, 0, 128, 0, 256, ("p (h e) -> p h e", dict(h=4))))
        if C.dbg and s == 0:
            dump(C, "QT%d" % hg, QT.all(), [128, 4, S], BF16)
            dump(C, "KT%d" % hg, KT.all(), [128, 4, S], BF16)
            dump(C, "VA%d" % hg, VA.all(), [128, NT, 4, 128], BF16)
        if stop <= 2:
            continue
        st3 = ExitStack()
        attention(mk, C, QT, KT, VA, mixT, 4, 96, 96.0 ** -0.5,
                  lambda hl: (0, hl, hl, hl, (hg * 4 + hl) // 2, (hl % 2) * 64), st3)
        st3.close()
    st2.close()
    stk.close()
    mk.barrier()


def even_gla(mk, C, s, xT, mixT):
    stk = ExitStack()
    we = C.w_in_even.h
    P = C.P
    wfm = mk.sb("gla_wfm", [128, 8, 512], BF16, stk)
    wtm = mk.sb("gla_wtm", [128, 8, 1280], BF16, stk)
    wlr = mk.sb("gla_wlr", [128, 8, 32], BF16, stk)
    gw = mk.sb("gla_gw", [17, 2, 256], BF16, stk)
    gnorm = mk.sb("gla_gnorm", [128, 512], F32, stk)
    one1 = mk.sb("gla_one1", [128, 1], F32, stk)
    mk.memset("dve", one1.all(), 1.0)
    mk.dma("pool", wfm.all(), None, reads=[], in_ap=wsrc(we, 0, D, 416, 928))
    mk.dma("pool", wtm[:, :, 0:768], None, reads=[], in_ap=wsrc(we, 0, D, 672, 1440))
    mk.dma("pool", wtm[:, :, 768:1280], None, reads=[], in_ap=wsrc(we, 0, D, 1472, 1984))
    mk.dma("pool", wlr.all(), None, reads=[], in_ap=wsrc(we, 0, D, 1440, 1472))
    mk.dma("pool", gw.all(), None, reads=[], in_ap=C.gla_gw.h[:, :, :].rearrange("k r n -> r k n"))
    mk.dma("sp", gnorm.all(), None, reads=[], in_ap=C.gla_norm_bc.h[:, :])
    gqT = mk.sb("gla_gqT", [128, 2, S], BF16, stk)
    gkT = mk.sb("gla_gkT", [128, 2, S], BF16, stk)
    gk_tok = mk.sb("gla_gk_tok", [128, NT, 256], BF16, stk)
    gv_tok = mk.sb("gla_gv_tok", [128, NT, 512], BF16, stk)
    o_f = mk.sb("gla_of", [128, NT, 512], F32, stk)
    ei = Rot(["act", "dve"])
    bk = Rot([P[0], P[1], P[2]])
    for tb in range(4):
        t0, t1 = tb * 512, (tb + 1) * 512
        for mc in range(4):
            bank = bk()
            proj_fm(mk, pv(bank, 0, 128, 0, 512), wfm, mc * 128, (mc + 1) * 128, xT, t0, t1)
            dst = (gqT if mc < 2 else gkT)[:, mc % 2, t0:t1]
            mk.copy(ei(), dst, pv(bank, 0, 128, 0, 512))
    for tt in range(NT):
        tk = slice(tt * 128, (tt + 1) * 128)
        b1, b2 = bk(), bk()
        for kc in range(8):
            mk.mm(pv(b1, 0, 128, 0, 256), xT[:, kc, tk], wtm[:, kc, 0:256], start=(kc == 0), stop=(kc == 7))
        for kc in range(8):
            mk.mm(pv(b2, 0, 128, 0, 512), xT[:, kc, tk], wtm[:, kc, 256:768], start=(kc == 0), stop=(kc == 7))
        mk.copy("act", gk_tok[:, tt, :], pv(b1, 0, 128, 0, 256))
        mk.copy("dve", gv_tok[:, tt, :], pv(b2, 0, 128, 0, 512))
    cut(C, 11)
    st2 = ExitStack()
    lrT = [mk.sb("gla_lrT%d" % i, [17, 128], BF16, st2) for i in range(2)]
    for t in lrT:
        mk.memset("dve", t.all(), 1.0)
    ez = [mk.sb("gla_ez%d" % i, [128, 256], F32, st2) for i in range(2)]
    nla = [mk.sb("gla_nla%d" % i, [128, 256], F32, st2) for i in range(2)]
    nlah = [mk.sb("gla_nlah%d" % i, [128, 2, 256], BF16, st2) for i in range(2)]
    E1 = [mk.sb("gla_E1%d" % i, [128, 2, 128], F32, st2) for i in range(3)]
    E2 = [mk.sb("gla_E2%d" % i, [128, 2, 128], F32, st2) for i in range(2)]
    E3 = [mk.sb("gla_E3%d" % i, [128, 256], F32, st2) for i in range(2)]
    ke = [mk.sb("gla_ke%d" % i, [128, 256], BF16, st2) for i in range(2)]
    qgT = [mk.sb("gla_qgT%d" % i, [128, 2, 128], BF16, st2) for i in range(2)]
    kgT = [mk.sb("gla_kgT%d" % i, [128, 2, 128], BF16, st2) for i in range(2)]
    attm = [mk.sb("gla_attm%d" % i, [128, 2, 2, 128], BF16, st2) for i in range(2)]
    Sf = mk.sb("gla_Sf", [128, 2, 128], F32, st2)
    Sb = [mk.sb("gla_Sb%d" % i, [128, 2, 128], BF16, st2) for i in range(4)]
    osum = [mk.sb("gla_osum%d" % i, [128, 512], F32, st2) for i in range(1)] * 2
    osq = mk.sb("gla_osq", [128, 512], F32, st2)
    sg = [mk.sb("gla_sg%d" % i, [128, 512], F32, st2) for i in range(1)] * 2
    og = [mk.sb("gla_og%d" % i, [128, 512], BF16, st2) for i in range(1)] * 2
    stt_ = [mk.sb("gla_st%d" % i, [128, 12], F32, st2) for i in range(2)]
    it = 0
    sbi = 0
    for d in range(2):
        tri = C.triFb if d == 0 else C.triBb
        tris = C.triSb if d == 0 else C.triSBb
        mk.memset("dve", Sf.all(), 0.0)
        mk.memset("dve", Sb[sbi % 4].all(), 0.0)
        tiles = range(NT) if d == 0 else range(NT - 1, -1, -1)
        order = [0, 1] if d == 0 else [1, 0]
        for tt in tiles:
            i2 = it % 2
            it += 1
            tk = slice(tt * 128, (tt + 1) * 128)
            for kc in range(8):
                mk.mm(pv(P[0], 0, 16, 0, 128), wlr[:, kc, d * 16:(d + 1) * 16], xT[:, kc, tk], start=(kc == 0), stop=(kc == 7))
            mk.copy("act", lrT[i2][0:16, :], pv(P[0], 0, 16, 0, 128))
            mk.mm(pv(P[0], 0, 128, 128, 384), lrT[i2][0:17, :], gw[0:17, d, :])
            mk.act(ez[i2].all(), pv(P[0], 0, 128, 128, 384), AF.Exp, scale=-1.0)
            mk.act(nla[i2].all(), ez[i2].all(), AF.Ln, bias=one1[:, 0:1])
            mk.copy("pool", nlah[i2][:, 0, :], nla[i2].all())
            mk.tt("pool", nlah[i2][:, 1, :], nla[i2].all(), nlah[i2][:, 0, :], ALU.subtract)
            cut(C, 12)
            for pc in range(2):
                for hl in range(2):
                    mk.mm(pv(P[1], 0, 128, pc * 128, (pc + 1) * 128), nlah[i2][:, hl, pc * 128:(pc + 1) * 128], tri,
                          start=(hl == 0), stop=(hl == 1))
            for hl in range(2):
                mk.mm(pv(P[1], 0, 128, 256, 512), tris, nlah[i2][:, hl, :], start=(hl == 0), stop=(hl == 1))
            e1 = E1[it % 3]
            cumv = pv(P[1], 0, 128, 0, 256, ("p (c n) -> p c n", dict(c=2)))
            mk.act(e1.all(), cumv, AF.Exp, scale=-1.0 / 16)
            mk.act(E2[i2].all(), cumv, AF.Exp, scale=1.0 / 16)
            mk.act(E3[i2].all(), pv(P[1], 0, 128, 256, 512), AF.Exp, scale=-1.0 / 16)
            mk.tt("pool", ke[i2].all(), gk_tok[:, tt, :], E3[i2].all(), ALU.mult)
            mk.stt("dve", qgT[i2].all(), gqT[:, :, tk], 0.125, e1.all(), ALU.mult, ALU.mult)
            mk.tt("dve", kgT[i2].all(), gkT[:, :, tk], E2[i2].all(), ALU.mult)
            cut(C, 13)
            for h in range(4):
                pc, b0 = h // 2, (h % 2) * 64
                bank = P[2] if h % 2 == 0 else P[5]
                mk.mm(pv(bank, 0, 128, pc * 128, (pc + 1) * 128), kgT[i2][b0:b0 + 64, pc, :], qgT[i2][b0:b0 + 64, pc, :])
            for hp in range(2):
                bank = P[2] if hp == 0 else P[5]
                mk.tt("dve", attm[i2][:, :, hp, :], pv(bank, 0, 128, 0, 256, ("p (a n) -> p a n", dict(a=2))),
                      V(C.maskt, C.maskt.h[:, d, 0:256].rearrange("p (a n) -> p a n", a=2), d * 512, d * 512 + 256), ALU.mult)
            cut(C, 14)
            for c in range(2):
                ubank = P[3] if c == 0 else P[6]
                for h in range(4):
                    pc, j = h // 2, h % 2
                    mk.mm(pv(ubank, j * 64, j * 64 + 64, pc * 128, (pc + 1) * 128), ke[i2][c * 64:(c + 1) * 64, h * 64:(h + 1) * 64],
                          gv_tok[c * 64:(c + 1) * 64, tt, h * 128:(h + 1) * 128])
            cut(C, 15)
            sb_for = {}
            for c in order:
                sb_for[c] = Sb[sbi % 4]
                dcol = c * 64 + (63 if d == 0 else 0)
                ubank = P[3] if c == 0 else P[6]
                for pc in range(2):
                    mk.stt("dve", Sf[:, pc, :], Sf[:, pc, :], e1[:, pc, dcol:dcol + 1], pv(ubank, 0, 128, pc * 128, (pc + 1) * 128), ALU.mult, ALU.add)
                sbi += 1
                mk.copy("pool", Sb[sbi % 4].all(), Sf.all())
            cut(C, 16)
            for h in range(4):
                pc, b0 = h // 2, (h % 2) * 64
                obank = P[4] if h % 2 == 0 else P[1]
                mk.mm(pv(obank, 0, 128, pc * 128, (pc + 1) * 128), attm[i2][:, pc, h % 2, :], gv_tok[:, tt, h * 128:(h + 1) * 128],
                      start=True, stop=False, skip_group_check=True)
                for ci, c in enumerate(order):
                    mk.mm(pv(obank, c * 64, c * 64 + 64, pc * 128, (pc + 1) * 128), qgT[i2][b0:b0 + 64, pc, c * 64:(c + 1) * 64],
                          sb_for[c][b0:b0 + 64, pc, :], start=False, stop=(ci == 1), skip_group_check=True)
            of4 = o_f.h[:, tt, :].rearrange("p (a b e) -> p a b e", a=2, b=2)
            if d == 0:
                for hp in range(2):
                    obank = P[4] if hp == 0 else P[1]
                    mk.copy("act", V(o_f, of4[:, :, hp, :], tt * 512, (tt + 1) * 512),
                            pv(obank, 0, 128, 0, 256, ("p (a e) -> p a e", dict(a=2))))
                cut(C, 17)
                continue
            os_ = osum[i2]
            os4 = os_.h[:, :].rearrange("p (a b e) -> p a b e", a=2, b=2)
            for hp in range(2):
                obank = P[4] if hp == 0 else P[1]
                mk.tt("dve", V(os_, os4[:, :, hp, :], 0, 512), V(o_f, of4[:, :, hp, :], tt * 512, (tt + 1) * 512),
                      pv(obank, 0, 128, 0, 256, ("p (a e) -> p a e", dict(a=2))), ALU.add)
            if C.dbg and s == 0 and getattr(C, "dbg_osum_on", False):
                if "osum" not in C.dumps:
                    C.dbg_osum = mk.dram("dbg_osum", [128, NT, 512], F32, kind="ExternalOutput")
                    C.dumps["osum"] = 1
                mk.dma("sp", C.dbg_osum[:, tt, :], os_.all())
            mk.tt("pool", osq.all(), os_.all(), os_.all(), ALU.mult)
            st = stt_[i2]
            osq3 = V(osq, osq.h[:, :].rearrange("p (h e) -> p h e", h=4), 0, 512)
            mk.op("dve", lambda e, st=st, osq3=osq3: e.tensor_reduce(st[:, 0:4].ap, osq3.ap, AX.X, ALU.add), reads=[osq3], writes=[st[:, 0:4]])
            mk.act(st[:, 4:8], st[:, 0:4], AF.Sqrt, bias=C.eps[:, 0:1], scale=1.0 / 128)
            mk.recip(st[:, 8:12], st[:, 4:8])
            for kc in range(8):
                mk.mm(pv(P[0], 0, 128, 0, 512), xT[:, kc, tk], wtm[:, kc, 768:1280], start=(kc == 0), stop=(kc == 7))
            mk.act(sg[i2].all(), pv(P[0], 0, 128, 0, 512), AF.Silu)
            for h in range(4):
                mk.ts("dve" if h % 2 else "pool", os_[:, h * 128:(h + 1) * 128], os_[:, h * 128:(h + 1) * 128], st[:, 8 + h:9 + h], None, ALU.mult)
            mk.tt("pool", os_.all(), os_.all(), gnorm.all(), ALU.mult)
            mk.tt("dve", og[i2].all(), os_.all(), sg[i2].all(), ALU.mult)
            for h in range(4):
                mk.transpose(pv(C.PH, 0, 128, h * 128, (h + 1) * 128), og[i2][:, h * 128:(h + 1) * 128], C.identb)
            mk.copy("act", mixT[:, 4:8, tk], pv(C.PH, 0, 128, 0, 512, ("p (c n) -> p c n", dict(c=4))))
    if C.dbg and s == 0:
        dump(C, "o_f", o_f.all(), [128, NT, 512], F32)
    st2.close()
    stk.close()
    mk.barrier()


def phase_post(mk, C, L, s, mixT):
    stk = ExitStack()
    P = C.P
    wo = mk.sb("po_wo", [128, 8, D], BF16, stk)
    wg = mk.sb("po_wg", [128, 8, D], BF16, stk)
    wp = mk.sb("po_wp", [128, 2, D], BF16, stk)
    pT = mk.sb("po_pT", [128, 2, S], BF16, stk)
    rwf = mk.sb("po_rwf", [128, 8, 16], F32, stk)
    rw = mk.sb("po_rw", [128, 8, 2, 16], BF16, stk)
    lnb = mk.sb("po_lnb", [128, 5, D], F32, stk)
    affT = mk.sb("po_affT", [16, S], F32, stk)
    ab = 0
    mk.dma("pool", wo.all(), None, reads=[], in_ap=wsrc(C.w_o.h[L], 0, D, 0, D))
    mk.dma("pool", wg.all(), None, reads=[], in_ap=wsrc(C.ple_gate_w.h[L], 0, D, 0, D))
    mk.dma("pool", wp.all(), None, reads=[], in_ap=wsrc(C.ple_w.h[L], 0, 256, 0, D))
    mk.dma("pool", pT.all(), None, reads=[], in_ap=C.pT.h[L, s].rearrange("(c p) n -> p c n", p=128))
    mk.dma("sp", rwf.all(), None, reads=[], in_ap=wsrc(C.router_w.h[L], 0, D, 0, 16))
    mk.dma("sp", lnb.all(), None, reads=[], in_ap=C.lnbc.h[L].rearrange("k p n -> p k n"))
    mk.copy("dve", rw[:, :, 0, :], rwf.all())
    mk.tt("dve", rw[:, :, 1, :], rwf.all(), rw[:, :, 0, :], ALU.subtract)
    stt_ = ExitStack()
    xts = [mk.sb("po_xt%d" % i, [128, D], F32, stt_) for i in range(2)]
    rts = [mk.sb("po_r%d" % i, [128, D], F32, stt_) for i in range(3)]
    ras = [mk.sb("po_ra%d" % i, [128, D], F32, stt_) for i in range(2)]
    x1bs = [mk.sb("po_x1b%d" % i, [128, D], BF16, stt_) for i in range(2)]
    x1Th = [mk.sb("po_x1Th%d" % i, [128, 8, 128], BF16, stt_) for i in range(3)]
    x1Tl = [mk.sb("po_x1Tl%d" % i, [128, 8, 128], BF16, stt_) for i in range(2)]
    accs = [mk.sb("po_acc%d" % i, [128, D], F32, stt_) for i in range(2)]
    tmp = mk.sb("po_tmp", [128, D], F32, stt_)
    gbs = [mk.sb("po_gb%d" % i, [128, 512], F32, stt_) for i in range(2)]
    sgs = [mk.sb("po_sg%d" % i, [128, 512], F32, stt_) for i in range(2)]
    sts = [mk.sb("po_st%d" % i, [128, 20], F32, stt_) for i in range(2)]
    sm = [mk.sb("po_sm%d" % i, [128, 40], F32, stt_) for i in range(2)]
    xsrc = C.x_in if L == 0 else C.xln[s]

    def stage_a(tt):
        tk = slice(tt * 128, (tt + 1) * 128)
        xt, r, st = xts[tt % 2], rts[tt % 3], sts[tt % 2]
        if L == 0:
            mk.dma("sp", xt.all(), xsrc[s, tk, :], reads=[])
        else:
            mk.dma("sp", xt.all(), xsrc[tk, :])
        for half in range(2):
            for kc in range(8):
                mk.mm(pv(P[half], 0, 128, 0, 512), mixT[:, kc, tk], wo[:, kc, half * 512:(half + 1) * 512], start=(kc == 0), stop=(kc == 7))
        for half in range(2):
            hs = slice(half * 512, (half + 1) * 512)
            mk.stt("dve", r[:, hs], xt[:, hs], ALPHA, pv(P[half], 0, 128, 0, 512), ALU.mult, ALU.add)
        for j in range(2):
            mk.op("dve", lambda e, j=j: e.bn_stats(st[:, j * 6:(j + 1) * 6].ap, r[:, j * 512:(j + 1) * 512].ap),
                  reads=[r[:, j * 512:(j + 1) * 512]], writes=[st[:, j * 6:(j + 1) * 6]])
        mk.op("dve", lambda e: e.bn_aggr(st[:, 12:14].ap, st[:, 0:12].ap), reads=[st[:, 0:12]], writes=[st[:, 12:14]])
        mk.act(st[:, 14:15], st[:, 13:14], AF.Sqrt, bias=C.eps[:, 0:1])
        mk.recip(st[:, 15:16], st[:, 14:15])
        mk.stt("dve", st[:, 16:17], st[:, 12:13], -1.0, st[:, 15:16], ALU.mult, ALU.mult)
        mk.act(tmp.all(), r.all(), AF.Identity, bias=st[:, 16:17], scale=st[:, 15:16])
        mk.tt("pool", tmp.all(), tmp.all(), lnb[:, 0, :], ALU.mult)
        mk.tt("pool", r.all(), tmp.all(), lnb[:, 1, :], ALU.add)

    def stage_b(tt):
        tk = slice(tt * 128, (tt + 1) * 128)
        r, x1b, xh, xl, m = rts[tt % 3], x1bs[tt % 2], x1Th[tt % 3], x1Tl[tt % 2], sm[tt % 2]
        mk.copy("act", x1b.all(), r.all())
        mk.dma("sp", C.xrows[L][s][tk, :], x1b.all())
        mk.act(ras[tt % 2].all(), r.all(), AF.Copy, scale=ALPHA)
        for g in range(2):
            bank = P[2 + g]
            for j in range(4):
                dc = g * 4 + j
                mk.transpose(pv(bank, 0, 128, j * 128, (j + 1) * 128), r[:, dc * 128:(dc + 1) * 128], C.ident)
            src = pv(bank, 0, 128, 0, 512, ("p (c n) -> p c n", dict(c=4)))
            mk.copy("act", xh[:, g * 4:(g + 1) * 4, :], src)
            mk.tt("dve", xl[:, g * 4:(g + 1) * 4, :], src, xh[:, g * 4:(g + 1) * 4, :], ALU.subtract)
        n = 0
        for kc in range(8):
            for (xa, wa) in [(xh, 0), (xl, 0), (xh, 1)]:
                mk.mm(pv(P[4], 0, 128, 0, 16), xa[:, kc, :], rw[:, kc, wa, :], start=(n == 0), stop=(n == 23))
                n += 1
        mk.op("dve", lambda e: e.tensor_reduce(m[:, 0:1].ap, P[4].h[:, 0:16], AX.X, ALU.max), reads=[pv(P[4], 0, 128, 0, 16)], writes=[m[:, 0:1]])
        mk.ts("dve", m[:, 1:2], m[:, 0:1], -1.0, None, ALU.mult)
        mk.act(m[:, 8:24], pv(P[4], 0, 128, 0, 16), AF.Exp, bias=m[:, 1:2], accum=m[:, 2:3])
        mk.recip(m[:, 3:4], m[:, 2:3])
        mk.ts("dve", m[:, 24:40], m[:, 8:24], m[:, 3:4], None, ALU.mult)
        mk.transpose(pv(P[4], 0, 16, 128, 256), m[:, 24:40], C.ident)
        mk.copy("act", affT[ab:ab + 16, tk], pv(P[4], 0, 16, 128, 256))

    def stage_c(tt):
        tk = slice(tt * 128, (tt + 1) * 128)
        xh, acc, ra = x1Th[tt % 3], accs[tt % 2], ras[tt % 2]
        for half in range(2):
            hs = slice(half * 512, (half + 1) * 512)
            for kc in range(8):
                mk.mm(pv(P[5], 0, 128, 0, 512), xh[:, kc, :], wg[:, kc, hs], start=(kc == 0), stop=(kc == 7))
            for kc in range(2):
                mk.mm(pv(P[6], 0, 128, 0, 512), pT[:, kc, tk], wp[:, kc, hs], start=(kc == 0), stop=(kc == 1))
            mk.tt("dve", gbs[half].all(), pv(P[5], 0, 128, 0, 512), lnb[:, 4, hs], ALU.add)
            mk.act(sgs[half].all(), gbs[half].all(), AF.Sigmoid)
            mk.tt("dve", gbs[half].all(), sgs[half].all(), pv(P[6], 0, 128, 0, 512), ALU.mult)
            mk.tt("pool", acc[:, hs], ra[:, hs], gbs[half].all(), ALU.add)
        mk.dma("sp", C.acc[L][s][tk, :], acc.all())

    SK1, SK2 = getattr(C, "skew", (1, 2))
    for i in range(NT + SK2):
        if i < NT:
            stage_a(i)
        if 0 <= i - SK1 < NT:
            stage_b(i - SK1)
        if 0 <= i - SK2 < NT:
            stage_c(i - SK2)
    stt_.close()
    mk.barrier()
    mk.dma("sp", C.affd[s], affT.all())
    stk.close()
    mk.barrier()


def phase_topk(mk, C, idx_s, gate_s):
    stk = ExitStack()
    P = C.P
    work = mk.sb("tk_work", [64, S], F32, stk)
    gat = mk.sb("tk_gat", [64, 256], F32, stk)
    idxu = mk.sb("tk_idxu", [64, 256], U32, stk)
    idxf = mk.sb("tk_idxf", [64, 256], F32, stk)
    mk.memset("dve", work.all(), 0.0)
    for s in range(2):
        mk.dma("sp", work[s * 32:s * 32 + 16, :], C.affd[s])
    for r_ in range(32):
        g8 = gat[:, r_ * 8:(r_ + 1) * 8]
        i8 = idxu[:, r_ * 8:(r_ + 1) * 8]
        mk.op("dve", lambda e, g8=g8: e.max(out=g8.ap, in_=work.all().ap), reads=[work.all()], writes=[g8])
        mk.op("dve", lambda e, g8=g8, i8=i8: e.max_index(out=i8.ap, in_max=g8.ap, in_values=work.all().ap),
              reads=[g8, work.all()], writes=[i8], strict=[g8])
        mk.op("dve", lambda e, g8=g8: e.match_replace(out=work.all().ap, in_to_replace=g8.ap, in_values=work.all().ap, imm_value=-1.0),
              reads=[g8, work.all()], writes=[work.all()], strict=[g8])
    mk.copy("dve", idxf.all(), idxu.all())
    for half in range(2):
        cs = slice(half * 128, (half + 1) * 128)
        mk.transpose(pv(P[0], 0, 128, half * 64, (half + 1) * 64), idxf[0:64, cs], C.cm[0:64, 0, 0:64])
        mk.transpose(pv(P[0], 0, 128, 128 + half * 64, 128 + (half + 1) * 64), gat[0:64, cs], C.cm[0:64, 0, 0:64])
    for s in range(2):
        iv = P[0].h[:, 0:128].rearrange("p (h e) -> p h e", h=2)[:, :, s * 32:s * 32 + 16]
        gv = P[0].h[:, 128:256].rearrange("p (h e) -> p h e", h=2)[:, :, s * 32:s * 32 + 16]
        mk.copy("dve", idx_s[s].all(), V(P[0], iv, 0, P[0].size))
        mk.copy("dve", gate_s[s].all(), V(P[0], gv, 0, P[0].size))
    stk.close()
    mk.barrier()


NWB = 3


def experts_alloc_w(mk, stk):
    return [[mk.sb("ex_w%d_%d" % (j, i), [128, 8, D], BF16, stk) for j in range(3)] for i in range(NWB)]


def experts_load_w(mk, C, L, wb, e):
    for j, w in enumerate([C.w1, C.w3, C.w2]):
        mk.dma("pool", wb[e % NWB][j].all(), None, reads=[], in_ap=wsrc(w.h[L, e], 0, D, 0, D))


def phase_experts(mk, C, L, seqs, idx_s, gate_s, wb=None):
    stk = ExitStack()
    P = C.P
    ns = len(seqs)
    NSL = ns * 256
    preloaded = wb is not None
    if wb is None:
        wb = experts_alloc_w(mk, stk)
    xg = [mk.sb("ex_xg%d" % i, [128, D], BF16, stk) for i in range(4)]
    xgT = [mk.sb("ex_xgT%d" % i, [128, 8, NSL], BF16, stk) for i in range(1)] * 2
    hidT = mk.sb("ex_hidT", [128, 8, NSL], BF16, stk)
    sl = [mk.sb("ex_sl%d" % i, [128, NSL], F32, stk) for i in range(2)]
    ye = [mk.sb("ex_ye%d" % i, [128, D], F32, stk) for i in range(2)]
    wsrcs = [C.w1, C.w3, C.w2]

    def load_w(e):
        experts_load_w(mk, C, L, wb, e)

    if not preloaded:
        load_w(0)
        load_w(1)
    gi = 0
    yi = 0
    for e in range(16):
        w1b, w3b, w2b = wb[e % NWB]
        xt_ = xgT[e % 2]
        for si, s in enumerate(seqs):
            for half in range(2):
                g = xg[gi % 4]
                gi += 1
                idxv = idx_s[s][:, half, e:e + 1]
                src = C.xrows[L][s]
                mk.dma("pool", g.all(), src.all(), reads=[src.all(), idxv],
                       indirect=lambda eng, g=g, src=src, idxv=idxv: eng.indirect_dma_start(
                           out=g.all().ap, out_offset=None, in_=src.h[:, :],
                           in_offset=bass.IndirectOffsetOnAxis(ap=idxv.ap, axis=0)))
                for dc in range(8):
                    mk.transpose(pv(C.PH, 0, 128, dc * 128, (dc + 1) * 128), g[:, dc * 128:(dc + 1) * 128], C.identb)
                sl0 = (si * 2 + half) * 128
                mk.copy("act" if half else "dve", xt_[:, :, sl0:sl0 + 128],
                        pv(C.PH, 0, 128, 0, 1024, ("p (c n) -> p c n", dict(c=8))))
        cut(C, 21)
        if e + 2 < 16:
            load_w(e + 2)
        for fc in range(8):
            fs = slice(fc * 128, (fc + 1) * 128)
            b1, b3 = (P[0], P[1]) if fc % 2 == 0 else (P[2], P[3])
            for kc in range(8):
                mk.mm(pv(b1, 0, 128, 0, NSL), w1b[:, kc, fs], xt_[:, kc, :], start=(kc == 0), stop=(kc == 7))
            for kc in range(8):
                mk.mm(pv(b3, 0, 128, 0, NSL), w3b[:, kc, fs], xt_[:, kc, :], start=(kc == 0), stop=(kc == 7))
            mk.act(sl[fc % 2].all(), pv(b1, 0, 128, 0, NSL), AF.Silu)
            mk.tt("dve", hidT[:, fc, :], sl[fc % 2].all(), pv(b3, 0, 128, 0, NSL), ALU.mult)
        cut(C, 22)
        for si, s in enumerate(seqs):
            for half in range(2):
                sl0 = (si * 2 + half) * 128
                y = ye[yi % 2]
                yi += 1
                gv = gate_s[s][:, half, e:e + 1]
                for h2 in range(2):
                    bank = P[4 + h2]
                    for fc in range(8):
                        mk.mm(pv(bank, 0, 128, 0, 512), hidT[:, fc, sl0:sl0 + 128], w2b[:, fc, h2 * 512:(h2 + 1) * 512], start=(fc == 0), stop=(fc == 7))
                    if h2 == 0:
                        mk.act(y[:, 0:512], pv(bank, 0, 128, 0, 512), AF.Copy, scale=gv)
                    else:
                        mk.ts("dve", y[:, 512:1024], pv(bank, 0, 128, 0, 512), gv, None, ALU.mult)
                idxv = idx_s[s][:, half, e:e + 1]
                dst = C.acc[L][s]
                mk.dma("pool", dst.all(), y.all(), reads=[y.all(), idxv], writes=[dst.all()],
                       indirect=lambda eng, y=y, dst=dst, idxv=idxv: eng.indirect_dma_start(
                           out=dst.h[:, :], out_offset=bass.IndirectOffsetOnAxis(ap=idxv.ap, axis=0),
                           in_=y.all().ap, in_offset=None, compute_op=ALU.add, oob_is_err=True))
                cut(C, 23)
        cut(C, 24)
    stk.close()
    mk.barrier()


def odd_mixer(mk, C, s, xT, mixT):
    stk = ExitStack()
    P = C.P
    wi = C.w_in_odd.h
    wk = mk.sb("gq_wk", [128, 8, 4, 2, 64], BF16, stk)
    wks = mk.sb("gq_wks", [128, 8, 4, 2, 64], BF16, stk)
    wv = mk.sb("gq_wv", [128, 8, 256], BF16, stk)
    gn = mk.sb("gq_gn", [128, 4], F32, stk)
    rope = mk.sb("gq_rope", [128, 2, S], F32, stk)
    for dup in range(2):
        for kc in range(8):
            mk.dma("pool", wk[:, kc, :, dup, :], None, reads=[],
                   in_ap=wi[kc * 128:(kc + 1) * 128, 1024:1280].rearrange("p (k e) -> p k e", k=4))
            mk.dma("pool", wks[:, kc, :, dup, :], None, reads=[],
                   in_ap=C.w_k_sw.h[kc * 128:(kc + 1) * 128, :].rearrange("p (k e) -> p k e", k=4))
    mk.dma("pool", wv.all(), None, reads=[], in_ap=wsrc(wi, 0, D, 1280, 1536))
    mk.dma("sp", gn.all(), None, reads=[], in_ap=C.gqa_n.h[:, :])
    mk.dma("sp", rope.all(), None, reads=[], in_ap=C.rope_c.h[:, :, :].rearrange("k p n -> p k n"))
    KT = mk.sb("gq_KT", [128, 4, S], BF16, stk)
    VA = mk.sb("gq_VA", [128, NT, 4, 128], BF16, stk)
    QT = mk.sb("gq_QT", [128, 4, S], BF16, stk)
    mk.memset("pool", VA.all(), 1.0)

    def norm_rope(tb, lhsA, lhsB, g0, dst, tmps, it):
        t0, t1 = tb * 512, (tb + 1) * 512
        sq, sqh, rs, ta, tb_ = tmps
        A = P[0] if it % 2 == 0 else P[3]
        B = P[1] if it % 2 == 0 else P[4]
        Sb = P[2] if it % 2 == 0 else P[5]
        for kc in range(8):
            mk.mm(pv(A, 0, 128, 0, 512), lhsA(kc), xT[:, kc, t0:t1], start=(kc == 0), stop=(kc == 7))
        for kc in range(8):
            mk.mm(pv(B, 0, 128, 0, 512), lhsB(kc), xT[:, kc, t0:t1], start=(kc == 0), stop=(kc == 7))
        mk.act(sq.all(), pv(A, 0, 128, 0, 512), AF.Square)
        mk.copy("pool", sqh[:, 0, :], sq.all())
        mk.tt("pool", sqh[:, 1, :], sq.all(), sqh[:, 0, :], ALU.subtract)
        for hl in range(2):
            mk.mm(pv(Sb, 0, 128, 0, 512), C.bd_onesb, sqh[:, hl, :], start=(hl == 0), stop=(hl == 1))
        mk.act(rs.all(), pv(Sb, 0, 128, 0, 512), AF.Sqrt, bias=C.eps[:, 0:1], scale=1.0 / 64)
        mk.recip(rs.all(), rs.all())
        mk.stt("dve", ta.all(), pv(A, 0, 128, 0, 512), gn[:, g0:g0 + 1], rope[:, 0, t0:t1], ALU.mult, ALU.mult)
        mk.stt("dve", tb_.all(), pv(B, 0, 128, 0, 512), gn[:, g0 + 1:g0 + 2], rope[:, 1, t0:t1], ALU.mult, ALU.mult)
        mk.tt("pool", ta.all(), ta.all(), tb_.all(), ALU.add)
        mk.tt("dve", dst, ta.all(), rs.all(), ALU.mult)

    st2 = ExitStack()
    tmps = [(mk.sb("gq_sq%d" % i, [128, 512], F32, st2), mk.sb("gq_sqh%d" % i, [128, 2, 512], BF16, st2),
             mk.sb("gq_rs%d" % i, [128, 512], F32, st2), mk.sb("gq_ta%d" % i, [128, 512], F32, st2),
             mk.sb("gq_tb%d" % i, [128, 512], F32, st2)) for i in range(2)]
    it = 0
    for tb in range(4):
        for kv in range(4):
            norm_rope(tb, lambda kc, kv=kv: V(wk, wk.h[:, kc, kv, :, :].rearrange("p a e -> p (a e)"), 0, wk.size),
                      lambda kc, kv=kv: V(wks, wks.h[:, kc, kv, :, :].rearrange("p a e -> p (a e)"), 0, wks.size),
                      2, KT[:, kv, tb * 512:(tb + 1) * 512], tmps[it % 2], it)
            it += 1
        for j in range(4):
            tt = tb * 4 + j
            for kc in range(8):
                mk.mm(pv(P[6], 0, 128, 0, 256), xT[:, kc, tt * 128:(tt + 1) * 128], wv[:, kc, :], start=(kc == 0), stop=(kc == 7))
            mk.copy("act", VA[:, tt, :, 0:64], pv(P[6], 0, 128, 0, 256, ("p (h e) -> p h e", dict(h=4))))
    for hg in range(2):
        st3 = ExitStack()
        wq = mk.sb("gq_wq", [128, 8, 512], BF16, st3)
        wqs = mk.sb("gq_wqs", [128, 8, 512], BF16, st3)
        mk.dma("pool", wq.all(), None, reads=[], in_ap=wsrc(wi, 0, D, hg * 512, (hg + 1) * 512))
        mk.dma("pool", wqs.all(), None, reads=[], in_ap=wsrc(C.w_q_sw.h, 0, D, hg * 512, (hg + 1) * 512))
        for tb in range(4):
            for c in range(4):
                norm_rope(tb, lambda kc, c=c: wq[:, kc, c * 128:(c + 1) * 128], lambda kc, c=c: wqs[:, kc, c * 128:(c + 1) * 128],
                          0, QT[:, c, tb * 512:(tb + 1) * 512], tmps[it % 2], it)
                it += 1
        if C.dbg and s == 0:
            dump(C, "gQT%d" % hg, QT.all(), [128, 4, S], BF16)
            if hg == 0:
                dump(C, "gKT", KT.all(), [128, 4, S], BF16)
                dump(C, "gVA", VA.all(), [128, NT, 4, 128], BF16)
        attention_pairs(mk, C, QT, KT, VA, mixT, 4, 64.0 ** -0.5,
                        lambda p: (p, (hg * 8 + 2 * p) // 4, (hg * 8 + 2 * p) // 4, (hg * 8 + 2 * p) // 2), st3)
        st3.close()
        mk.barrier()
    st2.close()
    stk.close()
    mk.barrier()


def phase_final(mk, C):
    stk = ExitStack()
    lnb = mk.sb("fin_lnb", [128, 2, D], F32, stk)
    mk.dma("sp", lnb.all(), None, reads=[], in_ap=C.lnbc.h[1, 2:4].rearrange("k p n -> p k n"))
    xts = [mk.sb("fin_x%d" % i, [128, D], F32, stk) for i in range(3)]
    tmp = mk.sb("fin_tmp", [128, D], F32, stk)
    sts = [mk.sb("fin_st%d" % i, [128, 16], F32, stk) for i in range(2)]
    i = 0
    for s in range(2):
        for tt in range(NT):
            xt = xts[i % 3]
            mk.dma("sp", xt.all(), C.acc[1][s][tt * 128:(tt + 1) * 128, :])
            layer_norm_tile(mk, C, xt, lnb[:, 0, :], lnb[:, 1, :], tmp, sts[i % 2])
            mk.dma("sp", C.out[s, tt * 128:(tt + 1) * 128, :], xt.all())
            i += 1
    stk.close()
    mk.barrier()


def build_program(nc, dbg=False, layers=(0, 1)):
    mk = MK(nc)
    C = setup(mk, dbg=dbg)
    idx_s = [mk.sb("idx_s%d" % i, [128, 2, 16], I32) for i in range(2)]
    gate_s = [mk.sb("gate_s%d" % i, [128, 2, 16], F32) for i in range(2)]
    for L in layers:
        for s in range(2):
            stk = ExitStack()
            xT = mk.sb("xT", [128, 8, S], BF16, stk)
            mixT = mk.sb("mixT", [128, 8, S], BF16, stk)
            st = ExitStack()
            if L == 1:
                lnb2 = mk.sb("pro_lnb", [128, 2, D], F32, st)
                mk.dma("sp", lnb2.all(), None, reads=[], in_ap=C.lnbc.h[0, 2:4].rearrange("k p n -> p k n"))
                phase_prologue(mk, C, L, s, xT, lnb2[:, 0, :], lnb2[:, 1, :], stk=st)
            else:
                phase_prologue(mk, C, L, s, xT, stk=st)
            st.close()
            mk.barrier()
            if L == 0:
                even_mla(mk, C, s, xT, mixT)
                even_gla(mk, C, s, xT, mixT)
            else:
                odd_mixer(mk, C, s, xT, mixT)
            phase_post(mk, C, L, s, mixT)
            stk.close()
            mk.barrier()
        st_ex = ExitStack()
        wb = experts_alloc_w(mk, st_ex)
        experts_load_w(mk, C, L, wb, 0)
        experts_load_w(mk, C, L, wb, 1)
        phase_topk(mk, C, idx_s, gate_s)
        phase_experts(mk, C, L, [0, 1], idx_s, gate_s, wb=wb)
        st_ex.close()
        mk.barrier()
    if 1 in layers:
        phase_final(mk, C)
    mk.finish()
    return mk, C


def _axial_rope_np(seq, rot_dim):
    rows = seq // 64
    row = np.repeat(np.arange(rows, dtype=np.float32), 64)
    col = np.tile(np.arange(64, dtype=np.float32), rows)
    axis_dim = rot_dim // 2
    inv = (np.float32(10000.0) ** (-np.arange(0, axis_dim, 2, dtype=np.float32) / np.float32(axis_dim))).astype(np.float32)
    ang = np.concatenate([row[:, None] * inv, col[:, None] * inv], axis=-1).astype(np.float32)
    return np.cos(ang).astype(np.float32), np.sin(ang).astype(np.float32)


def host_consts():
    f = np.float32
    idx = np.arange(128)
    same = (idx[:, None] // 64) == (idx[None, :] // 64)
    ident = np.eye(128, dtype=f)
    ones = np.ones((128, 128), f)
    bd = same.astype(f)
    s_, t_ = idx[:, None], idx[None, :]
    triF = (same & (s_ <= t_)).astype(f)
    triS = (same & (s_ > t_)).astype(f)
    triB = (same & (s_ >= t_)).astype(f)
    triSB = (same & (s_ < t_)).astype(f)
    cmat = np.stack([ident, ones, bd, triF, triS, triB, triSB]).astype(f)
    cmask = np.stack([np.tile(triF, (1, 4)), np.tile(triB, (1, 4))]).astype(f)
    ca, sa = _axial_rope_np(S, 32)
    rope_a = np.zeros((2, 128, S), f)
    rope_a[0, 64:96] = np.concatenate([ca, ca], 1).T
    rope_a[1, 64:96] = np.concatenate([-sa, sa], 1).T
    cc, sc = _axial_rope_np(S, 64)
    CC = np.concatenate([cc, cc], 1).T
    SS = np.concatenate([-sc, sc], 1).T
    rope_c = np.stack([np.concatenate([CC, CC], 0), np.concatenate([SS, SS], 0)]).astype(f)
    return dict(cmat=cmat, cmask=cmask, rope_a=rope_a, rope_c=rope_c)


def host_shared(I):
    f = np.float32
    g = lambda k: np.asarray(I[k], dtype=f)
    sh = dict(host_consts())
    we = g("w_in_even")[0]
    sh["w_in_even"] = np.ascontiguousarray(we)
    perm32 = (np.arange(32) + 16) % 32
    sh["w_kpe_sw"] = np.ascontiguousarray(we[:, 384:416][:, perm32])
    wuq = g("w_uq")[0]
    sh["w_uq"] = np.ascontiguousarray(wuq)
    sh["w_uq_sw"] = np.ascontiguousarray(wuq.reshape(256, 8, 96)[:, :, 64:][:, :, perm32].reshape(256, 256))
    sh["w_ukv"] = np.ascontiguousarray(g("w_ukv")[0])
    sh["mla_qn"] = np.ascontiguousarray(g("mla_q_norm")[0].reshape(2, 128).T)
    sh["mla_kvn"] = np.ascontiguousarray(g("mla_kv_norm")[0].reshape(128, 1))
    gw = np.zeros((2, 17, 256), f)
    gw[0, :16] = g("gla_gate_w_fwd")[0]
    gw[0, 16] = g("gla_gate_b_fwd")[0]
    gw[1, :16] = g("gla_gate_w_bwd")[0]
    gw[1, 16] = g("gla_gate_b_bwd")[0]
    sh["gla_gw"] = gw
    sh["gla_norm_bc"] = np.ascontiguousarray(np.broadcast_to(np.tile(g("gla_norm")[0], 4)[None, :], (128, 512)))
    wo_ = g("w_in_odd")[0]
    sh["w_in_odd"] = np.ascontiguousarray(wo_)
    perm64 = (np.arange(64) + 32) % 64
    sh["w_q_sw"] = np.ascontiguousarray(wo_[:, :1024].reshape(D, 16, 64)[:, :, perm64].reshape(D, 1024))
    sh["w_k_sw"] = np.ascontiguousarray(wo_[:, 1024:1280].reshape(D, 4, 64)[:, :, perm64].reshape(D, 256))
    qn, kn = g("gqa_q_norm")[0], g("gqa_k_norm")[0]
    sh["gqa_n"] = np.ascontiguousarray(np.stack([np.tile(qn, 2), np.tile(qn[perm64], 2), np.tile(kn, 2), np.tile(kn[perm64], 2)], 1))
    sh["w_o"] = g("w_o")
    lnbc = np.zeros((2, 5, 128, D), f)
    for L in range(2):
        for j, k in enumerate(["ln1_g", "ln1_b", "ln2_g", "ln2_b", "ple_gate_b"]):
            lnbc[L, j] = np.broadcast_to(g(k)[L][None, :], (128, D))
    sh["lnbc"] = lnbc
    for k in ["router_w", "w1", "w3", "w2", "ple_gate_w", "ple_w"]:
        sh[k] = g(k)
    return sh


def host_percore(I, c):
    f = np.float32
    x = np.asarray(I["x"], dtype=f)
    p = np.asarray(I["p"], dtype=f)
    return dict(x_in=np.ascontiguousarray(x[2 * c:2 * c + 2]),
                pT=np.ascontiguousarray(p[:, 2 * c:2 * c + 2].transpose(0, 1, 3, 2)))


def kernel(**inputs):
    sh = host_shared(inputs)
    in_maps = []
    for c in range(8):
        m = dict(sh)
        m.update(host_percore(inputs, c))
        in_maps.append(m)
    nc = bass.Bass("TRN2", target_bir_lowering=False)
    build_program(nc)
    res = run_bass_kernel_spmd(nc, in_maps, core_ids=list(range(8)))
    out = np.concatenate([np.asarray(r["out"]) for r in res.results], axis=0)
    return np.ascontiguousarray(out.astype(np.float32))
```
